# Optimizing a Trainium2 kernel written in Bass

```python
import math
import jax, jax.numpy as jnp
from jax import lax
import numpy as np

D_MODEL = 2048
BATCH = 2
SEQ = 4096
DEPTH = 4
DEC_BATCH = 32
DEC_SEQ = 8
PAST_LEN = 16384
PAGE_SIZE = 128

N_EVEN = (DEPTH + 1) // 2
N_ODD = DEPTH // 2

A_WIDTH = D_MODEL // 2
A_HEADS = 8
A_KDIM = 128
A_VDIM = A_WIDTH // A_HEADS
A_QK = A_HEADS * A_KDIM
A_CHUNK = 64
B_HEADS = 16
B_HEAD_DIM = 64
B_KV_HEADS = 4
B_GROUP = B_HEADS // B_KV_HEADS
B_WIDTH = B_HEADS * B_HEAD_DIM
B_KV_WIDTH = B_KV_HEADS * B_HEAD_DIM
WINDOW = 128
N_BUCKETS = 32
MAX_DISTANCE = 128
MASK_VALUE = -1e30
IN_A = 2 * A_QK + 2 * A_WIDTH
IN_EVEN = IN_A + B_WIDTH + 2 * B_KV_WIDTH
EVEN_SPLITS = (A_QK, 2 * A_QK, 2 * A_QK + A_WIDTH, IN_A, IN_A + B_WIDTH, IN_A + B_WIDTH + B_KV_WIDTH)
MIX_WIDTH = A_WIDTH + B_WIDTH
C_HEAD = 64
C_HEADS = D_MODEL // C_HEAD
LORA_DECAY = 96
LORA_AAA = 96
LORA_MV = 64
LORA_GATE = 256
GN_EPS = 64e-5
D_FF = 4 * D_MODEL
NORM_EPS = 1e-6

kernel_name = "hgrn2_swa_sink_rwkv7_hybrid_step"

F32 = jnp.float32


def rmsnorm(x, g):
    xf = x.astype(F32)
    y = xf * lax.rsqrt(jnp.mean(xf * xf, axis=-1, keepdims=True) + NORM_EPS)
    return (y * g.astype(F32)).astype(x.dtype)


def hgrn_lower_bounds(lb_raw):
    p = jax.nn.softmax(lb_raw.astype(F32), axis=0)
    return jnp.cumsum(p, axis=0) - p[0]


def gla_chunked(q, k, v, logf, s0):
    B, L, H, K = q.shape
    C = math.gcd(L, A_CHUNK)
    N = L // C
    def to_chunks(t):
        return t.reshape(B, N, C, H, t.shape[-1]).transpose(1, 0, 3, 2, 4)
    causal = jnp.tril(jnp.ones((C, C), bool))[:, :, None]
    def step(S, inp):
        qi, ki, vi, gi = inp
        G = jnp.cumsum(gi, axis=2)
        diff = G[:, :, :, None, :] - G[:, :, None, :, :]
        decay = jnp.where(causal, jnp.exp(jnp.minimum(diff, 0.0)), 0.0)
        attn = jnp.einsum("bhik,bhjk,bhijk->bhij", qi, ki, decay)
        o = jnp.einsum("bhij,bhjv->bhiv", attn, vi) + jnp.einsum("bhik,bhkv->bhiv", qi * jnp.exp(G), S)
        g_last = G[:, :, -1, :]
        S_new = jnp.exp(g_last)[..., None] * S + jnp.einsum("bhjk,bhjv->bhkv", ki * jnp.exp(g_last[:, :, None, :] - G), vi)
        return S_new, o
    S, o = lax.scan(step, s0, (to_chunks(q), to_chunks(k), to_chunks(v), to_chunks(logf)))
    o = o.transpose(1, 0, 3, 2, 4).reshape(B, L, H, v.shape[-1])
    return o, S


def t5_bucket(dist):
    max_exact = N_BUCKETS // 2
    d = np.maximum(dist, 0)
    large = max_exact + (np.log(np.maximum(d, max_exact).astype(np.float32) / max_exact)
                         / math.log(MAX_DISTANCE / max_exact) * (N_BUCKETS - max_exact)).astype(np.int32)
    large = np.minimum(large, N_BUCKETS - 1)
    return np.where(d < max_exact, d, large).astype(np.int32)


def swa_sinks(q, k_all, v_all, pos0, rel_bias, sinks):
    B, L, H, D = q.shape
    P = k_all.shape[1] - L
    pad = WINDOW - P
    k_pad = jnp.pad(k_all, ((0, 0), (pad, 0), (0, 0), (0, 0)))
    v_pad = jnp.pad(v_all, ((0, 0), (pad, 0), (0, 0), (0, 0)))
    QB = math.gcd(L, WINDOW)
    NB = L // QB
    span = WINDOW + QB
    idx = np.arange(NB)[:, None] * QB + np.arange(span)[None, :]
    kb = k_pad[:, idx]
    vb = v_pad[:, idx]
    qb = q.reshape(B, NB, QB, B_KV_HEADS, B_GROUP, D)
    dist = np.arange(QB)[:, None] + WINDOW - np.arange(span)[None, :]
    key_pos = pos0 - WINDOW + idx
    valid = (dist >= 0)[None] & (dist < WINDOW)[None] & (key_pos[:, None, :] >= 0)
    bias = rel_bias.astype(F32)[t5_bucket(dist)]
    bias = bias.transpose(2, 0, 1).reshape(B_KV_HEADS, B_GROUP, QB, span)
    s = jnp.einsum("bnqkgd,bnskd->bnkgqs", qb, kb).astype(F32) * (D ** -0.5) + bias
    s = jnp.where(valid[None, :, None, None], s, MASK_VALUE)
    sink = sinks.astype(F32).reshape(B_KV_HEADS, B_GROUP)[None, None, :, :, None, None]
    m = jnp.maximum(jnp.max(s, axis=-1, keepdims=True), sink)
    p = jnp.exp(s - m)
    p = p / (jnp.sum(p, axis=-1, keepdims=True) + jnp.exp(sink - m))
    o = jnp.einsum("bnkgqs,bnskd->bnqkgd", p.astype(vb.dtype), vb)
    return o.reshape(B, L, H * D)


def even_mixer(h, lb, s0, k_past, v_past, pos0, w_in, a_norm_g, rel_bias, sinks, w_out):
    B, L, _ = h.shape
    proj = h @ w_in
    q_a, f_a, i_a, g_a, q_b, k_b, v_b = jnp.split(proj, EVEN_SPLITS, axis=-1)
    fq = f_a.astype(F32).reshape(B, L, A_HEADS, A_KDIM)
    lbh = lb.reshape(A_HEADS, A_KDIM)
    f = lbh + (1.0 - lbh) * jax.nn.sigmoid(fq)
    logf = jnp.log(f)
    kk = (1.0 - lbh) * jax.nn.sigmoid(-fq)
    qq = jax.nn.silu(q_a.astype(F32)).reshape(B, L, A_HEADS, A_KDIM) * (A_KDIM ** -0.5)
    vv = i_a.astype(F32).reshape(B, L, A_HEADS, A_VDIM)
    o_a, s_new = gla_chunked(qq, kk, vv, logf, s0.astype(F32))
    o_a = (rmsnorm(o_a.reshape(B, L, A_WIDTH), a_norm_g) * jax.nn.silu(g_a.astype(F32))).astype(h.dtype)
    qb = q_b.reshape(B, L, B_HEADS, B_HEAD_DIM)
    k_all = jnp.concatenate([k_past.astype(h.dtype), k_b.reshape(B, L, B_KV_HEADS, B_HEAD_DIM)], axis=1)
    v_all = jnp.concatenate([v_past.astype(h.dtype), v_b.reshape(B, L, B_KV_HEADS, B_HEAD_DIM)], axis=1)
    o_b = swa_sinks(qb, k_all, v_all, pos0, rel_bias, sinks).astype(h.dtype)
    out = jnp.concatenate([o_a, o_b], axis=-1) @ w_out
    return out, s_new, k_all[:, -WINDOW:], v_all[:, -WINDOW:]


def rwkv7_mixer(h, shift0, S0, v_first, vres, mu, wr, wk, wv, wo, w0, w1, w2, a0, a1, a2, g1, g2, k_k, k_a, r_k, lnx_g, lnx_b):
    B, L, D = h.shape
    H, N = C_HEADS, C_HEAD
    x_prev = jnp.concatenate([shift0[:, None, :].astype(h.dtype), h[:, :-1]], axis=1)
    xx = x_prev - h
    xr, xw, xk, xv, xa, xg = [h + xx * mu[i] for i in range(6)]
    r = xr @ wr
    k = xk @ wk
    v = xv @ wv
    v_layer = v
    if vres is not None:
        v0, v1, v2 = vres
        v = v + (v_first - v) * jax.nn.sigmoid(v0 + (xv @ v1) @ v2)
    w_log = -jax.nn.softplus(-(w0 + jnp.tanh(xw @ w1) @ w2).astype(F32)) - 0.5
    a = jax.nn.sigmoid((a0 + (xa @ a1) @ a2).astype(F32))
    g = jax.nn.sigmoid(xg @ g1) @ g2
    heads = lambda t: t.astype(F32).reshape(B, L, H, N)
    rh, vh, ah = heads(r), heads(v), heads(a)
    kkh = heads(k * k_k)
    kkh = kkh * lax.rsqrt(jnp.maximum(jnp.sum(kkh * kkh, axis=-1, keepdims=True), 1e-24))
    kh = heads(k) * (1.0 + (ah - 1.0) * k_a.astype(F32).reshape(H, N))
    decay = jnp.exp(-jnp.exp(w_log)).reshape(B, L, H, N)
    def step(S, inp):
        r_t, w_t, k_t, v_t, kk_t, a_t = inp
        sa = jnp.einsum("bhij,bhj->bhi", S, -kk_t)
        S = S * w_t[:, :, None, :] + sa[..., None] * (kk_t * a_t)[:, :, None, :] + v_t[..., None] * k_t[:, :, None, :]
        return S, jnp.einsum("bhij,bhj->bhi", S, r_t)
    seq = lambda t: jnp.moveaxis(t, 1, 0)
    S_new, y = lax.scan(step, S0.astype(F32), (seq(rh), seq(decay), seq(kh), seq(vh), seq(kkh), seq(ah)))
    y = jnp.moveaxis(y, 0, 1)
    mean = jnp.mean(y, axis=-1, keepdims=True)
    var = jnp.mean(jnp.square(y - mean), axis=-1, keepdims=True)
    y = ((y - mean) * lax.rsqrt(var + GN_EPS)).reshape(B, L, D) * lnx_g.astype(F32) + lnx_b.astype(F32)
    y = y + (jnp.sum(rh * kh * r_k.astype(F32), axis=-1, keepdims=True) * vh).reshape(B, L, D)
    out = (y.astype(h.dtype) * g) @ wo
    return out, S_new, h[:, -1], v_layer


def trunk(x, st_hgrn, k_cache, v_cache, st_rwkv, st_shift, pos0, p):
    lbs = hgrn_lower_bounds(p["hgrn_lb_raw"])
    hgrn_out, k_out, v_out, rwkv_out, shift_out = [], [], [], [], []
    v_first = None
    for layer in range(DEPTH):
        h = rmsnorm(x, p["norm_mix_pre"][layer])
        if layer % 2 == 0:
            e = layer // 2
            mix, s_new, k_new, v_new = even_mixer(h, lbs[e], st_hgrn[e], k_cache[e], v_cache[e], pos0,
                                                  p["w_in_even"][e], p["hgrn_norm_g"][e], p["rel_bias"],
                                                  p["attn_sinks"][e], p["w_out_even"][e])
            hgrn_out.append(s_new.astype(st_hgrn.dtype))
            k_out.append(k_new.astype(k_cache.dtype))
            v_out.append(v_new.astype(v_cache.dtype))
        else:
            o = layer // 2
            vres = None if o == 0 else (p["rw_v0"][o - 1], p["rw_v1"][o - 1], p["rw_v2"][o - 1])
            mix, S_new, sh_new, v_layer = rwkv7_mixer(
                h, st_shift[o], st_rwkv[o], v_first, vres, p["rw_mu"][o], p["rw_wr"][o], p["rw_wk"][o],
                p["rw_wv"][o], p["rw_wo"][o], p["rw_w0"][o], p["rw_w1"][o], p["rw_w2"][o], p["rw_a0"][o],
                p["rw_a1"][o], p["rw_a2"][o], p["rw_g1"][o], p["rw_g2"][o], p["rw_kk"][o], p["rw_ka"][o],
                p["rw_rk"][o], p["rw_lnx_g"][o], p["rw_lnx_b"][o])
            if o == 0:
                v_first = v_layer
            rwkv_out.append(S_new.astype(st_rwkv.dtype))
            shift_out.append(sh_new.astype(st_shift.dtype))
        x = x + rmsnorm(mix, p["norm_mix_post"][layer])
        u = rmsnorm(x, p["norm_ffn_pre"][layer])
        u = jnp.square(jax.nn.relu(u @ p["w_up"][layer])) @ p["w_down"][layer]
        x = x + rmsnorm(u, p["norm_ffn_post"][layer])
    return x, jnp.stack(hgrn_out), jnp.stack(k_out), jnp.stack(v_out), jnp.stack(rwkv_out), jnp.stack(shift_out)


def setup_inputs(seed: int = 0) -> dict:
    key = jax.random.key(seed)
    keys = jax.random.split(key, 64)
    counter = [0]
    def nk():
        k = keys[counter[0]]
        counter[0] += 1
        return k
    def nrm(shape, scale=1.0):
        return jax.random.normal(nk(), shape, F32) * scale
    def gain(shape):
        return 1.0 + nrm(shape, 0.05)
    D = D_MODEL
    return {
        "x_prompt": nrm((BATCH, SEQ, D)),
        "x_sample": nrm((DEC_BATCH, DEC_SEQ, D)),
        "state_hgrn": nrm((N_EVEN, DEC_BATCH, A_HEADS, A_KDIM, A_VDIM), 0.5),
        "cache_swa_k": nrm((N_EVEN, DEC_BATCH, WINDOW, B_KV_HEADS, B_HEAD_DIM)),
        "cache_swa_v": nrm((N_EVEN, DEC_BATCH, WINDOW, B_KV_HEADS, B_HEAD_DIM)),
        "state_rwkv": nrm((N_ODD, DEC_BATCH, C_HEADS, C_HEAD, C_HEAD), 0.5),
        "state_shift": nrm((N_ODD, DEC_BATCH, D)),
        "norm_mix_pre": gain((DEPTH, D)),
        "norm_mix_post": gain((DEPTH, D)),
        "norm_ffn_pre": gain((DEPTH, D)),
        "norm_ffn_post": gain((DEPTH, D)),
        "w_in_even": nrm((N_EVEN, D, IN_EVEN), D ** -0.5),
        "w_out_even": nrm((N_EVEN, MIX_WIDTH, D), MIX_WIDTH ** -0.5),
        "hgrn_lb_raw": nrm((N_EVEN, A_QK), 0.5),
        "hgrn_norm_g": gain((N_EVEN, A_WIDTH)),
        "rel_bias": nrm((N_BUCKETS, B_HEADS), 0.5),
        "attn_sinks": nrm((N_EVEN, B_HEADS), 0.5),
        "rw_mu": jax.random.uniform(nk(), (N_ODD, 6, D), F32),
        "rw_wr": nrm((N_ODD, D, D), D ** -0.5),
        "rw_wk": nrm((N_ODD, D, D), D ** -0.5),
        "rw_wv": nrm((N_ODD, D, D), D ** -0.5),
        "rw_wo": nrm((N_ODD, D, D), D ** -0.5),
        "rw_w0": jax.random.uniform(nk(), (N_ODD, D), F32, -6.5, -1.5),
        "rw_w1": nrm((N_ODD, D, LORA_DECAY), D ** -0.5),
        "rw_w2": nrm((N_ODD, LORA_DECAY, D), 0.1 * LORA_DECAY ** -0.5),
        "rw_a0": nrm((N_ODD, D), 0.1),
        "rw_a1": nrm((N_ODD, D, LORA_AAA), D ** -0.5),
        "rw_a2": nrm((N_ODD, LORA_AAA, D), 0.1 * LORA_AAA ** -0.5),
        "rw_v0": nrm((N_ODD - 1, D), 0.1) + 1.0,
        "rw_v1": nrm((N_ODD - 1, D, LORA_MV), D ** -0.5),
        "rw_v2": nrm((N_ODD - 1, LORA_MV, D), 0.1 * LORA_MV ** -0.5),
        "rw_g1": nrm((N_ODD, D, LORA_GATE), D ** -0.5),
        "rw_g2": nrm((N_ODD, LORA_GATE, D), LORA_GATE ** -0.5),
        "rw_kk": 0.85 + nrm((N_ODD, D), 0.05),
        "rw_ka": 1.0 + nrm((N_ODD, D), 0.05),
        "rw_rk": nrm((N_ODD, C_HEADS, C_HEAD), 0.1),
        "rw_lnx_g": gain((N_ODD, D)),
        "rw_lnx_b": nrm((N_ODD, D), 0.02),
        "w_up": nrm((DEPTH, D, D_FF), D ** -0.5),
        "w_down": nrm((DEPTH, D_FF, D), D_FF ** -0.5),
    }


def reference(x_prompt, x_sample, state_hgrn, cache_swa_k, cache_swa_v, state_rwkv, state_shift,
              norm_mix_pre, norm_mix_post, norm_ffn_pre, norm_ffn_post,
              w_in_even, w_out_even, hgrn_lb_raw, hgrn_norm_g, rel_bias, attn_sinks,
              rw_mu, rw_wr, rw_wk, rw_wv, rw_wo, rw_w0, rw_w1, rw_w2, rw_a0, rw_a1, rw_a2,
              rw_v0, rw_v1, rw_v2, rw_g1, rw_g2, rw_kk, rw_ka, rw_rk, rw_lnx_g, rw_lnx_b,
              w_up, w_down):
    p = {
        "norm_mix_pre": norm_mix_pre, "norm_mix_post": norm_mix_post,
        "norm_ffn_pre": norm_ffn_pre, "norm_ffn_post": norm_ffn_post,
        "w_in_even": w_in_even, "w_out_even": w_out_even, "hgrn_lb_raw": hgrn_lb_raw,
        "hgrn_norm_g": hgrn_norm_g, "rel_bias": rel_bias, "attn_sinks": attn_sinks,
        "rw_mu": rw_mu, "rw_wr": rw_wr, "rw_wk": rw_wk, "rw_wv": rw_wv, "rw_wo": rw_wo,
        "rw_w0": rw_w0, "rw_w1": rw_w1, "rw_w2": rw_w2, "rw_a0": rw_a0, "rw_a1": rw_a1, "rw_a2": rw_a2,
        "rw_v0": rw_v0, "rw_v1": rw_v1, "rw_v2": rw_v2, "rw_g1": rw_g1, "rw_g2": rw_g2,
        "rw_kk": rw_kk, "rw_ka": rw_ka, "rw_rk": rw_rk, "rw_lnx_g": rw_lnx_g, "rw_lnx_b": rw_lnx_b,
        "w_up": w_up, "w_down": w_down,
    }
    Bp = x_prompt.shape[0]
    dt = x_prompt.dtype
    zero_hgrn = jnp.zeros((N_EVEN, Bp, A_HEADS, A_KDIM, A_VDIM), dt)
    zero_kv = jnp.zeros((N_EVEN, Bp, 0, B_KV_HEADS, B_HEAD_DIM), dt)
    zero_rwkv = jnp.zeros((N_ODD, Bp, C_HEADS, C_HEAD, C_HEAD), dt)
    zero_shift = jnp.zeros((N_ODD, Bp, D_MODEL), dt)
    y_prompt, hgrn_p, k_p, v_p, rwkv_p, shift_p = trunk(
        x_prompt, zero_hgrn, zero_kv, zero_kv, zero_rwkv, zero_shift, 0, p)
    y_sample, hgrn_s, k_s, v_s, rwkv_s, shift_s = trunk(
        x_sample, state_hgrn, cache_swa_k, cache_swa_v, state_rwkv, state_shift, PAST_LEN, p)
    return (y_prompt, y_sample, hgrn_p, hgrn_s, k_p, k_s, v_p, v_s, rwkv_p, rwkv_s, shift_p, shift_s)
```

```python
import math
import numpy as np
from contextlib import ExitStack
import concourse.bass as bass
import concourse.mybir as mybir
from concourse.bass_utils import run_bass_kernel_spmd

F32 = mybir.dt.float32
BF16 = mybir.dt.bfloat16
AF = mybir.ActivationFunctionType
ALU = mybir.AluOpType

D = 2048
KC = 16
T = 256
DFF = 8192
NCORES = 8


class Buf:
    __slots__ = ("name", "last_write", "reads", "sem")

    def __init__(self, name):
        self.name = name
        self.last_write = None
        self.reads = {}
        self.sem = None


class Sched:
    ENGS = ("pe", "act", "dve", "pool", "sp")

    def __init__(self, nc, stack):
        self.nc = nc
        self.stack = stack
        self.ops = {e: [] for e in self.ENGS}
        self.seq = {e: 0 for e in self.ENGS}
        self.seen = {e: {} for e in self.ENGS}
        self.sems = {}
        self.n_dma_sems = 0
        for e in self.ENGS:
            self.sems[("eng", e)] = stack.enter_context(nc.semaphore("s_" + e))
        self.dma_total = {}
        self.out_tokens = {}

    def _buf_sem(self, buf):
        if buf.sem is None:
            key = ("dma", self.n_dma_sems)
            self.n_dma_sems += 1
            self.sems[key] = self.stack.enter_context(self.nc.semaphore("d%d" % key[1]))
            buf.sem = key
        return buf.sem

    def _deps(self, eng, reads, writes):
        deps = {}
        mykey = ("eng", eng)
        for b in reads:
            t = b.last_write
            if t is not None and deps.get(t[0], 0) < t[1]:
                deps[t[0]] = t[1]
        for b in writes:
            t = b.last_write
            if t is not None and deps.get(t[0], 0) < t[1]:
                deps[t[0]] = t[1]
            for k, v in b.reads.items():
                if deps.get(k, 0) < v:
                    deps[k] = v
        out = []
        seen = self.seen[eng]
        for k, v in deps.items():
            if k == mykey and eng == "pe":
                continue
            if seen.get(k, 0) >= v:
                continue
            seen[k] = v
            out.append((k, v))
        return out

    def op(self, eng, fn, reads=(), writes=()):
        waits = self._deps(eng, reads, writes)
        self.seq[eng] += 1
        k = ("eng", eng)
        v = self.seq[eng]
        self.ops[eng].append((waits, fn, (k, 1)))
        for b in reads:
            if b.reads.get(k, 0) < v:
                b.reads[k] = v
        for b in writes:
            b.last_write = (k, v)
            b.reads = {}

    def dma(self, eng, fns, dst, reads=()):
        if dst is None:
            dsts = []
            waits = self._deps(eng, reads, [])
            key = self._buf_sem(reads[0])
        else:
            dsts = list(dst) if isinstance(dst, (list, tuple)) else [dst]
            waits = self._deps(eng, reads, dsts)
            key = self._buf_sem(dsts[0])
        for i, fn in enumerate(fns):
            self.ops[eng].append((waits if i == 0 else [], fn, (key, 16)))
        self.dma_total[key] = self.dma_total.get(key, 0) + 16 * len(fns)
        v = self.dma_total[key]
        for b in reads:
            if b.reads.get(key, 0) < v:
                b.reads[key] = v
        for d_ in dsts:
            d_.last_write = (key, v)
            d_.reads = {}
        if dst is None:
            self.out_tokens[key] = v

    def barrier(self):
        snap = {("eng", e): self.seq[e] for e in self.ENGS}
        snap.update(self.dma_total)
        for e in self.ENGS:
            waits = []
            for k, v in snap.items():
                if k == ("eng", e) or v <= self.seen[e].get(k, 0):
                    continue
                self.seen[e][k] = v
                waits.append((k, v))
            self.ops[e].append((waits, None, None))

    def final_wait(self, eng):
        waits = [(k, v) for k, v in self.out_tokens.items()]
        self.ops[eng].append((waits, None, None))

    def replay(self):
        nc = self.nc
        with nc.Block() as block:
            def mk(e):
                def body(engobj):
                    sems = self.sems
                    for waits, fn, inc in self.ops[e]:
                        for k, v in waits:
                            engobj.wait_ge(sems[k], v)
                        if fn is None:
                            continue
                        ins = fn(engobj)
                        if inc is not None:
                            ins.then_inc(sems[inc[0]], inc[1])
                return body
            block.tensor(mk("pe"))
            block.scalar(mk("act"))
            block.vector(mk("dve"))
            block.gpsimd(mk("pool"))
            block.sync(mk("sp"))


class V:
    __slots__ = ("ap", "bufs")

    def __init__(self, ap, bufs):
        self.ap = ap
        self.bufs = bufs

    def m(self, f):
        return V(f(self.ap), self.bufs)


class TT:
    def __init__(self, K, name, shape, dtype, kind="sb", nsub=1, dram_kind="Internal"):
        nc = K.nc
        if kind == "sb":
            h = K.st.enter_context(nc.sbuf_tensor(name, list(shape), dtype))
            self.base = h[:]
        elif kind == "ps":
            h = K.st.enter_context(nc.psum_tensor(name, list(shape), dtype))
            self.base = h[:]
        else:
            self.base = nc.dram_tensor(name, list(shape), dtype, kind=dram_kind).ap()
        self.name = name
        self.nsub = nsub
        self.bufs = [Buf("%s.%d" % (name, i)) for i in range(nsub)]

    def __getitem__(self, idx):
        ap = self.base[idx]
        if self.nsub == 1:
            return V(ap, self.bufs)
        i1 = idx[1] if isinstance(idx, tuple) and len(idx) > 1 else slice(None)
        if isinstance(i1, int):
            return V(ap, [self.bufs[i1]])
        lo, hi, _ = i1.indices(self.nsub)
        return V(ap, self.bufs[lo:hi])

    def all(self):
        return V(self.base, self.bufs)


class Arena:
    def __init__(self, K, name, nbytes):
        self.K = K
        self.n4 = nbytes // 4
        h = K.st.enter_context(K.nc.sbuf_tensor(name, [128, self.n4], F32))
        self.base = h[:]
        self.off = 0
        self.name = name

    def reset(self):
        self.off = 0

    def carve(self, name, shape, dtype, nsub=1):
        esz = 2 if dtype == BF16 else 4
        nel = 1
        for d_ in shape[1:]:
            nel *= d_
        n4 = (nel * esz + 31) // 32 * 8
        assert self.off + n4 <= self.n4, "arena %s overflow at %s (%d > %d)" % (self.name, name, (self.off + n4) * 4, self.n4 * 4)
        ap = self.base[0:shape[0], self.off:self.off + n4]
        self.off += n4
        if dtype == BF16:
            ap = ap.bitcast(BF16)
        ap = ap[:, 0:nel]
        if len(shape) == 3:
            ap = ap.rearrange("p (a b) -> p a b", a=shape[1])
        elif len(shape) == 4:
            ap = ap.rearrange("p (a b c) -> p a b c", a=shape[1], b=shape[2])
        t = TT.__new__(TT)
        t.base = ap
        t.name = name
        t.nsub = nsub
        t.bufs = [Buf("%s.%d" % (name, i)) for i in range(nsub)]
        return t


def _b(vs):
    out = []
    for v in vs:
        if isinstance(v, V):
            out.extend(v.bufs)
    return out


def _a(x):
    return x.ap if isinstance(x, V) else x


class PSB:
    def __init__(self, K, i):
        h = K.st.enter_context(K.nc.psum_tensor("psb%d" % i, [128, 512], F32))
        self.base = h[:]
        b_ = Buf("psb%d" % i)
        self.bufs = [b_, b_, b_, b_]

    def __getitem__(self, idx):
        ps_, cs = idx
        c0, c1, _ = cs.indices(512)
        return V(self.base[ps_, cs], self.bufs[c0 // 128:(c1 - 1) // 128 + 1])

    def bf(self, ps_, c0, c1):
        return V(self.base.bitcast(BF16)[ps_, c0:c1], self.bufs[c0 // 256:(c1 - 1) // 256 + 1])

    def v3(self, ps_, c0, c1, inner):
        return V(self.base[ps_, c0:c1].rearrange("p (a b) -> p a b", b=inner), self.bufs[c0 // 128:(c1 - 1) // 128 + 1])


class KB:
    def __init__(self, nc, st):
        self.nc = nc
        self.st = st
        self.S = Sched(nc, st)
        self.psb = [PSB(self, i) for i in range(8)]
        self.psi = 0
        self.wsl = [TT(self, "wsl%d" % i, [128, 16, 256], BF16) for i in range(3)]
        self.wsi = 0
        self.outs = []
        self.pending = []

    def ps(self):
        p = self.psb[self.psi % 8]
        self.psi += 1
        return p

    def wslot(self):
        p = self.wsl[self.wsi % 3]
        self.wsi += 1
        return p

    def act(self, out, in_, func, bias=0.0, scale=1.0, eng="act"):
        o, i, b, s = out.ap, in_.ap, _a(bias), _a(scale)
        self.S.op(eng, lambda e: e.activation(o, i, func, bias=b, scale=s), _b([in_, bias, scale]), _b([out]))

    def tt(self, out, a, b, op, eng="dve"):
        o, x, y = out.ap, a.ap, b.ap
        self.S.op(eng, lambda e: e.tensor_tensor(o, x, y, op), _b([a, b]), _b([out]))

    def ts(self, out, a, s1, op0, s2=None, op1=None, eng="dve"):
        o, x, p1, p2 = out.ap, a.ap, _a(s1), _a(s2)
        if op1 is None:
            self.S.op(eng, lambda e: e.tensor_scalar(o, x, p1, None, op0), _b([a, s1]), _b([out]))
        else:
            self.S.op(eng, lambda e: e.tensor_scalar(o, x, p1, p2, op0, op1), _b([a, s1, s2]), _b([out]))

    def stt(self, out, a, s, b, op0, op1, eng="dve"):
        o, x, p, y = out.ap, a.ap, _a(s), b.ap
        self.S.op(eng, lambda e: e.scalar_tensor_tensor(o, x, p, y, op0, op1), _b([a, s, b]), _b([out]))

    def copy(self, out, a, eng="dve"):
        o, x = out.ap, a.ap
        if eng == "act":
            self.S.op(eng, lambda e: e.copy(o, x), _b([a]), _b([out]))
        else:
            self.S.op(eng, lambda e: e.tensor_copy(o, x), _b([a]), _b([out]))

    def memset(self, out, val, eng="pool"):
        o = out.ap
        self.S.op(eng, lambda e: e.memset(o, val), [], _b([out]))

    def mm(self, out, lhsT, rhs, start=True, stop=True):
        o, l, r = out.ap, lhsT.ap, rhs.ap
        self.S.op("pe", lambda e: e.matmul(o, l, r, start=start, stop=stop), _b([lhsT, rhs]), _b([out]))

    def tr(self, out, in_, ident):
        o, i, d = out.ap, in_.ap, ident.ap
        self.S.op("pe", lambda e: e.transpose(o, i, d), _b([in_, ident]), _b([out]))

    def scan(self, out, d0, d1, init, op0, op1, eng="dve"):
        o, x, y = out.ap, d0.ap, d1.ap
        self.S.op(eng, lambda e: e.tensor_tensor_scan(o, x, y, init, op0, op1), _b([d0, d1]), _b([out]))

    def recip(self, out, a):
        o, x = out.ap, a.ap
        self.S.op("dve", lambda e: e.reciprocal(o, x), _b([a]), _b([out]))

    def dma(self, out, in_, eng="sp"):
        o, i = out.ap, in_.ap
        self.S.dma(eng, [lambda e: e.dma_start(out=o, in_=i)], out.bufs, _b([in_]))

    def dma_multi(self, outs_ins, dstbuf, reads, eng="sp"):
        fns = [(lambda o, i: (lambda e: e.dma_start(out=o, in_=i)))(o, i) for o, i in outs_ins]
        self.S.dma(eng, fns, dstbuf, reads)

    def store(self, out_ap, src_ap, src_bufs, eng="act"):
        self.S.dma(eng, [lambda e: e.dma_start(out=out_ap, in_=src_ap)], None, src_bufs)

    def pdma(self, out, in_ap):
        self.pending.append((out, in_ap))

    def flush_params(self):
        if not self.pending:
            return
        bufs = []
        for o, _ in self.pending:
            for b_ in o.bufs:
                if b_ not in bufs:
                    bufs.append(b_)
        fns = [(lambda o, i: (lambda e: e.dma_start(out=o, in_=i)))(o.ap, i) for o, i in self.pending]
        self.S.dma("sp", fns, bufs, [])
        self.pending = []


NB_HEADS = 16
KVH = 4
import os
DBG = os.environ.get("KDBG", "")
KSTOP = int(os.environ.get("KSTOP", "99"))
RD = F32 if os.environ.get("RWP", "f32") == "f32" else BF16


def rmsnorm_fm(K, C, xin, gcol, out, nch, width, eps=1e-6):
    ps = K.ps()
    for c in range(nch):
        sq = C["sq"][c % 2]
        K.act(sq[:, :], xin(c), AF.Square)
        K.mm(ps[:, 0:T], C["ones"][:, :], sq[:, :], start=(c == 0), stop=(c == nch - 1))
    rstd = C["rstd"]
    K.act(rstd[:, :], ps[:, 0:T], AF.Ln, bias=C["epsc"][:, 0:1] if eps == 1e-6 else C["epsc"][:, 1:2], scale=1.0 / width)
    K.act(rstd[:, :], rstd[:, :], AF.Exp, scale=-0.5)
    for c in range(nch):
        K.stt(out(c), xin(c), gcol(c), rstd[:, :], ALU.mult, ALU.mult)


def load_w(K, w16, r0, nkc, c0, nb, krows=128):
    slot = K.wslot()
    src = w16.base[r0:r0 + nkc * krows, c0:c0 + nb].rearrange("(kc p) c -> p kc c", p=krows)
    K.dma(slot[0:krows, 0:nkc, 0:nb], V(src, w16.bufs))
    return slot


def dense_fm(K, w16, r0, nkc, c0, ncols, rhs, evac, mrows=128, krows=128):
    for cb in range(0, ncols, 256):
        nb = min(256, ncols - cb)
        slot = load_w(K, w16, r0, nkc, c0 + cb, nb, krows)
        for m in range(0, nb, mrows):
            ps = K.ps()
            for kc in range(nkc):
                K.mm(ps[0:mrows, 0:T], slot[0:krows, kc, m:m + mrows], rhs(kc), start=(kc == 0), stop=(kc == nkc - 1))
            evac((cb + m) // mrows, ps)


def dense_tm(K, w16, r0, c0, ncols, hT, evac):
    for cb in range(0, ncols, 256):
        slot = load_w(K, w16, r0, KC, c0 + cb, 256)
        for b in range(T // 128):
            ps = K.ps()
            for kc in range(KC):
                K.mm(ps[:, 0:256], hT[:, kc, b * 128:(b + 1) * 128], slot[:, kc, 0:256], start=(kc == 0), stop=(kc == KC - 1))
            evac(b, cb // 256, ps)


def post_residual2(K, C, mo, gcol, xT):
    ps = K.ps()
    for c in range(KC):
        sq = C["sq"][c % 2]
        K.act(sq[:, :], mo[:, c, :], AF.Square)
        K.mm(ps[:, 0:T], C["ones"][:, :], sq[:, :], start=(c == 0), stop=(c == KC - 1))
    rstd = C["rstd"]
    K.act(rstd[:, :], ps[:, 0:T], AF.Ln, bias=C["epsc"][:, 0:1], scale=1.0 / D)
    K.act(rstd[:, :], rstd[:, :], AF.Exp, scale=-0.5)
    for c in range(KC):
        tmp = C["tmpf"][c % 2]
        K.stt(tmp[:, 0:T], mo[:, c, :], gcol(c), rstd[:, :], ALU.mult, ALU.mult)
        K.tt(xT[:, c, :], xT[:, c, :], tmp[:, 0:T], ALU.add, eng="pool")


def mlp_tile(K, C, W, layer, xT):
    uT = C["hT"]
    rmsnorm_fm(K, C, lambda c: xT[:, c, :], lambda c: C["g_ffn_pre"][:, layer, c:c + 1], lambda c: uT[:, c, :], KC, D)
    hid = C["hid"]

    def ev_up(m, ps):
        r = C["tmpa"][m % 2]
        K.act(r[:, :], ps[:, 0:T], AF.Relu)
        K.tt(hid[:, m, :], r[:, :], r[:, :], ALU.mult, eng=("dve" if m % 2 == 0 else "pool"))
    dense_fm(K, W["w_up"], layer * D, KC, 0, DFF, lambda kc: uT[:, kc, :], ev_up)
    mo = C["mo"]
    wd = W["w_down"]
    for cb in range(0, D, 256):
        pss = [K.ps(), K.ps()]
        for kg in range(4):
            slot = load_w(K, wd, layer * DFF + kg * 2048, 16, cb, 256)
            for m in range(2):
                for kc in range(16):
                    K.mm(pss[m][:, 0:T], slot[:, kc, m * 128:(m + 1) * 128], hid[:, kg * 16 + kc, :],
                         start=(kg == 0 and kc == 0), stop=(kg == 3 and kc == 15))
        for m in range(2):
            K.copy(mo[:, cb // 128 + m, :], pss[m][:, 0:T], eng=("act" if m == 0 else "dve"))
    post_residual2(K, C, mo, lambda c: C["g_ffn_post"][:, layer, c:c + 1], xT)


class TileInfo:
    def __init__(self, t, NTP):
        self.t = t
        self.sample = t >= NTP
        self.first = (t == 0)
        self.last_prompt = (t == NTP - 1)
        self.sq0 = (t - NTP) * 2


def even_tile(K, C, E, Wv, I, O, e, layer, ti, xT):
    hT = C["hT"]
    rmsnorm_fm(K, C, lambda c: xT[:, c, :], lambda c: C["g_mix_pre"][:, layer, c:c + 1], lambda c: hT[:, c, :], KC, D)
    if KSTOP <= 0:
        return
    win = Wv["w_in"]
    r0 = e * D
    rhs = lambda kc: hT[:, kc, :]
    qq, qt, kt, kh, dec, gate, oT, oa = E["qq"], E["qt"], E["kt"], E["kh"], E["dec"], E["gate"], E["oT"], E["oa"]
    tf = C["tmpf"]
    dense_fm(K, win, r0, KC, 0, 1024, rhs, lambda m, ps: K.act(qq[:, m, :], ps[:, 0:T], AF.Silu))
    if KSTOP <= 1:
        return

    def ev_f(m, ps):
        s_, f_, g_, x_ = tf[0], tf[1], tf[2], tf[3]
        K.act(s_[:, 0:T], ps[:, 0:T], AF.Sigmoid)
        K.ts(f_[:, 0:T], s_[:, 0:T], E["oml"][:, e, m:m + 1], ALU.mult, E["lb"][:, e, m:m + 1], ALU.add)
        K.act(f_[:, 0:T], f_[:, 0:T], AF.Ln)
        K.ts(s_[:, 0:T], s_[:, 0:T], E["noml"][:, e, m:m + 1], ALU.mult, E["oml"][:, e, m:m + 1], ALU.add)
        if ti.sample:
            K.tt(f_[:, 0:T], f_[:, 0:T], C["tokmask"][:, :], ALU.mult, eng="pool")
            K.tt(s_[:, 0:T], s_[:, 0:T], C["tokmask"][:, :], ALU.mult, eng="pool")
        K.scan(g_[:, 0:T], C["rm"][:, :], f_[:, 0:T], 0.0, ALU.mult, ALU.add)
        K.act(x_[:, 0:T], g_[:, 0:T], AF.Exp)
        K.stt(qt[:, m, :], qq[:, m, :], 128 ** -0.5, x_[:, 0:T], ALU.mult, ALU.mult)
        K.act(x_[:, 0:T], g_[:, 0:T], AF.Exp, scale=-1.0)
        K.tt(kt[:, m, :], s_[:, 0:T], x_[:, 0:T], ALU.mult)
        g3 = g_[:, 0:T].m(lambda ap: ap.rearrange("p (c t) -> p c t", t=64))
        gl = g3.m(lambda ap: ap[:, :, 63:64])
        K.act(dec[:, m, :, :], gl, AF.Exp)
        x3 = x_[:, 0:T].m(lambda ap: ap.rearrange("p (c t) -> p c t", t=64))
        K.tt(x3, gl.m(lambda ap: ap.to_broadcast([128, T // 64, 64])), g3, ALU.subtract)
        K.act(x_[:, 0:T], x_[:, 0:T], AF.Exp)
        K.tt(kh[:, m, :], s_[:, 0:T], x_[:, 0:T], ALU.mult)
    dense_fm(K, win, r0, KC, 1024, 1024, rhs, ev_f)
    if KSTOP <= 2:
        return
    vtok = E["vtok"]
    dense_tm(K, win, r0, 2048, 1024, hT, lambda b, cb, ps: K.copy(vtok[:, b, cb * 256:(cb + 1) * 256], ps[:, 0:256], eng=("act" if cb % 2 == 0 else "dve")))
    if KSTOP <= 3:
        return
    dense_fm(K, win, r0, KC, 3072, 1024, rhs, lambda m, ps: K.act(gate[:, m, :], ps[:, 0:T], AF.Silu))
    if KSTOP <= 4:
        return
    S, Sbf = E["S"], E["Sbf"]
    if ti.first:
        K.memset(S.all(), 0.0)
        K.memset(Sbf.all(), 0.0)
    for b in range(T // 128) if "H" not in DBG else ():
        c0 = b * 128
        if ti.sample:
            sq = ti.sq0 + b
            K.dma(S.all(), V(I["state_hgrn"][e, sq].rearrange("h k v -> k h v"), []))
            K.copy(Sbf.all(), S.all(), eng="pool")
        for h in range(8):
            ps = K.ps()
            K.mm(ps[:, 0:128], kt[:, h, c0:c0 + 128], qt[:, h, c0:c0 + 128])
            at = E["attnT"][h % 2]
            K.tt(at[:, :], ps[:, 0:128], C["maskBD"][:, :], ALU.mult)
            K.tr(ps.bf(slice(None), 512, 640), kh[:, h, c0:c0 + 128], C["identb"][:, :])
            khT = E["khT"][h % 2]
            K.copy(khT[:, :], ps.bf(slice(None), 512, 640), eng="act")
            po = K.ps()
            K.mm(po[:, 0:128], vtok[:, b, h * 128:(h + 1) * 128], at[:, :], start=True, stop=False)
            K.mm(po[:, 0:64], Sbf[:, h, :], qt[:, h, c0:c0 + 64], start=False, stop=ti.sample)
            pd = K.ps()
            K.mm(pd[:, 0:128], khT[0:64, :], vtok[0:64, b, h * 128:(h + 1) * 128])
            K.stt(S[:, h, :], S[:, h, :], dec[:, h, 2 * b, :], pd[:, 0:128], ALU.mult, ALU.add)
            if not ti.sample:
                K.copy(Sbf[:, h, :], S[:, h, :], eng="act")
                K.mm(po[:, 64:128], Sbf[:, h, :], qt[:, h, c0 + 64:c0 + 128], start=False, stop=True)
                pd = K.ps()
                K.mm(pd[:, 0:128], khT[64:128, :], vtok[64:128, b, h * 128:(h + 1) * 128])
                K.stt(S[:, h, :], S[:, h, :], dec[:, h, 2 * b + 1, :], pd[:, 0:128], ALU.mult, ALU.add)
                K.copy(Sbf[:, h, :], S[:, h, :], eng="act")
            K.copy(oT[:, h, c0:c0 + 128], po[:, 0:128], eng="act")
        if ti.sample:
            K.store(O["hgrn_s"][e, ti.sq0 + b].rearrange("h k v -> k h v"), S.base, S.bufs)
    if ti.last_prompt:
        K.store(O["hgrn_p"][e].rearrange("h k v -> k h v"), S.base, S.bufs)
    if KSTOP <= 5:
        return
    rmsnorm_fm(K, C, lambda c: oT[:, c, :], lambda c: E["gn"][:, e, c:c + 1], lambda c: oT[:, c, :], 8, 1024)
    for h in range(8):
        K.tt(oa[:, h, :], oT[:, h, :], gate[:, h, :], ALU.mult, eng=("dve" if h % 2 == 0 else "pool"))

    if KSTOP <= 6:
        return
    qbT, kbT, vtb, ktokf, vtokf, obT = E["qbT"], E["kbT"], E["vtb"], E["ktokf"], E["vtokf"], E["obT"]
    dense_fm(K, win, r0, KC, 4096, 1024, rhs, lambda m, ps: K.ts(qbT[0:64, m, :], ps[0:64, 0:T], 0.125, ALU.mult), mrows=64)
    if os.environ.get("KSUB") == "1":
        return
    dense_fm(K, win, r0, KC, 5120, 256, rhs, lambda m, ps: K.copy(kbT[0:64, m, 128:128 + T], ps[0:64, 0:T], eng="act"), mrows=64)

    KV = os.environ.get("KV", "")

    def ev_kv(b, cb, ps):
        if cb == 0:
            if "a" not in KV:
                K.copy(ktokf[:, b, :], ps[:, 0:256], eng="act")
        else:
            if "b" not in KV:
                K.copy(vtokf[:, b, :], ps[:, 0:256], eng="act")
            if "c" not in KV:
                K.copy(vtb[:, 1 + b, :], vtokf[:, b, :], eng="dve")
    if os.environ.get("KSUB") == "2":
        return
    dense_tm(K, win, r0, 5120, 512, hT, ev_kv)
    if KSTOP <= 7:
        return
    for b in range(T // 128) if "W" not in DBG else ():
        c0 = b * 128
        if ti.sample:
            sq = ti.sq0 + b
            kc32, vc32 = E["kc32"][b % 2], E["vc32"][b % 2]
            K.dma(kc32[:, :], V(I["cache_k"][e, sq].rearrange("s h d -> s (h d)"), []))
            K.dma(vc32[:, :], V(I["cache_v"][e, sq].rearrange("s h d -> s (h d)"), []))
            pst = K.ps()
            for kv in range(KVH):
                K.tr(pst[0:64, kv * 128:(kv + 1) * 128], kc32[:, kv * 64:(kv + 1) * 64], C["identf"][:, :])
            kcT = E["kcT"][b % 2]
            K.copy(kcT[0:64, :, :], pst.v3(slice(0, 64), 0, 512, 128), eng="act")
            vc = E["vc"][b % 2]
            K.copy(vc[:, :], vc32[:, :], eng="pool")
            kprev = lambda kv: kcT[0:64, kv, :]
            vprev = lambda kv: vc[:, kv * 64:(kv + 1) * 64]
            has_prev = True
            for nm, src32, inp in (("k_s", ktokf, "cache_k"), ("v_s", vtokf, "cache_v")):
                K.store(O[nm][e, sq, 120:128].rearrange("s h d -> s (h d)"), src32.base[0:8, b, :], [src32.bufs[b]])
                K.store(O[nm][e, sq, 0:120].rearrange("s h d -> s (h d)"), I[inp][e, sq, 8:128].rearrange("s h d -> s (h d)"), [src32.bufs[b]])
        else:
            kprev = (lambda b_: (lambda kv: kbT[0:64, kv, b_ * 128:(b_ + 1) * 128]))(b)
            vprev = (lambda b_: (lambda kv: vtb[:, b_, kv * 64:(kv + 1) * 64]))(b)
            has_prev = not (ti.first and b == 0)
        for kv in range(KVH):
            q3 = qbT[0:64, kv * 4:(kv + 1) * 4, c0:c0 + 128]
            pP, pC = E["pP"][kv % 2], E["pC"][kv % 2]
            v3f = lambda vv: vv.m(lambda ap: ap.rearrange("p (g q) -> p g q", g=4))
            if has_prev:
                psP = K.ps()
                K.mm(psP.v3(slice(None), 0, 512, 128), kprev(kv), q3)
                K.act(tf[0][:, :], psP[:, 0:512], AF.Exp)
                K.tt(v3f(pP[:, :]), v3f(tf[0][:, :]), E["EBp"][:, kv * 4:(kv + 1) * 4, :], ALU.mult)
            psC = K.ps()
            K.mm(psC.v3(slice(None), 0, 512, 128), kbT[0:64, kv, 128 + c0:128 + c0 + 128], q3)
            K.act(tf[1][:, :], psC[:, 0:512], AF.Exp)
            K.tt(v3f(pC[:, :]), v3f(tf[1][:, :]), E["EBc"][:, kv * 4:(kv + 1) * 4, :], ALU.mult, eng="pool")
            psN, psD = K.ps(), K.ps()
            if has_prev:
                K.mm(psN[0:64, 0:512], vprev(kv), pP[:, :], start=True, stop=False)
                K.mm(psD[0:64, 0:512], C["ones"][:, 0:64], pP[:, :], start=True, stop=False)
            K.mm(psN[0:64, 0:512], vtb[:, 1 + b, kv * 64:(kv + 1) * 64], pC[:, :], start=not has_prev, stop=True)
            K.mm(psD[0:64, 0:512], C["ones"][:, 0:64], pC[:, :], start=not has_prev, stop=True)
            den = E["den"]
            K.tt(v3f(den[0:64, :]), psD.v3(slice(0, 64), 0, 512, 128),
                 E["sinkexp"][0:64, e, kv * 4:(kv + 1) * 4, :].m(lambda ap: ap.to_broadcast([64, 4, 128])), ALU.add)
            K.recip(den[0:64, :], den[0:64, :])
            K.tt(obT[0:64, kv * 4:(kv + 1) * 4, c0:c0 + 128], psN.v3(slice(0, 64), 0, 512, 128), v3f(den[0:64, :]), ALU.mult)
    if not ti.sample:
        K.copy(kbT[0:64, :, 0:128], kbT[0:64, :, T:T + 128], eng="pool")
        K.copy(vtb[:, 0, :], vtb[:, 2, :], eng="pool")
        if ti.last_prompt:
            for nm, src32 in (("k_p", ktokf), ("v_p", vtokf)):
                K.store(O[nm][e].rearrange("s h d -> s (h d)"), src32.base[:, 1, :], [src32.bufs[1]])
    if KSTOP <= 8:
        return
    wout = Wv["w_out"]
    mo = C["mo"]
    for cb in range(0, D, 256):
        slotA = load_w(K, wout, e * D, 8, cb, 256)
        slotB = load_w(K, wout, e * D + 1024, 16, cb, 256, krows=64)
        for m in range(2):
            ps = K.ps()
            for kc in range(8):
                K.mm(ps[:, 0:T], slotA[:, kc, m * 128:(m + 1) * 128], oa[:, kc, :], start=(kc == 0), stop=False)
            for hh in range(16):
                K.mm(ps[:, 0:T], slotB[0:64, hh, m * 128:(m + 1) * 128], obT[0:64, hh, :], start=False, stop=(hh == 15))
            K.copy(mo[:, cb // 128 + m, :], ps[:, 0:T], eng=("act" if m == 0 else "dve"))
    post_residual2(K, C, mo, lambda c: C["g_mix_post"][:, layer, c:c + 1], xT)


def build_program(SEQ, DEPTH):
    NTP = SEQ // T
    NT = NTP + 2
    NE = (DEPTH + 1) // 2
    NO = DEPTH // 2
    nc = bass.Bass("TRN2", target_bir_lowering=False)
    dth = lambda name, shape, dtype=F32, kind="ExternalInput": nc.dram_tensor({"ExternalInput": "i_", "ExternalOutput": "o_", "Internal": "s_"}[kind] + name, list(shape), dtype, kind=kind)
    dt = lambda *a, **k: dth(*a, **k).ap()
    I = {}
    I["xp"] = dt("xp", [SEQ, D])
    I["xsm"] = dt("xsm", [512, D])
    for n in ["norm_mix_pre", "norm_mix_post", "norm_ffn_pre", "norm_ffn_post"]:
        I[n] = dt(n, [128, DEPTH, KC])
    I["w_up"] = dt("w_up", [DEPTH * D, DFF])
    I["w_down"] = dt("w_down", [DEPTH * DFF, D])
    I["w_in"] = dt("w_in", [NE * D, 5632])
    I["w_out"] = dt("w_out", [NE * D, D])
    I["state_hgrn"] = dt("state_hgrn", [NE, 4, 8, 128, 128])
    I["cache_k"] = dt("cache_k", [NE, 4, 128, 4, 64])
    I["cache_v"] = dt("cache_v", [NE, 4, 128, 4, 64])
    I["lb_raw"] = dt("lb_raw", [128, NE, 8])
    I["gn"] = dt("gn", [128, NE, 8])
    I["rel_bias"] = dt("rel_bias", [32, 16])
    I["sinks"] = dt("sinks", [64, NE, 16])
    if NO > 0:
        for n in ("wr", "wk", "wv", "wo"):
            I[n] = dt(n, [NO * D, D])
        I["w1"] = dt("w1", [NO * D, 96]); I["w2"] = dt("w2", [NO * 96, D])
        I["a1"] = dt("a1", [NO * D, 96]); I["a2"] = dt("a2", [NO * 96, D])
        I["g1"] = dt("g1", [NO * D, 256]); I["g2"] = dt("g2", [NO * 256, D])
        if NO > 1:
            I["v1"] = dt("v1", [(NO - 1) * D, 64]); I["v2"] = dt("v2", [(NO - 1) * 64, D])
            I["p_v0"] = dt("p_v0", [128, NO - 1, 16])
        I["p_mu"] = dt("p_mu", [128, NO, 6, 16])
        for n in ("w0", "a0", "kk", "ka", "rk", "lnx_g", "lnx_b"):
            I["p_" + n] = dt("p_" + n, [128, NO, 16])
        I["state_rwkv"] = dt("state_rwkv", [NO, 4, 32, 64, 64])
        I["state_shift"] = dt("state_shift", [128, NO, 4, 16])
        I["maskG"] = dt("maskG", [128, 512])
        I["onesblk"] = dt("onesblk", [128, 128])
    for n, shp in (("ident", [128, 128]), ("antiid", [128, 128]), ("oh384", [32, 384]), ("validm", [16, 384]), ("maskBD", [128, 128]),
                   ("rm", [128, T]), ("tokmask", [128, T])):
        I[n] = dt(n, shp)
    O = {"_buf": Buf("outputs")}
    O["yp"] = dt("yp", [SEQ, D], kind="ExternalOutput")
    O["ysm"] = dt("ysm", [4, 8, D], kind="ExternalOutput")
    O["hgrn_p"] = dt("hgrn_p", [NE, 8, 128, 128], kind="ExternalOutput")
    O["hgrn_s"] = dt("hgrn_s", [NE, 4, 8, 128, 128], kind="ExternalOutput")
    for nm in ("k_p", "v_p"):
        O[nm] = dt(nm, [NE, 128, 4, 64], kind="ExternalOutput")
    for nm in ("k_s", "v_s"):
        O[nm] = dt(nm, [NE, 4, 128, 4, 64], kind="ExternalOutput")
    if NO > 0:
        O["rwkv_p"] = dt("rwkv_p", [NO, 32, 64, 64], kind="ExternalOutput")
        O["rwkv_s"] = dt("rwkv_s", [NO, 4, 32, 64, 64], kind="ExternalOutput")
        O["shift_p"] = dt("shift_p", [NO, D], kind="ExternalOutput")
        O["shift_s"] = dt("shift_s", [NO, 4, D], kind="ExternalOutput")

    with ExitStack() as st:
        K = KB(nc, st)
        S = K.S
        W = {}
        Wl = {}
        wspecs = [("w_up", D, DFF, DEPTH), ("w_down", DFF, D, DEPTH), ("w_in", D, 5632, NE), ("w_out", D, D, NE)]
        if NO > 0:
            wspecs += [("wr", D, D, NO), ("wk", D, D, NO), ("wv", D, D, NO), ("wo", D, D, NO), ("w1", D, 96, NO), ("w2", 96, D, NO),
                       ("a1", D, 96, NO), ("a2", 96, D, NO), ("g1", D, 256, NO), ("g2", 256, D, NO)]
        if NO > 1:
            wspecs += [("v1", D, 64, NO - 1), ("v2", 64, D, NO - 1)]
        WS = {n_: (r_, c_) for n_, r_, c_, _ in wspecs}
        Lb = {}
        for l in range(DEPTH):
            Lb[(l, "mix")] = Buf("wmix%d" % l)
            Lb[(l, "mlp")] = Buf("wmlp%d" % l)
        for name, rows, cols, nl in wspecs:
            W[name] = TT(K, name + "16", [nl * rows, cols], BF16, kind="dram")
            for l in range(nl):
                if name in ("w_up", "w_down"):
                    Wl[(name, l)] = Lb[(l, "mlp")]
                elif name in ("w_in", "w_out"):
                    Wl[(name, l)] = Lb[(2 * l, "mix")]
                elif name in ("v1", "v2"):
                    Wl[(name, l)] = Lb[(2 * (l + 1) + 1, "mix")]
                else:
                    Wl[(name, l)] = Lb[(2 * l + 1, "mix")]
        xs = TT(K, "xs", [128, NT, KC, T], F32, kind="dram", nsub=NT)
        xsb3 = [Buf("xs%d" % i) for i in range(3)]
        xs.bufs = [xsb3[t % 3] for t in range(NT)]
        xs_vf = TT(K, "xs_vf", [128, NT, KC, T], F32, kind="dram", nsub=NT) if NO > 1 else None
        if xs_vf is not None:
            vfb3 = [Buf("xsvf%d" % i) for i in range(3)]
            xs_vf.bufs = [vfb3[t % 3] for t in range(NT)]
        cast_pairs = {}
        Edh = dth("Ed", [16, 384], F32, kind="Internal")
        Edbuf = Buf("Ed")

        def wview(name, l):
            tv = TT.__new__(TT)
            tv.base = W[name].base
            tv.name = name
            tv.nsub = 1
            tv.bufs = [Wl[(name, l)]]
            return tv

        def cast_weight(name, l, rows_per_layer, cols):
            step = max(1, (1 << 20) // cols)
            pairs = []
            rend = (l + 1) * rows_per_layer
            for r in range(l * rows_per_layer, rend, step):
                r1 = min(r + step, rend)
                pairs.append((W[name].base[r:r1, :], I[name][r:r1, :]))
            cast_pairs.setdefault(Wl[(name, l)].name, (Wl[(name, l)], []))[1].extend(pairs)

        C = {}
        C["ones"] = TT(K, "ones", [128, 128], BF16)
        C["identf"] = TT(K, "identf", [128, 128], F32)
        C["identb"] = TT(K, "identb", [128, 128], BF16)
        C["maskBD"] = TT(K, "maskBD", [128, 128], F32)
        C["rm"] = TT(K, "rm", [128, T], F32)
        C["tokmask"] = TT(K, "tokmask", [128, T], F32)
        C["sq"] = [TT(K, "sq%d" % i, [128, T], BF16) for i in range(2)]
        C["rstd"] = TT(K, "rstd", [128, T], F32)
        C["tmpa"] = [TT(K, "tmpa%d" % i, [128, T], BF16) for i in range(2)]
        C["tmpf"] = [TT(K, "tmpf%d" % i, [128, 512 if i < 2 else T], F32) for i in range(4)]
        C["hT"] = TT(K, "hT", [128, KC, T], BF16, nsub=KC)
        C["mo"] = TT(K, "mo", [128, KC, T], F32, nsub=KC)
        C["epsc"] = TT(K, "epsc", [128, 3], F32)
        xT = TT(K, "xT", [128, KC, T], F32, nsub=KC)
        for n, short in (("norm_mix_pre", "g_mix_pre"), ("norm_mix_post", "g_mix_post"), ("norm_ffn_pre", "g_ffn_pre"), ("norm_ffn_post", "g_ffn_post")):
            C[short] = TT(K, short, [128, DEPTH, KC], F32)
            K.pdma(C[short][:, :, :], I[n])
        K.memset(C["ones"][:, :], 1.0)
        K.memset(C["epsc"][:, 0:1], 1e-6)
        K.memset(C["epsc"][:, 1:2], 64e-5)
        K.memset(C["epsc"][:, 2:3], 1e-24)
        R = {}
        if NO > 0:
            C["maskG"] = TT(K, "maskG_sb", [128, 512], F32)
            K.pdma(C["maskG"][:, :], I["maskG"])
            ob32 = TT(K, "ob32", [128, 128], F32)
            K.pdma(ob32[:, :], I["onesblk"])
            C["ones_blk"] = TT(K, "ones_blk", [128, 128], BF16)
            P = {}
            for n in ("w0", "a0", "kk", "ka", "rk", "lnx_g", "lnx_b"):
                P[n] = TT(K, "P_" + n, [128, NO, 16], F32)
                K.pdma(P[n][:, :, :], I["p_" + n])
            P["omka"] = TT(K, "P_omka", [128, NO, 16], F32)
            if NO > 1:
                P["v0"] = TT(K, "P_v0", [128, NO - 1, 16], F32)
                K.pdma(P["v0"][:, :, :], I["p_v0"])
            R["mu"] = TT(K, "P_mu", [128, NO, 6, 16], F32)
            K.pdma(R["mu"][:, :, :, :], I["p_mu"])
            R["P"] = P
        for n in ("maskBD", "rm", "tokmask"):
            K.pdma(C[n][:, :], I[n])
        K.pdma(C["identf"][:, :], I["ident"])
        K.flush_params()
        K.copy(C["identb"][:, :], C["identf"][:, :])
        if NO > 0:
            K.copy(C["ones_blk"][:, :], ob32[:, :])
            K.ts(R["P"]["omka"][:, :, :], R["P"]["ka"][:, :, :], -1.0, ALU.mult, 1.0, ALU.add)
        arH = Arena(K, "arH", 32 * 1024)
        arM = Arena(K, "arM", 85 * 1024)
        C["hid"] = arH.carve("hid", [128, 64, T], BF16, nsub=64)

        for l in range(DEPTH):
            if l % 2 == 0:
                cast_weight("w_in", l // 2, D, 5632)
                cast_weight("w_out", l // 2, D, D)
            else:
                oo = l // 2
                for n_ in ("w1", "a1", "g1", "w2", "a2", "g2", "wr", "wk", "wv", "wo"):
                    cast_weight(n_, oo, WS[n_][0], WS[n_][1])
                if oo >= 1:
                    cast_weight("v1", oo - 1, D, 64)
                    cast_weight("v2", oo - 1, 64, D)
            cast_weight("w_up", l, D, DFF)
            cast_weight("w_down", l, DFF, D)
            for kind_ in ("mix", "mlp"):
                bb, prs = cast_pairs[Lb[(l, kind_)].name]
                K.dma_multi(prs, bb, [], eng="pool")

        E = {}
        if NE > 0:
            E["lb"] = TT(K, "lb", [128, NE, 8], F32)
            E["oml"] = TT(K, "oml", [128, NE, 8], F32)
            E["noml"] = TT(K, "noml", [128, NE, 8], F32)
            E["gn"] = TT(K, "gn", [128, NE, 8], F32)
            lbr = TT(K, "lbr", [128, NE, 8], F32)
            E["sinkexp"] = TT(K, "sinkexp", [64, NE, 16, 1], F32)
            E["EBp"] = TT(K, "EBp", [128, 16, 128], BF16)
            E["EBc"] = TT(K, "EBc", [128, 16, 128], BF16)
            arH.reset()
            brev = arH.carve("brev", [128, 16, 128], F32)
            relb = arH.carve("relb", [32, 16], F32)
            oh = arH.carve("oh", [32, 384], F32)
            vm = arH.carve("vm", [16, 384], F32)
            Et = arH.carve("Et", [16, 384], F32)
            antiid = arH.carve("antiid", [128, 128], F32)
            K.pdma(lbr[:, :, :], I["lb_raw"])
            K.pdma(E["gn"][:, :, :], I["gn"])
            K.pdma(E["sinkexp"][:, :, :, :], I["sinks"].rearrange("p e (h o) -> p e h o", o=1))
            K.pdma(relb[:, :], I["rel_bias"])
            K.pdma(oh[:, :], I["oh384"])
            K.pdma(vm[:, :], I["validm"])
            K.pdma(antiid[:, :], I["antiid"])
            K.flush_params()
            K.memset(E["lb"][:, :, :], 0.0)
            if NE > 1:
                K.tt(lbr[:, 1, :], lbr[:, 1, :], lbr[:, 0, :], ALU.subtract)
                K.act(E["lb"][:, 1, :], lbr[:, 1, :], AF.Sigmoid)
            K.ts(E["oml"][:, :, :], E["lb"][:, :, :], -1.0, ALU.mult, 1.0, ALU.add)
            K.ts(E["noml"][:, :, :], E["oml"][:, :, :], -1.0, ALU.mult)
            K.act(E["sinkexp"][:, :, :, :], E["sinkexp"][:, :, :, :], AF.Exp)
            ps = K.ps()
            if "E" in DBG:
                K.memset(E["EBp"].all(), 1.0)
                K.memset(E["EBc"].all(), 1.0)
            K.mm(ps[0:16, 0:384], relb[:, :], oh[:, :])
            K.act(Et[:, :], ps[0:16, 0:384], AF.Exp)
            K.tt(Et[:, :], Et[:, :], vm[:, :], ALU.mult)
            K.S.dma("sp", [lambda en: en.dma_start(out=Edh.ap(), in_=Et.base)], Edbuf, Et.bufs)
            for off, name in ((1, "EBc"), (129, "EBp")) if "E" not in DBG else ():
                src = bass.AP(Edh, off, [[1, 128], [384, 16], [1, 128]])
                K.S.dma("sp", [(lambda s_: (lambda en: en.dma_start(out=brev.base, in_=s_)))(src)], brev.bufs, [Edbuf])
                for g in range(4):
                    ps = K.ps()
                    K.mm(ps.v3(slice(None), 0, 512, 128), antiid[:, :], brev[:, g * 4:(g + 1) * 4, :])
                    K.copy(E[name][:, g * 4:(g + 1) * 4, :], ps.v3(slice(None), 0, 512, 128), eng=("act" if g % 2 == 0 else "dve"))

        S.barrier()
        arH.reset()
        xin = [arH.carve("xin%d" % i, [128, 2, D], F32) for i in range(2)]
        for t in range(NT):
            xi = xin[t % 2]
            if t < NTP:
                src = I["xp"][t * T:(t + 1) * T, :].rearrange("(b p) d -> p b d", p=128)
            else:
                src = I["xsm"][(t - NTP) * T:(t - NTP + 1) * T, :].rearrange("(b p) d -> p b d", p=128)
            K.dma(xi[:, :, :], V(src, []))
            for c in range(KC):
                ps = K.ps()
                for b in range(2):
                    K.tr(ps[:, b * 128:(b + 1) * 128], xi[:, b, c * 128:(c + 1) * 128], C["identf"][:, :])
                K.copy(xT[:, c, :], ps[:, 0:T], eng=("act" if c % 2 == 0 else "dve"))
            K.dma(xs[:, t, :, :], xT.all(), eng="act")
        S.barrier()
        arH.reset()
        C["hid"] = arH.carve("hid", [128, 64, T], BF16, nsub=64)

        for layer in range(DEPTH):
            Wv = {"w_up": wview("w_up", layer), "w_down": wview("w_down", layer)}
            S.barrier()
            arM.reset()
            if layer % 2 == 0:
                e = layer // 2
                Wv["w_in"] = wview("w_in", e)
                Wv["w_out"] = wview("w_out", e)
                for nm in ("qq", "qt", "kt", "kh", "gate", "oa"):
                    E[nm] = arM.carve(nm, [128, 8, T], BF16, nsub=8)
                E["oT"] = arM.carve("oT", [128, 8, T], F32, nsub=8)
                E["dec"] = arM.carve("dec", [128, 8, T // 64, 1], F32, nsub=8)
                E["vtok"] = arM.carve("vtok", [128, 2, 1024], BF16, nsub=2)
                E["S"] = arM.carve("S", [128, 8, 128], F32, nsub=8)
                E["Sbf"] = arM.carve("Sbf", [128, 8, 128], BF16, nsub=8)
                E["attnT"] = [arM.carve("attnT%d" % i, [128, 128], BF16) for i in range(2)]
                E["khT"] = [arM.carve("khT%d" % i, [128, 128], BF16) for i in range(2)]
                E["qbT"] = arM.carve("qbT", [64, 16, T], BF16, nsub=16)
                E["obT"] = arM.carve("obT", [64, 16, T], BF16, nsub=16)
                E["kbT"] = arM.carve("kbT", [64, 4, 128 + T], BF16, nsub=4)
                E["vtb"] = arM.carve("vtb", [128, 3, 256], BF16, nsub=3)
                E["ktokf"] = arM.carve("ktokf", [128, 2, 256], F32, nsub=2)
                E["vtokf"] = arM.carve("vtokf", [128, 2, 256], F32, nsub=2)
                E["kc32"] = [arM.carve("kc32%d" % i, [128, 256], F32) for i in range(2)]
                E["vc32"] = [arM.carve("vc32%d" % i, [128, 256], F32) for i in range(2)]
                E["kcT"] = [arM.carve("kcT%d" % i, [64, 4, 128], BF16) for i in range(2)]
                E["vc"] = [arM.carve("vc%d" % i, [128, 256], BF16) for i in range(2)]
                E["pP"] = [arM.carve("pP%d" % i, [128, 512], BF16) for i in range(2)]
                E["pC"] = [arM.carve("pC%d" % i, [128, 512], BF16) for i in range(2)]
                E["den"] = arM.carve("den", [64, 512], F32)
            else:
                oo = layer // 2
                for n_ in ("w1", "a1", "g1", "w2", "a2", "g2", "wr", "wk", "wv", "wo"):
                    Wv[n_] = wview(n_, oo)
                if oo >= 1:
                    Wv["v1"] = wview("v1", oo - 1)
                    Wv["v2"] = wview("v2", oo - 1)
                R["xx"] = arM.carve("xx", [128, KC, T], BF16, nsub=KC)
                R["l1"] = arM.carve("l1", [128, 5, T], BF16, nsub=5)
                R["carry"] = arM.carve("carry", [128, KC], F32)
                R["shs"] = arM.carve("shs", [128, 2, KC], F32, nsub=2)
                R["shout"] = arM.carve("shout", [128, 2, KC], F32)
                for nm in ("rg", "kg", "vg", "lwg", "ag", "vfg"):
                    R[nm] = arM.carve(nm, [128, 2, T], F32, nsub=2)
                R["gg"] = arM.carve("gg", [128, 2, T], BF16, nsub=2)
                R["ft"] = [arM.carve("ft%d" % i, [128, T], F32) for i in range(11)]
                R["sho"] = R["ft"][10]
                nbd = 1 if RD == F32 else 2
                R["bd"] = [arM.carve("bd%d" % i, [128, 6, T // 64, 128], RD) for i in range(nbd)]
                R["rt"] = [arM.carve("rt%d" % i, [128, T], RD) for i in range(nbd)]
                R["decr"] = [arM.carve("decr%d" % i, [128, T // 64, 1], F32) for i in range(2)]
                R["tok"] = [arM.carve("tok%d" % i, [128, 384], RD) for i in range(4)]
                R["gs"] = [arM.carve("gs%d" % i, [128, 512], RD) for i in range(4)]
                R["Q"] = [arM.carve("Q%d" % i, [128, 128], RD) for i in range(4)]
                R["Pn"] = [arM.carve("Pn%d" % i, [128, 256], RD) for i in range(4)]
                R["xsb"] = arM.carve("xsb", [128, 128], RD)
                R["nu"] = arM.carve("nu", [128, 128], RD)
                R["H"] = arM.carve("H", [128, 16, 128], F32, nsub=16)
                R["Hbf"] = arM.carve("Hbf", [128, 16, 128], BF16, nsub=16) if RD == BF16 else R["H"]
                R["stg"] = arM.carve("stg", [128, 16, 128], F32)
                for i in range(len(R["bd"])):
                    K.memset(R["bd"][i].all(), 0.0)
                K.memset(R["stg"].all(), 0.0)
            for t in range(NT):
                ti = TileInfo(t, NTP)
                K.dma(xT.all(), xs[:, t, :, :])
                if layer % 2 == 0:
                    even_tile(K, C, E, Wv, I, O, layer // 2, layer, ti, xT)
                else:
                    S.barrier()
                    arH.reset()
                    for nm in ("xr", "xk", "xv", "xm"):
                        R[nm] = arH.carve(nm, [128, KC, T], BF16, nsub=KC)
                    odd_tile(K, C, R, Wv, I, O, layer // 2, layer, ti, xT, xs_vf)
                    S.barrier()
                    arH.reset()
                    C["hid"] = arH.carve("hid", [128, 64, T], BF16, nsub=64)
                mlp_tile(K, C, Wv, layer, xT)
                K.dma(xs[:, t, :, :], xT.all(), eng="act")

        S.barrier()
        arH.reset()
        yo = [arH.carve("yo%d" % i, [128, 2, D], F32) for i in range(2)]
        for t in range(NT):
            K.dma(xT.all(), xs[:, t, :, :])
            y = yo[t % 2]
            for b in range(2):
                for c4 in range(4):
                    ps = K.ps()
                    for j in range(4):
                        c = c4 * 4 + j
                        K.tr(ps[:, j * 128:(j + 1) * 128], xT[:, c, b * 128:(b + 1) * 128], C["identf"][:, :])
                    K.copy(y[:, b, c4 * 512:(c4 + 1) * 512], ps[:, 0:512], eng=("act" if c4 % 2 == 0 else "dve"))
            if t < NTP:
                dst = O["yp"][t * T:(t + 1) * T, :].rearrange("(b p) d -> p b d", p=128)
                K.store(dst, y.base, y.bufs)
            else:
                for b in range(2):
                    sq = (t - NTP) * 2 + b
                    K.store(O["ysm"][sq], y.base[0:8, b, :], y.bufs)
        S.final_wait("sp")
        S.replay()
    return nc


def t5_bucket_np(dist):
    max_exact = 16
    d = np.maximum(dist, 0)
    large = max_exact + (np.log(np.maximum(d, max_exact).astype(np.float32) / max_exact)
                         / math.log(128 / max_exact) * (32 - max_exact)).astype(np.int32)
    large = np.minimum(large, 31)
    return np.where(d < max_exact, d, large).astype(np.int32)


def make_consts():
    c = {}
    c["ident"] = np.eye(128, dtype=np.float32)
    c["antiid"] = np.ascontiguousarray(np.eye(128, dtype=np.float32)[::-1])
    oh = np.zeros((32, 384), np.float32)
    bk = t5_bucket_np(np.arange(128))
    oh[bk, 128 + np.arange(128)] = 1.0
    c["oh384"] = oh
    vm = np.zeros((16, 384), np.float32)
    vm[:, 128:256] = 1.0
    c["validm"] = vm
    j = np.arange(128)[:, None]
    i = np.arange(128)[None, :]
    c["maskBD"] = ((j <= i) & (j // 64 == i // 64)).astype(np.float32)
    rm = np.ones((128, T), np.float32)
    rm[:, ::64] = 0.0
    c["rm"] = rm
    tm = np.zeros((128, T), np.float32)
    for b in range(T // 128):
        tm[:, b * 128:b * 128 + 8] = 1.0
    c["tokmask"] = tm
    p = np.arange(128)[:, None] % 64
    fcol = np.arange(128)[None, :] % 64
    mg = np.zeros((128, 512), np.float32)
    mg[:, 0:128] = (fcol > p)
    mg[:, 128:256] = (fcol > p)
    mg[:, 256:384] = (fcol < p)
    t64 = np.arange(64)[None, :]
    mg[:, 384:448] = (t64 >= p)
    mg[:, 448:512] = (t64 >= p)
    c["maskG"] = mg
    ob = np.zeros((128, 128), np.float32)
    ob[0:64, 0:64] = 1.0
    ob[64:128, 64:128] = 1.0
    c["onesblk"] = ob
    return c


def kernel(_cfg=None, **inp):
    SEQ, DEPTH = (4096, 4) if _cfg is None else _cfg
    NE = (DEPTH + 1) // 2
    NO = DEPTH // 2
    nc = build_program(SEQ, DEPTH)
    consts = make_consts()
    f = lambda a: np.ascontiguousarray(np.asarray(a, dtype=np.float32))

    def pc(a):
        a = np.asarray(a, dtype=np.float32)
        lead = a.shape[:-1]
        n = a.shape[-1] // 128
        a = a.reshape(lead + (n, 128))
        return np.ascontiguousarray(np.moveaxis(a, -1, 0))
    shared = {}
    for n in ["norm_mix_pre", "norm_mix_post", "norm_ffn_pre", "norm_ffn_post"]:
        shared[n] = pc(inp[n])
    shared["w_up"] = f(inp["w_up"]).reshape(DEPTH * D, DFF)
    shared["w_down"] = f(inp["w_down"]).reshape(DEPTH * DFF, D)
    shared["w_in"] = f(inp["w_in_even"]).reshape(NE * D, 5632)
    shared["w_out"] = f(inp["w_out_even"]).reshape(NE * D, D)
    shared["lb_raw"] = pc(inp["hgrn_lb_raw"])
    shared["gn"] = pc(inp["hgrn_norm_g"])
    shared["rel_bias"] = f(inp["rel_bias"])
    shared["sinks"] = np.ascontiguousarray(np.broadcast_to(f(inp["attn_sinks"])[None], (64, NE, 16)))
    if NO > 0:
        for n, src in (("wr", "rw_wr"), ("wk", "rw_wk"), ("wv", "rw_wv"), ("wo", "rw_wo")):
            shared[n] = f(inp[src]).reshape(NO * D, D)
        shared["w1"] = f(inp["rw_w1"]).reshape(NO * D, 96); shared["w2"] = f(inp["rw_w2"]).reshape(NO * 96, D)
        shared["a1"] = f(inp["rw_a1"]).reshape(NO * D, 96); shared["a2"] = f(inp["rw_a2"]).reshape(NO * 96, D)
        shared["g1"] = f(inp["rw_g1"]).reshape(NO * D, 256); shared["g2"] = f(inp["rw_g2"]).reshape(NO * 256, D)
        if NO > 1:
            shared["v1"] = f(inp["rw_v1"]).reshape((NO - 1) * D, 64); shared["v2"] = f(inp["rw_v2"]).reshape((NO - 1) * 64, D)
            shared["p_v0"] = pc(inp["rw_v0"])
        shared["p_mu"] = pc(inp["rw_mu"])
        for n, src in (("w0", "rw_w0"), ("a0", "rw_a0"), ("kk", "rw_kk"), ("ka", "rw_ka"), ("lnx_g", "rw_lnx_g"), ("lnx_b", "rw_lnx_b")):
            shared["p_" + n] = pc(inp[src])
        shared["p_rk"] = pc(np.asarray(inp["rw_rk"]).reshape(NO, D))
    shared.update(consts)
    in_maps = []
    for c in range(NCORES):
        m = dict(shared)
        m["xp"] = f(inp["x_prompt"][c % 2])
        xs_ = np.zeros((4, 128, D), np.float32)
        xs_[:, 0:8, :] = np.asarray(inp["x_sample"])[4 * c:4 * c + 4]
        m["xsm"] = xs_.reshape(512, D)
        m["state_hgrn"] = f(inp["state_hgrn"][:, 4 * c:4 * c + 4])
        m["cache_k"] = f(inp["cache_swa_k"][:, 4 * c:4 * c + 4])
        m["cache_v"] = f(inp["cache_swa_v"][:, 4 * c:4 * c + 4])
        if NO > 0:
            m["state_rwkv"] = f(inp["state_rwkv"][:, 4 * c:4 * c + 4])
            m["state_shift"] = pc(inp["state_shift"][:, 4 * c:4 * c + 4])
        in_maps.append({"i_" + k: v for k, v in m.items()})
    ncr = int(os.environ.get("K_NCORES", NCORES))
    res = run_bass_kernel_spmd(nc, in_maps[:ncr], core_ids=list(range(ncr)))
    R = [{k[2:]: v for k, v in r.items()} for r in res.results]
    R = (R * NCORES)[:NCORES]
    cat = lambda nm, ax: np.concatenate([R[c][nm] for c in range(NCORES)], axis=ax)
    y_prompt = np.stack([R[0]["yp"], R[1]["yp"]], 0)
    y_sample = cat("ysm", 0)
    hgrn_p = np.stack([R[0]["hgrn_p"], R[1]["hgrn_p"]], 1)
    hgrn_s = cat("hgrn_s", 1)
    k_p = np.stack([R[0]["k_p"], R[1]["k_p"]], 1)
    v_p = np.stack([R[0]["v_p"], R[1]["v_p"]], 1)
    k_s = cat("k_s", 1)
    v_s = cat("v_s", 1)
    if NO == 0:
        return (y_prompt, y_sample, hgrn_p, hgrn_s, k_p, k_s, v_p, v_s)
    rwkv_p = np.stack([R[0]["rwkv_p"], R[1]["rwkv_p"]], 1)
    rwkv_s = cat("rwkv_s", 1)
    shift_p = np.stack([R[0]["shift_p"], R[1]["shift_p"]], 1)
    shift_s = cat("shift_s", 1)
    return (y_prompt, y_sample, hgrn_p, hgrn_s, k_p, k_s, v_p, v_s, rwkv_p, rwkv_s, shift_p, shift_s)


def odd_tile(K, C, R, Wv, I, O, o, layer, ti, xT, xs_vf):
    hT = C["hT"]
    tf = C["tmpf"]
    NCH = T // 64
    ps = K.ps()
    for c in range(KC):
        sq = C["sq"][c % 2]
        K.act(sq[:, :], xT[:, c, :], AF.Square)
        K.mm(ps[:, 0:T], C["ones"][:, :], sq[:, :], start=(c == 0), stop=(c == KC - 1))
    rstd = C["rstd"]
    K.act(rstd[:, :], ps[:, 0:T], AF.Ln, bias=C["epsc"][:, 0:1], scale=1.0 / D)
    K.act(rstd[:, :], rstd[:, :], AF.Exp, scale=-0.5)
    carry = R["carry"]
    xx = R["xx"]
    if ti.first:
        K.memset(carry[:, :], 0.0)
    if ti.sample:
        shs = R["shs"]
        for b in range(2):
            K.dma(shs[:, b, :], V(I["state_shift"][:, o, ti.sq0 + b, :], []))
    for c in range(KC):
        hx = tf[c % 2]
        K.stt(hx[:, 1:T + 1], xT[:, c, :], C["g_mix_pre"][:, layer, c:c + 1], rstd[:, :], ALU.mult, ALU.mult)
        if ti.sample:
            K.copy(hx[:, 0:1], shs[:, 0, c:c + 1], eng="pool")
        else:
            K.copy(hx[:, 0:1], carry[:, c:c + 1], eng="pool")
        K.copy(hT[:, c, :], hx[:, 1:T + 1], eng="act")
        K.tt(xx[:, c, :], hx[:, 0:T], hx[:, 1:T + 1], ALU.subtract)
        if ti.sample:
            K.tt(xx[:, c, 128:129], shs[:, 1, c:c + 1], hx[:, 129:130], ALU.subtract, eng="pool")
            for b in range(2):
                K.copy(R["shout"][:, b, c:c + 1], hx[:, b * 128 + 8:b * 128 + 9], eng="pool")
        else:
            K.copy(carry[:, c:c + 1], hx[:, T:T + 1], eng="pool")
    if ti.sample or ti.last_prompt:
        pst = K.ps()
        srcs = [R["shout"][:, b, :] for b in range(2)] if ti.sample else [carry[:, :]]
        for i_, s_ in enumerate(srcs):
            K.tr(pst[0:16, i_ * 128:(i_ + 1) * 128], s_, C["identf"][:, :])
        sho = R["sho"]
        K.copy(sho[0:16, 0:128 * len(srcs)], pst[0:16, 0:128 * len(srcs)], eng="act")
        if ti.sample:
            for b in range(2):
                K.store(O["shift_s"][o, ti.sq0 + b].rearrange("(c p) -> c p", p=128), sho.base[0:16, b * 128:(b + 1) * 128], sho.bufs)
        else:
            K.store(O["shift_p"][o].rearrange("(c p) -> c p", p=128), sho.base[0:16, 0:128], sho.bufs)

    mu = R["mu"]

    def build_mix(dst, i):
        for c in range(KC):
            K.stt(dst[:, c, :], xx[:, c, :], mu[:, o, i, c:c + 1], hT[:, c, :], ALU.mult, ALU.add, eng="dve")
    xr, xk, xv, xm = R["xr"], R["xk"], R["xv"], R["xm"]
    l1 = R["l1"]
    build_mix(xm, 1)
    dense_fm(K, Wv["w1"], o * D, KC, 0, 96, lambda kc: xm[:, kc, :], lambda m, ps: K.act(l1[0:96, 0, :], ps[0:96, 0:T], AF.Tanh), mrows=96)
    build_mix(xm, 4)
    dense_fm(K, Wv["a1"], o * D, KC, 0, 96, lambda kc: xm[:, kc, :], lambda m, ps: K.copy(l1[0:96, 1, :], ps[0:96, 0:T], eng="act"), mrows=96)
    build_mix(xm, 5)
    dense_fm(K, Wv["g1"], o * D, KC, 0, 256, lambda kc: xm[:, kc, :], lambda m, ps: K.act(l1[:, 2 + m, :], ps[:, 0:T], AF.Sigmoid))
    build_mix(xv, 3)
    if o >= 1:
        dense_fm(K, Wv["v1"], (o - 1) * D, KC, 0, 64, lambda kc: xv[:, kc, :], lambda m, ps: K.copy(l1[0:64, 4, :], ps[0:64, 0:T], eng="act"), mrows=64)
    build_mix(xr, 0)
    build_mix(xk, 2)

    H, Hbf = R["H"], R["Hbf"]
    if ti.first:
        K.memset(H.all(), 0.0)
        if Hbf is not H:
            K.memset(Hbf.all(), 0.0)
    passes = [[0, 1, 2, 3]] if not ti.sample else [[0], [2]]
    yg = R["xx"]
    for pi, chunks in enumerate(passes):
        if ti.sample:
            sq = ti.sq0 + pi
            stg = R["stg"]
            for hh in range(2):
                K.dma(stg[hh * 64:(hh + 1) * 64, :, hh * 64:(hh + 1) * 64],
                      V(I["state_rwkv"][o, sq].rearrange("(hp two) i j -> two i hp j", two=2)[hh], []))
            for g4 in range(4):
                pst = K.ps()
                for j in range(4):
                    K.tr(pst[:, j * 128:(j + 1) * 128], stg[:, g4 * 4 + j, :], C["identf"][:, :])
                K.copy(H[:, g4 * 4:(g4 + 1) * 4, :], pst.v3(slice(None), 0, 512, 128), eng="act")
                if Hbf is not H:
                    K.copy(Hbf[:, g4 * 4:(g4 + 1) * 4, :], H[:, g4 * 4:(g4 + 1) * 4, :], eng="pool")
        rwkv_groups(K, C, R, Wv, I, O, o, ti, chunks, yg, xs_vf, first_pass=(pi == 0))
        if ti.sample or ti.last_prompt:
            stg = R["stg"]
            for g4 in range(4):
                pst = K.ps()
                for j in range(4):
                    K.tr(pst[:, j * 128:(j + 1) * 128], H[:, g4 * 4 + j, :], C["identf"][:, :])
                K.copy(stg[:, g4 * 4:(g4 + 1) * 4, :], pst.v3(slice(None), 0, 512, 128), eng="act")
            dst = (O["rwkv_s"][o, ti.sq0 + pi] if ti.sample else O["rwkv_p"][o]).rearrange("(hp two) i j -> two i hp j", two=2)
            for hh in range(2):
                K.store(dst[hh], stg.base[hh * 64:(hh + 1) * 64, :, hh * 64:(hh + 1) * 64], stg.bufs)
            if ti.sample:
                pass
    mo = C["mo"]
    dense_fm(K, Wv["wo"], o * D, KC, 0, D, lambda kc: yg[:, kc, :],
             lambda m, ps: K.copy(mo[:, m, :], ps[:, 0:T], eng=("act" if m % 2 == 0 else "dve")))
    post_residual2(K, C, mo, lambda c: C["g_mix_post"][:, layer, c:c + 1], xT)


def rwkv_groups(K, C, R, Wv, I, O, o, ti, chunks, yg, xs_vf, first_pass):
    tf = C["tmpf"]
    xr, xk, xv = R["xr"], R["xk"], R["xv"]
    l1 = R["l1"]
    H, Hbf = R["H"], R["Hbf"]
    P = R["P"]
    ft = R["ft"]
    for grp in range(8):
        rg, kg, vg, lwg, ag, gg = R["rg"], R["kg"], R["vg"], R["lwg"], R["ag"], R["gg"]
        c0 = grp * 256
        dense_fm(K, Wv["wr"], o * D, KC, c0, 256, lambda kc: xr[:, kc, :], lambda m, ps: K.copy(rg[:, m, :], ps[:, 0:T], eng="act"))
        dense_fm(K, Wv["wk"], o * D, KC, c0, 256, lambda kc: xk[:, kc, :], lambda m, ps: K.copy(kg[:, m, :], ps[:, 0:T], eng="act"))
        dense_fm(K, Wv["wv"], o * D, KC, c0, 256, lambda kc: xv[:, kc, :], lambda m, ps: K.copy(vg[:, m, :], ps[:, 0:T], eng="act"))
        def ev_w(m, ps):
            hp = grp * 2 + m
            K.act(lwg[:, m, :], ps[:, 0:T], AF.Sigmoid, bias=P["w0"][:, o, hp:hp + 1])
            K.ts(lwg[:, m, :], lwg[:, m, :], -math.exp(-0.5), ALU.mult)
            if ti.sample:
                K.tt(lwg[:, m, :], lwg[:, m, :], C["tokmask"][:, :], ALU.mult, eng="pool")
        dense_fm(K, Wv["w2"], o * 96, 1, c0, 256, lambda kc: l1[0:96, 0, :], ev_w, krows=96)

        def ev_a(m, ps):
            hp = grp * 2 + m
            K.act(ag[:, m, :], ps[:, 0:T], AF.Sigmoid, bias=P["a0"][:, o, hp:hp + 1])
        dense_fm(K, Wv["a2"], o * 96, 1, c0, 256, lambda kc: l1[0:96, 1, :], ev_a, krows=96)
        dense_fm(K, Wv["g2"], o * 256, 2, c0, 256, lambda kc: l1[:, 2 + kc, :], lambda m, ps: K.copy(gg[:, m, :], ps[:, 0:T], eng="act"))
        if o >= 1:
            vfg = R["vfg"]
            K.dma(vfg.all(), xs_vf[:, ti.t, grp * 2:(grp + 1) * 2, :])

            def ev_v(m, ps):
                hp = grp * 2 + m
                sv, dd = ft[0], ft[1]
                K.act(sv[:, :], ps[:, 0:T], AF.Sigmoid, bias=P["v0"][:, o - 1, hp:hp + 1])
                K.tt(dd[:, :], vfg[:, m, :], vg[:, m, :], ALU.subtract)
                K.tt(dd[:, :], dd[:, :], sv[:, :], ALU.mult)
                K.tt(vg[:, m, :], vg[:, m, :], dd[:, :], ALU.add)
            dense_fm(K, Wv["v2"], (o - 1) * 64, 1, c0, 256, lambda kc: l1[0:64, 4, :], ev_v, krows=64)
        elif first_pass and xs_vf is not None:
            K.dma(xs_vf[:, ti.t, grp * 2:(grp + 1) * 2, :], vg.all(), eng="act")
        for m in range(2):
            rwkv_hp(K, C, R, o, ti, chunks, grp * 2 + m, rg[:, m, :], kg[:, m, :], vg[:, m, :], lwg[:, m, :], ag[:, m, :], gg[:, m, :], yg)


def rwkv_hp(K, C, R, o, ti, chunks, hp, r, k, v, lw, a, g, yg):
    P = R["P"]
    ft = R["ft"]
    H, Hbf = R["H"], R["Hbf"]
    NCH = T // 64
    pcol = lambda nm: P[nm][:, o, hp:hp + 1]
    t_kk, t_kap, t_kp, t_b, t_G, t_e, t_x = ft[2], ft[3], ft[4], ft[5], ft[6], ft[7], ft[8]
    K.ts(t_kk[:, :], k, pcol("kk"), ALU.mult)
    sqb = C["sq"][0]
    K.act(sqb[:, :], t_kk[:, :], AF.Square)
    ps = K.ps()
    K.mm(ps[:, 0:T], C["ones_blk"][:, :], sqb[:, :])
    K.act(t_x[:, :], ps[:, 0:T], AF.Ln, bias=C["epsc"][:, 2:3])
    K.act(t_x[:, :], t_x[:, :], AF.Exp, scale=-0.5)
    K.tt(t_kap[:, :], t_kk[:, :], t_x[:, :], ALU.mult)
    if ti.sample:
        K.tt(t_kap[:, :], t_kap[:, :], C["tokmask"][:, :], ALU.mult, eng="pool")
    K.ts(t_x[:, :], a, pcol("ka"), ALU.mult, pcol("omka"), ALU.add)
    K.tt(t_kp[:, :], k, t_x[:, :], ALU.mult)
    if ti.sample:
        K.tt(t_kp[:, :], t_kp[:, :], C["tokmask"][:, :], ALU.mult, eng="pool")
    K.tt(t_b[:, :], t_kap[:, :], a, ALU.mult)
    K.stt(t_x[:, :], r, pcol("rk"), t_kp[:, :], ALU.mult, ALU.mult)
    sq1 = C["sq"][1]
    K.copy(sq1[:, :], t_x[:, :], eng="act")
    psb = K.ps()
    K.mm(psb[:, 0:T], C["ones_blk"][:, :], sq1[:, :])
    bonus = ft[9]
    K.tt(bonus[:, :], psb[:, 0:T], v, ALU.mult)
    K.scan(t_G[:, :], C["rm"][:, :], lw, 0.0, ALU.mult, ALU.add)
    bd = R["bd"][hp % len(R["bd"])]
    identR = C["identf"] if RD == F32 else C["identb"]
    rt = R["rt"][hp % len(R["rt"])]
    dec = R["decr"][hp % 2]
    K.act(t_e[:, :], t_G[:, :], AF.Exp)
    K.tt(rt[:, :], r, t_e[:, :], ALU.mult)

    def to_bd(idx, a_, b_):
        for hh in range(2):
            sl = slice(hh * 64, (hh + 1) * 64)
            o3 = bd[sl, idx, :, hh * 64:(hh + 1) * 64]
            a3 = a_.m(lambda ap: ap[sl].rearrange("p (c t) -> p c t", t=64))
            if b_ is None:
                K.copy(o3, a3, eng="pool")
            else:
                b3 = b_.m(lambda ap: ap[sl].rearrange("p (c t) -> p c t", t=64))
                K.tt(o3, a3, b3, ALU.mult, eng=("dve" if hh == 0 else "pool"))
    K.tt(t_x[:, :], t_G[:, :], lw, ALU.subtract)
    K.act(t_x[:, :], t_x[:, :], AF.Exp)
    to_bd(2, t_kap[:, :], t_x[:, :])
    K.act(t_e[:, :], t_G[:, :], AF.Exp, scale=-1.0)
    to_bd(0, t_kp[:, :], t_e[:, :])
    to_bd(1, t_b[:, :], t_e[:, :])
    g3 = t_G[:, :].m(lambda ap: ap.rearrange("p (c t) -> p c t", t=64))
    gl = g3.m(lambda ap: ap[:, :, 63:64])
    K.act(dec[:, :, :], gl, AF.Exp)
    x3 = t_x[:, :].m(lambda ap: ap.rearrange("p (c t) -> p c t", t=64))
    K.tt(x3, gl.m(lambda ap: ap.to_broadcast([128, NCH, 64])), g3, ALU.subtract)
    K.act(t_x[:, :], t_x[:, :], AF.Exp)
    to_bd(3, t_kp[:, :], t_x[:, :])
    to_bd(4, t_b[:, :], t_x[:, :])
    to_bd(5, v, None)
    yps = K.ps()
    yf = ft[10]
    if ti.sample:
        K.memset(yf[:, :], 0.0)
    toks, gss, Qs, Pns = R["tok"], R["gs"], R["Q"], R["Pn"]
    for c in chunks:
        pst = K.ps()
        for j, idx in enumerate((5, 3, 4)):
            if RD == F32:
                K.tr(pst[:, j * 128:(j + 1) * 128], bd[:, idx, c, :], identR[:, :])
            else:
                K.tr(pst.bf(slice(None), j * 128, (j + 1) * 128), bd[:, idx, c, :], identR[:, :])
        K.copy(toks[c][:, :], pst[:, 0:384] if RD == F32 else pst.bf(slice(None), 0, 384), eng="act")
    for c in chunks:
        cs = slice(c * 64, (c + 1) * 64)
        pg = K.ps()
        K.mm(pg[:, 0:128], bd[:, 0, c, :], bd[:, 2, c, :])
        K.mm(pg[:, 128:256], bd[:, 1, c, :], bd[:, 2, c, :])
        K.mm(pg[:, 256:384], bd[:, 2, c, :], bd[:, 1, c, :])
        K.mm(pg[:, 384:448], bd[:, 0, c, :], rt[:, cs])
        K.mm(pg[:, 448:512], bd[:, 1, c, :], rt[:, cs])
        K.tt(gss[c][:, :], pg[:, 0:512], C["maskG"][:, :], ALU.mult)
    for c in chunks:
        K.tt(Qs[c][:, :], identR[:, :], gss[c][:, 128:256], ALU.subtract, eng="pool")
    for lvl in range(1, 6):
        for c in chunks:
            Pk = gss[c][:, 128:384] if lvl == 1 else Pns[c][:, 0:256]
            M_, MT_ = Pk.m(lambda ap: ap[:, 0:128]), Pk.m(lambda ap: ap[:, 128:256])
            pp = K.ps()
            if lvl < 5:
                K.mm(pp[:, 0:128], MT_, M_)
            K.mm(pp[:, 128:256], M_, MT_)
            if lvl < 5:
                K.copy(Pns[c][:, 0:256], pp[:, 0:256], eng="act")
            else:
                K.copy(Pns[c][:, 128:256], pp[:, 128:256], eng="act")
        for c in chunks:
            pq = K.ps()
            K.mm(pq[:, 0:128], Pns[c][:, 128:256], Qs[c][:, :])
            K.tt(Qs[c][:, :], pq[:, 0:128], Qs[c][:, :], ALU.add)
    for c in chunks:
        cs = slice(c * 64, (c + 1) * 64)
        tok, gs, Q = toks[c], gss[c], Qs[c]
        Vbd, Ktok, Btok = tok[:, 0:128], tok[:, 128:256], tok[:, 256:384]
        px = K.ps()
        K.mm(px[:, 0:128], bd[:, 2, c, :], Hbf[:, hp, :], start=True, stop=False)
        K.mm(px[:, 0:128], gs[:, 0:128], Vbd, start=False, stop=True)
        xsb = R["xsb"]
        K.copy(xsb[:, :], px[:, 0:128], eng="act")
        pu = K.ps()
        K.mm(pu[:, 0:128], Q[:, :], xsb[:, :])
        nu = R["nu"]
        K.ts(nu[:, :], pu[:, 0:128], -1.0, ALU.mult)
        K.mm(yps[:, cs], Hbf[:, hp, :], rt[:, cs], start=True, stop=False)
        K.mm(yps[:, cs], Vbd, gs[:, 384:448], start=False, stop=False)
        K.mm(yps[:, cs], nu[:, :], gs[:, 448:512], start=False, stop=True)
        ph = K.ps()
        K.mm(ph[:, 0:128], Ktok, Vbd, start=True, stop=False)
        K.mm(ph[:, 0:128], Btok, nu[:, :], start=False, stop=True)
        K.stt(H[:, hp, :], H[:, hp, :], dec[:, c, :], ph[:, 0:128], ALU.mult, ALU.add)
        if Hbf is not H:
            K.copy(Hbf[:, hp, :], H[:, hp, :], eng="act")
        K.copy(yf[:, cs], yps[:, cs], eng="act")
    ybf, ysq = C["sq"][0], C["sq"][1]
    K.copy(ybf[:, :], yf[:, :], eng="pool")
    K.act(ysq[:, :], yf[:, :], AF.Square)
    pn = K.ps()
    K.mm(pn[:, 0:T], C["ones_blk"][:, :], ybf[:, :])
    K.mm(pn[:, T:2 * T], C["ones_blk"][:, :], ysq[:, :])
    mean, var = ft[2], ft[3]
    K.ts(mean[:, :], pn[:, 0:T], 1.0 / 64, ALU.mult)
    K.tt(var[:, :], mean[:, :], mean[:, :], ALU.mult)
    K.stt(var[:, :], pn[:, T:2 * T], 1.0 / 64, var[:, :], ALU.mult, ALU.subtract)
    K.act(var[:, :], var[:, :], AF.Ln, bias=C["epsc"][:, 1:2])
    K.act(var[:, :], var[:, :], AF.Exp, scale=-0.5)
    K.tt(yf[:, :], yf[:, :], mean[:, :], ALU.subtract)
    K.tt(yf[:, :], yf[:, :], var[:, :], ALU.mult)
    K.ts(yf[:, :], yf[:, :], pcol("lnx_g"), ALU.mult, pcol("lnx_b"), ALU.add)
    K.tt(yf[:, :], yf[:, :], bonus[:, :], ALU.add)
    cr = slice(min(chunks) * 64, (max(chunks) + 1) * 64)
    K.tt(yg[:, hp, cr], yf[:, cr], g.m(lambda ap: ap[:, cr]), ALU.mult)
```

```python
import math
import numpy as np
from contextlib import ExitStack
import concourse.bass as bass
import concourse.mybir as mybir
from concourse.bass_utils import run_bass_kernel_spmd

F32 = mybir.dt.float32
BF16 = mybir.dt.bfloat16
AF = mybir.ActivationFunctionType
ALU = mybir.AluOpType

D = 2048
KC = 16
T = 256
DFF = 8192
NCORES = 8


class Buf:
    __slots__ = ("name", "last_write", "reads", "sem")

    def __init__(self, name):
        self.name = name
        self.last_write = None
        self.reads = {}
        self.sem = None


class Sched:
    ENGS = ("pe", "act", "dve", "pool", "sp")

    def __init__(self, nc, stack):
        self.nc = nc
        self.stack = stack
        self.ops = {e: [] for e in self.ENGS}
        self.seq = {e: 0 for e in self.ENGS}
        self.seen = {e: {} for e in self.ENGS}
        self.sems = {}
        self.n_dma_sems = 0
        for e in self.ENGS:
            self.sems[("eng", e)] = stack.enter_context(nc.semaphore("s_" + e))
        self.dma_total = {}
        self.out_tokens = {}

    def _buf_sem(self, buf):
        if buf.sem is None:
            key = ("dma", self.n_dma_sems)
            self.n_dma_sems += 1
            self.sems[key] = self.stack.enter_context(self.nc.semaphore("d%d" % key[1]))
            buf.sem = key
        return buf.sem

    def _deps(self, eng, reads, writes):
        deps = {}
        mykey = ("eng", eng)
        for b in reads:
            t = b.last_write
            if t is not None and deps.get(t[0], 0) < t[1]:
                deps[t[0]] = t[1]
        for b in writes:
            t = b.last_write
            if t is not None and deps.get(t[0], 0) < t[1]:
                deps[t[0]] = t[1]
            for k, v in b.reads.items():
                if deps.get(k, 0) < v:
                    deps[k] = v
        out = []
        seen = self.seen[eng]
        for k, v in deps.items():
            if k == mykey and eng == "pe":
                continue
            if seen.get(k, 0) >= v:
                continue
            seen[k] = v
            out.append((k, v))
        return out

    def op(self, eng, fn, reads=(), writes=()):
        waits = self._deps(eng, reads, writes)
        self.seq[eng] += 1
        k = ("eng", eng)
        v = self.seq[eng]
        self.ops[eng].append((waits, fn, (k, 1)))
        for b in reads:
            if b.reads.get(k, 0) < v:
                b.reads[k] = v
        for b in writes:
            b.last_write = (k, v)
            b.reads = {}

    def dma(self, eng, fns, dst, reads=()):
        if dst is None:
            dsts = []
            waits = self._deps(eng, reads, [])
            key = self._buf_sem(reads[0])
        else:
            dsts = list(dst) if isinstance(dst, (list, tuple)) else [dst]
            waits = self._deps(eng, reads, dsts)
            key = self._buf_sem(dsts[0])
        for i, fn in enumerate(fns):
            self.ops[eng].append((waits if i == 0 else [], fn, (key, 16)))
        self.dma_total[key] = self.dma_total.get(key, 0) + 16 * len(fns)
        v = self.dma_total[key]
        for b in reads:
            if b.reads.get(key, 0) < v:
                b.reads[key] = v
        for d_ in dsts:
            d_.last_write = (key, v)
            d_.reads = {}
        if dst is None:
            self.out_tokens[key] = v

    def barrier(self):
        snap = {("eng", e): self.seq[e] for e in self.ENGS}
        snap.update(self.dma_total)
        for e in self.ENGS:
            waits = []
            for k, v in snap.items():
                if k == ("eng", e) or v <= self.seen[e].get(k, 0):
                    continue
                self.seen[e][k] = v
                waits.append((k, v))
            self.ops[e].append((waits, None, None))

    def final_wait(self, eng):
        waits = [(k, v) for k, v in self.out_tokens.items()]
        self.ops[eng].append((waits, None, None))

    def replay(self):
        nc = self.nc
        with nc.Block() as block:
            def mk(e):
                def body(engobj):
                    sems = self.sems
                    for waits, fn, inc in self.ops[e]:
                        for k, v in waits:
                            engobj.wait_ge(sems[k], v)
                        if fn is None:
                            continue
                        ins = fn(engobj)
                        if inc is not None:
                            ins.then_inc(sems[inc[0]], inc[1])
                return body
            block.tensor(mk("pe"))
            block.scalar(mk("act"))
            block.vector(mk("dve"))
            block.gpsimd(mk("pool"))
            block.sync(mk("sp"))


class V:
    __slots__ = ("ap", "bufs")

    def __init__(self, ap, bufs):
        self.ap = ap
        self.bufs = bufs

    def m(self, f):
        return V(f(self.ap), self.bufs)


class TT:
    def __init__(self, K, name, shape, dtype, kind="sb", nsub=1, dram_kind="Internal"):
        nc = K.nc
        if kind == "sb":
            h = K.st.enter_context(nc.sbuf_tensor(name, list(shape), dtype))
            self.base = h[:]
        elif kind == "ps":
            h = K.st.enter_context(nc.psum_tensor(name, list(shape), dtype))
            self.base = h[:]
        else:
            self.base = nc.dram_tensor(name, list(shape), dtype, kind=dram_kind).ap()
        self.name = name
        self.nsub = nsub
        self.bufs = [Buf("%s.%d" % (name, i)) for i in range(nsub)]

    def __getitem__(self, idx):
        ap = self.base[idx]
        if self.nsub == 1:
            return V(ap, self.bufs)
        i1 = idx[1] if isinstance(idx, tuple) and len(idx) > 1 else slice(None)
        if isinstance(i1, int):
            return V(ap, [self.bufs[i1]])
        lo, hi, _ = i1.indices(self.nsub)
        return V(ap, self.bufs[lo:hi])

    def all(self):
        return V(self.base, self.bufs)


class Arena:
    def __init__(self, K, name, nbytes):
        self.K = K
        self.n4 = nbytes // 4
        h = K.st.enter_context(K.nc.sbuf_tensor(name, [128, self.n4], F32))
        self.base = h[:]
        self.off = 0
        self.name = name

    def reset(self):
        self.off = 0

    def carve(self, name, shape, dtype, nsub=1):
        esz = 2 if dtype == BF16 else 4
        nel = 1
        for d_ in shape[1:]:
            nel *= d_
        n4 = (nel * esz + 31) // 32 * 8
        assert self.off + n4 <= self.n4, "arena %s overflow at %s (%d > %d)" % (self.name, name, (self.off + n4) * 4, self.n4 * 4)
        ap = self.base[0:shape[0], self.off:self.off + n4]
        self.off += n4
        if dtype == BF16:
            ap = ap.bitcast(BF16)
        ap = ap[:, 0:nel]
        if len(shape) == 3:
            ap = ap.rearrange("p (a b) -> p a b", a=shape[1])
        elif len(shape) == 4:
            ap = ap.rearrange("p (a b c) -> p a b c", a=shape[1], b=shape[2])
        t = TT.__new__(TT)
        t.base = ap
        t.name = name
        t.nsub = nsub
        t.bufs = [Buf("%s.%d" % (name, i)) for i in range(nsub)]
        return t


def _b(vs):
    out = []
    for v in vs:
        if isinstance(v, V):
            out.extend(v.bufs)
    return out


def _a(x):
    return x.ap if isinstance(x, V) else x


class PSB:
    def __init__(self, K, i):
        h = K.st.enter_context(K.nc.psum_tensor("psb%d" % i, [128, 512], F32))
        self.base = h[:]
        b_ = Buf("psb%d" % i)
        self.bufs = [b_, b_, b_, b_]

    def __getitem__(self, idx):
        ps_, cs = idx
        c0, c1, _ = cs.indices(512)
        return V(self.base[ps_, cs], self.bufs[c0 // 128:(c1 - 1) // 128 + 1])

    def bf(self, ps_, c0, c1):
        return V(self.base.bitcast(BF16)[ps_, c0:c1], self.bufs[c0 // 256:(c1 - 1) // 256 + 1])

    def v3(self, ps_, c0, c1, inner):
        return V(self.base[ps_, c0:c1].rearrange("p (a b) -> p a b", b=inner), self.bufs[c0 // 128:(c1 - 1) // 128 + 1])


class KB:
    def __init__(self, nc, st):
        self.nc = nc
        self.st = st
        self.S = Sched(nc, st)
        self.psb = [PSB(self, i) for i in range(8)]
        self.psi = 0
        self.wsl = [TT(self, "wsl%d" % i, [128, 16, 256], BF16) for i in range(3)]
        self.wsi = 0
        self.outs = []
        self.pending = []

    def ps(self):
        p = self.psb[self.psi % 8]
        self.psi += 1
        return p

    def wslot(self):
        p = self.wsl[self.wsi % 3]
        self.wsi += 1
        return p

    def act(self, out, in_, func, bias=0.0, scale=1.0, eng="act"):
        o, i, b, s = out.ap, in_.ap, _a(bias), _a(scale)
        self.S.op(eng, lambda e: e.activation(o, i, func, bias=b, scale=s), _b([in_, bias, scale]), _b([out]))

    def tt(self, out, a, b, op, eng="dve"):
        o, x, y = out.ap, a.ap, b.ap
        self.S.op(eng, lambda e: e.tensor_tensor(o, x, y, op), _b([a, b]), _b([out]))

    def ts(self, out, a, s1, op0, s2=None, op1=None, eng="dve"):
        o, x, p1, p2 = out.ap, a.ap, _a(s1), _a(s2)
        if op1 is None:
            self.S.op(eng, lambda e: e.tensor_scalar(o, x, p1, None, op0), _b([a, s1]), _b([out]))
        else:
            self.S.op(eng, lambda e: e.tensor_scalar(o, x, p1, p2, op0, op1), _b([a, s1, s2]), _b([out]))

    def stt(self, out, a, s, b, op0, op1, eng="dve"):
        o, x, p, y = out.ap, a.ap, _a(s), b.ap
        self.S.op(eng, lambda e: e.scalar_tensor_tensor(o, x, p, y, op0, op1), _b([a, s, b]), _b([out]))

    def copy(self, out, a, eng="dve"):
        o, x = out.ap, a.ap
        if eng == "act":
            self.S.op(eng, lambda e: e.copy(o, x), _b([a]), _b([out]))
        else:
            self.S.op(eng, lambda e: e.tensor_copy(o, x), _b([a]), _b([out]))

    def memset(self, out, val, eng="pool"):
        o = out.ap
        self.S.op(eng, lambda e: e.memset(o, val), [], _b([out]))

    def mm(self, out, lhsT, rhs, start=True, stop=True):
        o, l, r = out.ap, lhsT.ap, rhs.ap
        self.S.op("pe", lambda e: e.matmul(o, l, r, start=start, stop=stop), _b([lhsT, rhs]), _b([out]))

    def tr(self, out, in_, ident):
        o, i, d = out.ap, in_.ap, ident.ap
        self.S.op("pe", lambda e: e.transpose(o, i, d), _b([in_, ident]), _b([out]))

    def scan(self, out, d0, d1, init, op0, op1, eng="dve"):
        o, x, y = out.ap, d0.ap, d1.ap
        self.S.op(eng, lambda e: e.tensor_tensor_scan(o, x, y, init, op0, op1), _b([d0, d1]), _b([out]))

    def recip(self, out, a):
        o, x = out.ap, a.ap
        self.S.op("dve", lambda e: e.reciprocal(o, x), _b([a]), _b([out]))

    def dma(self, out, in_, eng="sp"):
        o, i = out.ap, in_.ap
        self.S.dma(eng, [lambda e: e.dma_start(out=o, in_=i)], out.bufs, _b([in_]))

    def dma_multi(self, outs_ins, dstbuf, reads, eng="sp"):
        fns = [(lambda o, i: (lambda e: e.dma_start(out=o, in_=i)))(o, i) for o, i in outs_ins]
        self.S.dma(eng, fns, dstbuf, reads)

    def store(self, out_ap, src_ap, src_bufs, eng="act"):
        self.S.dma(eng, [lambda e: e.dma_start(out=out_ap, in_=src_ap)], None, src_bufs)

    def pdma(self, out, in_ap):
        self.pending.append((out, in_ap))

    def flush_params(self):
        if not self.pending:
            return
        bufs = []
        for o, _ in self.pending:
            for b_ in o.bufs:
                if b_ not in bufs:
                    bufs.append(b_)
        fns = [(lambda o, i: (lambda e: e.dma_start(out=o, in_=i)))(o.ap, i) for o, i in self.pending]
        self.S.dma("sp", fns, bufs, [])
        self.pending = []


NB_HEADS = 16
KVH = 4
import os
DBG = os.environ.get("KDBG", "")
KSTOP = int(os.environ.get("KSTOP", "99"))
RD = F32 if os.environ.get("RWP", "f32") == "f32" else BF16


def rmsnorm_fm(K, C, xin, gcol, out, nch, width, eps=1e-6):
    ps = K.ps()
    for c in range(nch):
        sq = C["sq"][c % 2]
        K.act(sq[:, :], xin(c), AF.Square)
        K.mm(ps[:, 0:T], C["ones"][:, :], sq[:, :], start=(c == 0), stop=(c == nch - 1))
    rstd = C["rstd"]
    K.act(rstd[:, :], ps[:, 0:T], AF.Ln, bias=C["epsc"][:, 0:1] if eps == 1e-6 else C["epsc"][:, 1:2], scale=1.0 / width)
    K.act(rstd[:, :], rstd[:, :], AF.Exp, scale=-0.5)
    for c in range(nch):
        K.stt(out(c), xin(c), gcol(c), rstd[:, :], ALU.mult, ALU.mult)


def load_w(K, w16, r0, nkc, c0, nb, krows=128):
    slot = K.wslot()
    src = w16.base[r0:r0 + nkc * krows, c0:c0 + nb].rearrange("(kc p) c -> p kc c", p=krows)
    K.dma(slot[0:krows, 0:nkc, 0:nb], V(src, w16.bufs))
    return slot


def dense_fm(K, w16, r0, nkc, c0, ncols, rhs, evac, mrows=128, krows=128):
    for cb in range(0, ncols, 256):
        nb = min(256, ncols - cb)
        slot = load_w(K, w16, r0, nkc, c0 + cb, nb, krows)
        for m in range(0, nb, mrows):
            ps = K.ps()
            for kc in range(nkc):
                K.mm(ps[0:mrows, 0:T], slot[0:krows, kc, m:m + mrows], rhs(kc), start=(kc == 0), stop=(kc == nkc - 1))
            evac((cb + m) // mrows, ps)


def dense_tm(K, w16, r0, c0, ncols, hT, evac):
    for cb in range(0, ncols, 256):
        slot = load_w(K, w16, r0, KC, c0 + cb, 256)
        for b in range(T // 128):
            ps = K.ps()
            for kc in range(KC):
                K.mm(ps[:, 0:256], hT[:, kc, b * 128:(b + 1) * 128], slot[:, kc, 0:256], start=(kc == 0), stop=(kc == KC - 1))
            evac(b, cb // 256, ps)


def post_residual2(K, C, mo, gcol, xT):
    ps = K.ps()
    for c in range(KC):
        sq = C["sq"][c % 2]
        K.act(sq[:, :], mo[:, c, :], AF.Square)
        K.mm(ps[:, 0:T], C["ones"][:, :], sq[:, :], start=(c == 0), stop=(c == KC - 1))
    rstd = C["rstd"]
    K.act(rstd[:, :], ps[:, 0:T], AF.Ln, bias=C["epsc"][:, 0:1], scale=1.0 / D)
    K.act(rstd[:, :], rstd[:, :], AF.Exp, scale=-0.5)
    for c in range(KC):
        tmp = C["tmpf"][c % 2]
        K.stt(tmp[:, 0:T], mo[:, c, :], gcol(c), rstd[:, :], ALU.mult, ALU.mult)
        K.tt(xT[:, c, :], xT[:, c, :], tmp[:, 0:T], ALU.add, eng="pool")


def mlp_tile(K, C, W, layer, xT):
    uT = C["hT"]
    rmsnorm_fm(K, C, lambda c: xT[:, c, :], lambda c: C["g_ffn_pre"][:, layer, c:c + 1], lambda c: uT[:, c, :], KC, D)
    hid = C["hid"]

    def ev_up(m, ps):
        r = C["tmpa"][m % 2]
        K.act(r[:, :], ps[:, 0:T], AF.Relu)
        K.tt(hid[:, m, :], r[:, :], r[:, :], ALU.mult, eng=("dve" if m % 2 == 0 else "pool"))
    dense_fm(K, W["w_up"], layer * D, KC, 0, DFF, lambda kc: uT[:, kc, :], ev_up)
    mo = C["mo"]
    wd = W["w_down"]
    for cb in range(0, D, 256):
        pss = [K.ps(), K.ps()]
        for kg in range(4):
            slot = load_w(K, wd, layer * DFF + kg * 2048, 16, cb, 256)
            for m in range(2):
                for kc in range(16):
                    K.mm(pss[m][:, 0:T], slot[:, kc, m * 128:(m + 1) * 128], hid[:, kg * 16 + kc, :],
                         start=(kg == 0 and kc == 0), stop=(kg == 3 and kc == 15))
        for m in range(2):
            K.copy(mo[:, cb // 128 + m, :], pss[m][:, 0:T], eng=("act" if m == 0 else "dve"))
    post_residual2(K, C, mo, lambda c: C["g_ffn_post"][:, layer, c:c + 1], xT)


class TileInfo:
    def __init__(self, t, NTP):
        self.t = t
        self.sample = t >= NTP
        self.first = (t == 0)
        self.last_prompt = (t == NTP - 1)
        self.sq0 = (t - NTP) * 2


def even_tile(K, C, E, Wv, I, O, e, layer, ti, xT):
    hT = C["hT"]
    rmsnorm_fm(K, C, lambda c: xT[:, c, :], lambda c: C["g_mix_pre"][:, layer, c:c + 1], lambda c: hT[:, c, :], KC, D)
    if KSTOP <= 0:
        return
    win = Wv["w_in"]
    r0 = e * D
    rhs = lambda kc: hT[:, kc, :]
    qq, qt, kt, kh, dec, gate, oT, oa = E["qq"], E["qt"], E["kt"], E["kh"], E["dec"], E["gate"], E["oT"], E["oa"]
    tf = C["tmpf"]
    dense_fm(K, win, r0, KC, 0, 1024, rhs, lambda m, ps: K.act(qq[:, m, :], ps[:, 0:T], AF.Silu))
    if KSTOP <= 1:
        return

    def ev_f(m, ps):
        s_, f_, g_, x_ = tf[0], tf[1], tf[2], tf[3]
        K.act(s_[:, 0:T], ps[:, 0:T], AF.Sigmoid)
        K.ts(f_[:, 0:T], s_[:, 0:T], E["oml"][:, e, m:m + 1], ALU.mult, E["lb"][:, e, m:m + 1], ALU.add)
        K.act(f_[:, 0:T], f_[:, 0:T], AF.Ln)
        K.ts(s_[:, 0:T], s_[:, 0:T], E["noml"][:, e, m:m + 1], ALU.mult, E["oml"][:, e, m:m + 1], ALU.add)
        if ti.sample:
            K.tt(f_[:, 0:T], f_[:, 0:T], C["tokmask"][:, :], ALU.mult, eng="pool")
            K.tt(s_[:, 0:T], s_[:, 0:T], C["tokmask"][:, :], ALU.mult, eng="pool")
        K.scan(g_[:, 0:T], C["rm"][:, :], f_[:, 0:T], 0.0, ALU.mult, ALU.add)
        K.act(x_[:, 0:T], g_[:, 0:T], AF.Exp)
        K.stt(qt[:, m, :], qq[:, m, :], 128 ** -0.5, x_[:, 0:T], ALU.mult, ALU.mult)
        K.act(x_[:, 0:T], g_[:, 0:T], AF.Exp, scale=-1.0)
        K.tt(kt[:, m, :], s_[:, 0:T], x_[:, 0:T], ALU.mult)
        g3 = g_[:, 0:T].m(lambda ap: ap.rearrange("p (c t) -> p c t", t=64))
        gl = g3.m(lambda ap: ap[:, :, 63:64])
        K.act(dec[:, m, :, :], gl, AF.Exp)
        x3 = x_[:, 0:T].m(lambda ap: ap.rearrange("p (c t) -> p c t", t=64))
        K.tt(x3, gl.m(lambda ap: ap.to_broadcast([128, T // 64, 64])), g3, ALU.subtract)
        K.act(x_[:, 0:T], x_[:, 0:T], AF.Exp)
        K.tt(kh[:, m, :], s_[:, 0:T], x_[:, 0:T], ALU.mult)
    dense_fm(K, win, r0, KC, 1024, 1024, rhs, ev_f)
    if KSTOP <= 2:
        return
    vtok = E["vtok"]
    dense_tm(K, win, r0, 2048, 1024, hT, lambda b, cb, ps: K.copy(vtok[:, b, cb * 256:(cb + 1) * 256], ps[:, 0:256], eng=("act" if cb % 2 == 0 else "dve")))
    if KSTOP <= 3:
        return
    dense_fm(K, win, r0, KC, 3072, 1024, rhs, lambda m, ps: K.act(gate[:, m, :], ps[:, 0:T], AF.Silu))
    if KSTOP <= 4:
        return
    S, Sbf = E["S"], E["Sbf"]
    if ti.first:
        K.memset(S.all(), 0.0)
        K.memset(Sbf.all(), 0.0)
    for b in range(T // 128) if "H" not in DBG else ():
        c0 = b * 128
        if ti.sample:
            sq = ti.sq0 + b
            K.dma(S.all(), V(I["state_hgrn"][e, sq].rearrange("h k v -> k h v"), []))
            K.copy(Sbf.all(), S.all(), eng="pool")
        for h0 in range(0, 8, 2):
            hs = (h0, h0 + 1)
            pss, pos, ats, khTs = {}, {}, {}, {}
            for h in hs:
                ps = pss[h] = K.ps()
                K.mm(ps[:, 0:128], kt[:, h, c0:c0 + 128], qt[:, h, c0:c0 + 128])
            for h in hs:
                ps = pss[h]
                at = ats[h] = E["attnT"][h % 2]
                K.tt(at[:, :], ps[:, 0:128], C["maskBD"][:, :], ALU.mult)
                K.tr(ps.bf(slice(None), 512, 640), kh[:, h, c0:c0 + 128], C["identb"][:, :])
            for h in hs:
                khT = khTs[h] = E["khT"][h % 2]
                K.copy(khT[:, :], pss[h].bf(slice(None), 512, 640), eng="act")
            for h in hs:
                po = pos[h] = K.ps()
                K.mm(po[:, 0:128], vtok[:, b, h * 128:(h + 1) * 128], ats[h][:, :], start=True, stop=False)
                K.mm(po[:, 0:64], Sbf[:, h, :], qt[:, h, c0:c0 + 64], start=False, stop=ti.sample)
            pds = {}
            for h in hs:
                pd = pds[h] = K.ps()
                K.mm(pd[:, 0:128], khTs[h][0:64, :], vtok[0:64, b, h * 128:(h + 1) * 128])
            for h in hs:
                K.stt(S[:, h, :], S[:, h, :], dec[:, h, 2 * b, :], pds[h][:, 0:128], ALU.mult, ALU.add)
            if not ti.sample:
                for h in hs:
                    K.copy(Sbf[:, h, :], S[:, h, :], eng="act")
                for h in hs:
                    K.mm(pos[h][:, 64:128], Sbf[:, h, :], qt[:, h, c0 + 64:c0 + 128], start=False, stop=True)
                    pd = pds[h] = K.ps()
                    K.mm(pd[:, 0:128], khTs[h][64:128, :], vtok[64:128, b, h * 128:(h + 1) * 128])
                for h in hs:
                    K.stt(S[:, h, :], S[:, h, :], dec[:, h, 2 * b + 1, :], pds[h][:, 0:128], ALU.mult, ALU.add)
                for h in hs:
                    K.copy(Sbf[:, h, :], S[:, h, :], eng="act")
            for h in hs:
                K.copy(oT[:, h, c0:c0 + 128], pos[h][:, 0:128], eng="act")
        if ti.sample:
            K.store(O["hgrn_s"][e, ti.sq0 + b].rearrange("h k v -> k h v"), S.base, S.bufs)
    if ti.last_prompt:
        K.store(O["hgrn_p"][e].rearrange("h k v -> k h v"), S.base, S.bufs)
    if KSTOP <= 5:
        return
    rmsnorm_fm(K, C, lambda c: oT[:, c, :], lambda c: E["gn"][:, e, c:c + 1], lambda c: oT[:, c, :], 8, 1024)
    for h in range(8):
        K.tt(oa[:, h, :], oT[:, h, :], gate[:, h, :], ALU.mult, eng=("dve" if h % 2 == 0 else "pool"))

    if KSTOP <= 6:
        return
    qbT, kbT, vtb, ktokf, vtokf, obT = E["qbT"], E["kbT"], E["vtb"], E["ktokf"], E["vtokf"], E["obT"]
    dense_fm(K, win, r0, KC, 4096, 1024, rhs, lambda m, ps: K.ts(qbT[0:64, m, :], ps[0:64, 0:T], 0.125, ALU.mult), mrows=64)
    if os.environ.get("KSUB") == "1":
        return
    dense_fm(K, win, r0, KC, 5120, 256, rhs, lambda m, ps: K.copy(kbT[0:64, m, 128:128 + T], ps[0:64, 0:T], eng="act"), mrows=64)

    KV = os.environ.get("KV", "")

    def ev_kv(b, cb, ps):
        if cb == 0:
            if "a" not in KV:
                K.copy(ktokf[:, b, :], ps[:, 0:256], eng="act")
        else:
            if "b" not in KV:
                K.copy(vtokf[:, b, :], ps[:, 0:256], eng="act")
            if "c" not in KV:
                K.copy(vtb[:, 1 + b, :], vtokf[:, b, :], eng="dve")
    if os.environ.get("KSUB") == "2":
        return
    dense_tm(K, win, r0, 5120, 512, hT, ev_kv)
    if KSTOP <= 7:
        return
    for b in range(T // 128) if "W" not in DBG else ():
        c0 = b * 128
        if ti.sample:
            sq = ti.sq0 + b
            kc32, vc32 = E["kc32"][b % 2], E["vc32"][b % 2]
            K.dma(kc32[:, :], V(I["cache_k"][e, sq].rearrange("s h d -> s (h d)"), []))
            K.dma(vc32[:, :], V(I["cache_v"][e, sq].rearrange("s h d -> s (h d)"), []))
            pst = K.ps()
            for kv in range(KVH):
                K.tr(pst[0:64, kv * 128:(kv + 1) * 128], kc32[:, kv * 64:(kv + 1) * 64], C["identf"][:, :])
            kcT = E["kcT"][b % 2]
            K.copy(kcT[0:64, :, :], pst.v3(slice(0, 64), 0, 512, 128), eng="act")
            vc = E["vc"][b % 2]
            K.copy(vc[:, :], vc32[:, :], eng="pool")
            kprev = lambda kv: kcT[0:64, kv, :]
            vprev = lambda kv: vc[:, kv * 64:(kv + 1) * 64]
            has_prev = True
            for nm, src32, inp in (("k_s", ktokf, "cache_k"), ("v_s", vtokf, "cache_v")):
                K.store(O[nm][e, sq, 120:128].rearrange("s h d -> s (h d)"), src32.base[0:8, b, :], [src32.bufs[b]])
                K.store(O[nm][e, sq, 0:120].rearrange("s h d -> s (h d)"), I[inp][e, sq, 8:128].rearrange("s h d -> s (h d)"), [src32.bufs[b]])
        else:
            kprev = (lambda b_: (lambda kv: kbT[0:64, kv, b_ * 128:(b_ + 1) * 128]))(b)
            vprev = (lambda b_: (lambda kv: vtb[:, b_, kv * 64:(kv + 1) * 64]))(b)
            has_prev = not (ti.first and b == 0)
        for kv in range(KVH):
            q3 = qbT[0:64, kv * 4:(kv + 1) * 4, c0:c0 + 128]
            pP, pC = E["pP"][kv % 2], E["pC"][kv % 2]
            v3f = lambda vv: vv.m(lambda ap: ap.rearrange("p (g q) -> p g q", g=4))
            if has_prev:
                psP = K.ps()
                K.mm(psP.v3(slice(None), 0, 512, 128), kprev(kv), q3)
                K.act(tf[0][:, :], psP[:, 0:512], AF.Exp)
                K.tt(v3f(pP[:, :]), v3f(tf[0][:, :]), E["EBp"][:, kv * 4:(kv + 1) * 4, :], ALU.mult)
            psC = K.ps()
            K.mm(psC.v3(slice(None), 0, 512, 128), kbT[0:64, kv, 128 + c0:128 + c0 + 128], q3)
            K.act(tf[1][:, :], psC[:, 0:512], AF.Exp)
            K.tt(v3f(pC[:, :]), v3f(tf[1][:, :]), E["EBc"][:, kv * 4:(kv + 1) * 4, :], ALU.mult, eng="pool")
            psN, psD = K.ps(), K.ps()
            if has_prev:
                K.mm(psN[0:64, 0:512], vprev(kv), pP[:, :], start=True, stop=False)
                K.mm(psD[0:64, 0:512], C["ones"][:, 0:64], pP[:, :], start=True, stop=False)
            K.mm(psN[0:64, 0:512], vtb[:, 1 + b, kv * 64:(kv + 1) * 64], pC[:, :], start=not has_prev, stop=True)
            K.mm(psD[0:64, 0:512], C["ones"][:, 0:64], pC[:, :], start=not has_prev, stop=True)
            den = E["den"]
            K.tt(v3f(den[0:64, :]), psD.v3(slice(0, 64), 0, 512, 128),
                 E["sinkexp"][0:64, e, kv * 4:(kv + 1) * 4, :].m(lambda ap: ap.to_broadcast([64, 4, 128])), ALU.add)
            K.recip(den[0:64, :], den[0:64, :])
            K.tt(obT[0:64, kv * 4:(kv + 1) * 4, c0:c0 + 128], psN.v3(slice(0, 64), 0, 512, 128), v3f(den[0:64, :]), ALU.mult)
    if not ti.sample:
        K.copy(kbT[0:64, :, 0:128], kbT[0:64, :, T:T + 128], eng="pool")
        K.copy(vtb[:, 0, :], vtb[:, 2, :], eng="pool")
        if ti.last_prompt:
            for nm, src32 in (("k_p", ktokf), ("v_p", vtokf)):
                K.store(O[nm][e].rearrange("s h d -> s (h d)"), src32.base[:, 1, :], [src32.bufs[1]])
    if KSTOP <= 8:
        return
    wout = Wv["w_out"]
    mo = C["mo"]
    for cb in range(0, D, 256):
        slotA = load_w(K, wout, e * D, 8, cb, 256)
        slotB = load_w(K, wout, e * D + 1024, 16, cb, 256, krows=64)
        for m in range(2):
            ps = K.ps()
            for kc in range(8):
                K.mm(ps[:, 0:T], slotA[:, kc, m * 128:(m + 1) * 128], oa[:, kc, :], start=(kc == 0), stop=False)
            for hh in range(16):
                K.mm(ps[:, 0:T], slotB[0:64, hh, m * 128:(m + 1) * 128], obT[0:64, hh, :], start=False, stop=(hh == 15))
            K.copy(mo[:, cb // 128 + m, :], ps[:, 0:T], eng=("act" if m == 0 else "dve"))
    post_residual2(K, C, mo, lambda c: C["g_mix_post"][:, layer, c:c + 1], xT)


def build_program(SEQ, DEPTH):
    NTP = SEQ // T
    NT = NTP + 2
    NE = (DEPTH + 1) // 2
    NO = DEPTH // 2
    nc = bass.Bass("TRN2", target_bir_lowering=False)
    dth = lambda name, shape, dtype=F32, kind="ExternalInput": nc.dram_tensor({"ExternalInput": "i_", "ExternalOutput": "o_", "Internal": "s_"}[kind] + name, list(shape), dtype, kind=kind)
    dt = lambda *a, **k: dth(*a, **k).ap()
    I = {}
    I["xp"] = dt("xp", [SEQ, D])
    I["xsm"] = dt("xsm", [512, D])
    for n in ["norm_mix_pre", "norm_mix_post", "norm_ffn_pre", "norm_ffn_post"]:
        I[n] = dt(n, [128, DEPTH, KC])
    I["w_up"] = dt("w_up", [DEPTH * D, DFF])
    I["w_down"] = dt("w_down", [DEPTH * DFF, D])
    I["w_in"] = dt("w_in", [NE * D, 5632])
    I["w_out"] = dt("w_out", [NE * D, D])
    I["state_hgrn"] = dt("state_hgrn", [NE, 4, 8, 128, 128])
    I["cache_k"] = dt("cache_k", [NE, 4, 128, 4, 64])
    I["cache_v"] = dt("cache_v", [NE, 4, 128, 4, 64])
    I["lb_raw"] = dt("lb_raw", [128, NE, 8])
    I["gn"] = dt("gn", [128, NE, 8])
    I["rel_bias"] = dt("rel_bias", [32, 16])
    I["sinks"] = dt("sinks", [64, NE, 16])
    if NO > 0:
        for n in ("wr", "wk", "wv", "wo"):
            I[n] = dt(n, [NO * D, D])
        I["w1"] = dt("w1", [NO * D, 96]); I["w2"] = dt("w2", [NO * 96, D])
        I["a1"] = dt("a1", [NO * D, 96]); I["a2"] = dt("a2", [NO * 96, D])
        I["g1"] = dt("g1", [NO * D, 256]); I["g2"] = dt("g2", [NO * 256, D])
        if NO > 1:
            I["v1"] = dt("v1", [(NO - 1) * D, 64]); I["v2"] = dt("v2", [(NO - 1) * 64, D])
            I["p_v0"] = dt("p_v0", [128, NO - 1, 16])
        I["p_mu"] = dt("p_mu", [128, NO, 6, 16])
        for n in ("w0", "a0", "kk", "ka", "rk", "lnx_g", "lnx_b"):
            I["p_" + n] = dt("p_" + n, [128, NO, 16])
        I["state_rwkv"] = dt("state_rwkv", [NO, 4, 32, 64, 64])
        I["state_shift"] = dt("state_shift", [128, NO, 4, 16])
        I["maskG"] = dt("maskG", [128, 512])
        I["onesblk"] = dt("onesblk", [128, 128])
    for n, shp in (("ident", [128, 128]), ("antiid", [128, 128]), ("oh384", [32, 384]), ("validm", [16, 384]), ("maskBD", [128, 128]),
                   ("rm", [128, T]), ("tokmask", [128, T])):
        I[n] = dt(n, shp)
    O = {"_buf": Buf("outputs")}
    O["yp"] = dt("yp", [SEQ, D], kind="ExternalOutput")
    O["ysm"] = dt("ysm", [4, 8, D], kind="ExternalOutput")
    O["hgrn_p"] = dt("hgrn_p", [NE, 8, 128, 128], kind="ExternalOutput")
    O["hgrn_s"] = dt("hgrn_s", [NE, 4, 8, 128, 128], kind="ExternalOutput")
    for nm in ("k_p", "v_p"):
        O[nm] = dt(nm, [NE, 128, 4, 64], kind="ExternalOutput")
    for nm in ("k_s", "v_s"):
        O[nm] = dt(nm, [NE, 4, 128, 4, 64], kind="ExternalOutput")
    if NO > 0:
        O["rwkv_p"] = dt("rwkv_p", [NO, 32, 64, 64], kind="ExternalOutput")
        O["rwkv_s"] = dt("rwkv_s", [NO, 4, 32, 64, 64], kind="ExternalOutput")
        O["shift_p"] = dt("shift_p", [NO, D], kind="ExternalOutput")
        O["shift_s"] = dt("shift_s", [NO, 4, D], kind="ExternalOutput")

    with ExitStack() as st:
        K = KB(nc, st)
        S = K.S
        W = {}
        Wl = {}
        wspecs = [("w_up", D, DFF, DEPTH), ("w_down", DFF, D, DEPTH), ("w_in", D, 5632, NE), ("w_out", D, D, NE)]
        if NO > 0:
            wspecs += [("wr", D, D, NO), ("wk", D, D, NO), ("wv", D, D, NO), ("wo", D, D, NO), ("w1", D, 96, NO), ("w2", 96, D, NO),
                       ("a1", D, 96, NO), ("a2", 96, D, NO), ("g1", D, 256, NO), ("g2", 256, D, NO)]
        if NO > 1:
            wspecs += [("v1", D, 64, NO - 1), ("v2", 64, D, NO - 1)]
        WS = {n_: (r_, c_) for n_, r_, c_, _ in wspecs}
        Lb = {}
        for l in range(DEPTH):
            Lb[(l, "mix")] = Buf("wmix%d" % l)
            Lb[(l, "mlp")] = Buf("wmlp%d" % l)
        for name, rows, cols, nl in wspecs:
            W[name] = TT(K, name + "16", [nl * rows, cols], BF16, kind="dram")
            for l in range(nl):
                if name in ("w_up", "w_down"):
                    Wl[(name, l)] = Lb[(l, "mlp")]
                elif name in ("w_in", "w_out"):
                    Wl[(name, l)] = Lb[(2 * l, "mix")]
                elif name in ("v1", "v2"):
                    Wl[(name, l)] = Lb[(2 * (l + 1) + 1, "mix")]
                else:
                    Wl[(name, l)] = Lb[(2 * l + 1, "mix")]
        xs = TT(K, "xs", [128, NT, KC, T], F32, kind="dram", nsub=NT)
        xsb3 = [Buf("xs%d" % i) for i in range(3)]
        xs.bufs = [xsb3[t % 3] for t in range(NT)]
        xs_vf = TT(K, "xs_vf", [128, NT, KC, T], F32, kind="dram", nsub=NT) if NO > 1 else None
        if xs_vf is not None:
            vfb3 = [Buf("xsvf%d" % i) for i in range(3)]
            xs_vf.bufs = [vfb3[t % 3] for t in range(NT)]
        cast_pairs = {}
        Edh = dth("Ed", [16, 384], F32, kind="Internal")
        Edbuf = Buf("Ed")

        def wview(name, l):
            tv = TT.__new__(TT)
            tv.base = W[name].base
            tv.name = name
            tv.nsub = 1
            tv.bufs = [Wl[(name, l)]]
            return tv

        def cast_weight(name, l, rows_per_layer, cols):
            step = max(1, (1 << 20) // cols)
            pairs = []
            rend = (l + 1) * rows_per_layer
            for r in range(l * rows_per_layer, rend, step):
                r1 = min(r + step, rend)
                pairs.append((W[name].base[r:r1, :], I[name][r:r1, :]))
            cast_pairs.setdefault(Wl[(name, l)].name, (Wl[(name, l)], []))[1].extend(pairs)

        C = {}
        C["ones"] = TT(K, "ones", [128, 128], BF16)
        C["identf"] = TT(K, "identf", [128, 128], F32)
        C["identb"] = TT(K, "identb", [128, 128], BF16)
        C["maskBD"] = TT(K, "maskBD", [128, 128], F32)
        C["rm"] = TT(K, "rm", [128, T], F32)
        C["tokmask"] = TT(K, "tokmask", [128, T], F32)
        C["sq"] = [TT(K, "sq%d" % i, [128, T], BF16) for i in range(2)]
        C["rstd"] = TT(K, "rstd", [128, T], F32)
        C["tmpa"] = [TT(K, "tmpa%d" % i, [128, T], BF16) for i in range(2)]
        C["tmpf"] = [TT(K, "tmpf%d" % i, [128, 512 if i < 2 else T], F32) for i in range(4)]
        C["hT"] = TT(K, "hT", [128, KC, T], BF16, nsub=KC)
        C["mo"] = TT(K, "mo", [128, KC, T], F32, nsub=KC)
        C["epsc"] = TT(K, "epsc", [128, 3], F32)
        xT = TT(K, "xT", [128, KC, T], F32, nsub=KC)
        for n, short in (("norm_mix_pre", "g_mix_pre"), ("norm_mix_post", "g_mix_post"), ("norm_ffn_pre", "g_ffn_pre"), ("norm_ffn_post", "g_ffn_post")):
            C[short] = TT(K, short, [128, DEPTH, KC], F32)
            K.pdma(C[short][:, :, :], I[n])
        K.memset(C["ones"][:, :], 1.0)
        K.memset(C["epsc"][:, 0:1], 1e-6)
        K.memset(C["epsc"][:, 1:2], 64e-5)
        K.memset(C["epsc"][:, 2:3], 1e-24)
        R = {}
        if NO > 0:
            C["maskG"] = TT(K, "maskG_sb", [128, 512], F32)
            K.pdma(C["maskG"][:, :], I["maskG"])
            ob32 = TT(K, "ob32", [128, 128], F32)
            K.pdma(ob32[:, :], I["onesblk"])
            C["ones_blk"] = TT(K, "ones_blk", [128, 128], BF16)
            P = {}
            for n in ("w0", "a0", "kk", "ka", "rk", "lnx_g", "lnx_b"):
                P[n] = TT(K, "P_" + n, [128, NO, 16], F32)
                K.pdma(P[n][:, :, :], I["p_" + n])
            P["omka"] = TT(K, "P_omka", [128, NO, 16], F32)
            if NO > 1:
                P["v0"] = TT(K, "P_v0", [128, NO - 1, 16], F32)
                K.pdma(P["v0"][:, :, :], I["p_v0"])
            R["mu"] = TT(K, "P_mu", [128, NO, 6, 16], F32)
            K.pdma(R["mu"][:, :, :, :], I["p_mu"])
            R["P"] = P
        for n in ("maskBD", "rm", "tokmask"):
            K.pdma(C[n][:, :], I[n])
        K.pdma(C["identf"][:, :], I["ident"])
        K.flush_params()
        K.copy(C["identb"][:, :], C["identf"][:, :])
        if NO > 0:
            K.copy(C["ones_blk"][:, :], ob32[:, :])
            K.ts(R["P"]["omka"][:, :, :], R["P"]["ka"][:, :, :], -1.0, ALU.mult, 1.0, ALU.add)
        arH = Arena(K, "arH", 32 * 1024)
        arM = Arena(K, "arM", 85 * 1024)
        C["hid"] = arH.carve("hid", [128, 64, T], BF16, nsub=64)

        for l in range(DEPTH):
            if l % 2 == 0:
                cast_weight("w_in", l // 2, D, 5632)
                cast_weight("w_out", l // 2, D, D)
            else:
                oo = l // 2
                for n_ in ("w1", "a1", "g1", "w2", "a2", "g2", "wr", "wk", "wv", "wo"):
                    cast_weight(n_, oo, WS[n_][0], WS[n_][1])
                if oo >= 1:
                    cast_weight("v1", oo - 1, D, 64)
                    cast_weight("v2", oo - 1, 64, D)
            cast_weight("w_up", l, D, DFF)
            cast_weight("w_down", l, DFF, D)
            for kind_ in ("mix", "mlp"):
                bb, prs = cast_pairs[Lb[(l, kind_)].name]
                K.dma_multi(prs, bb, [], eng="pool")

        E = {}
        if NE > 0:
            E["lb"] = TT(K, "lb", [128, NE, 8], F32)
            E["oml"] = TT(K, "oml", [128, NE, 8], F32)
            E["noml"] = TT(K, "noml", [128, NE, 8], F32)
            E["gn"] = TT(K, "gn", [128, NE, 8], F32)
            lbr = TT(K, "lbr", [128, NE, 8], F32)
            E["sinkexp"] = TT(K, "sinkexp", [64, NE, 16, 1], F32)
            E["EBp"] = TT(K, "EBp", [128, 16, 128], BF16)
            E["EBc"] = TT(K, "EBc", [128, 16, 128], BF16)
            arH.reset()
            brev = arH.carve("brev", [128, 16, 128], F32)
            relb = arH.carve("relb", [32, 16], F32)
            oh = arH.carve("oh", [32, 384], F32)
            vm = arH.carve("vm", [16, 384], F32)
            Et = arH.carve("Et", [16, 384], F32)
            antiid = arH.carve("antiid", [128, 128], F32)
            K.pdma(lbr[:, :, :], I["lb_raw"])
            K.pdma(E["gn"][:, :, :], I["gn"])
            K.pdma(E["sinkexp"][:, :, :, :], I["sinks"].rearrange("p e (h o) -> p e h o", o=1))
            K.pdma(relb[:, :], I["rel_bias"])
            K.pdma(oh[:, :], I["oh384"])
            K.pdma(vm[:, :], I["validm"])
            K.pdma(antiid[:, :], I["antiid"])
            K.flush_params()
            K.memset(E["lb"][:, :, :], 0.0)
            if NE > 1:
                K.tt(lbr[:, 1, :], lbr[:, 1, :], lbr[:, 0, :], ALU.subtract)
                K.act(E["lb"][:, 1, :], lbr[:, 1, :], AF.Sigmoid)
            K.ts(E["oml"][:, :, :], E["lb"][:, :, :], -1.0, ALU.mult, 1.0, ALU.add)
            K.ts(E["noml"][:, :, :], E["oml"][:, :, :], -1.0, ALU.mult)
            K.act(E["sinkexp"][:, :, :, :], E["sinkexp"][:, :, :, :], AF.Exp)
            ps = K.ps()
            if "E" in DBG:
                K.memset(E["EBp"].all(), 1.0)
                K.memset(E["EBc"].all(), 1.0)
            K.mm(ps[0:16, 0:384], relb[:, :], oh[:, :])
            K.act(Et[:, :], ps[0:16, 0:384], AF.Exp)
            K.tt(Et[:, :], Et[:, :], vm[:, :], ALU.mult)
            K.S.dma("sp", [lambda en: en.dma_start(out=Edh.ap(), in_=Et.base)], Edbuf, Et.bufs)
            for off, name in ((1, "EBc"), (129, "EBp")) if "E" not in DBG else ():
                src = bass.AP(Edh, off, [[1, 128], [384, 16], [1, 128]])
                K.S.dma("sp", [(lambda s_: (lambda en: en.dma_start(out=brev.base, in_=s_)))(src)], brev.bufs, [Edbuf])
                for g in range(4):
                    ps = K.ps()
                    K.mm(ps.v3(slice(None), 0, 512, 128), antiid[:, :], brev[:, g * 4:(g + 1) * 4, :])
                    K.copy(E[name][:, g * 4:(g + 1) * 4, :], ps.v3(slice(None), 0, 512, 128), eng=("act" if g % 2 == 0 else "dve"))

        S.barrier()
        arH.reset()
        xin = [arH.carve("xin%d" % i, [128, 2, D], F32) for i in range(2)]
        for t in range(NT):
            xi = xin[t % 2]
            if t < NTP:
                src = I["xp"][t * T:(t + 1) * T, :].rearrange("(b p) d -> p b d", p=128)
            else:
                src = I["xsm"][(t - NTP) * T:(t - NTP + 1) * T, :].rearrange("(b p) d -> p b d", p=128)
            K.dma(xi[:, :, :], V(src, []))
            for c in range(KC):
                ps = K.ps()
                for b in range(2):
                    K.tr(ps[:, b * 128:(b + 1) * 128], xi[:, b, c * 128:(c + 1) * 128], C["identf"][:, :])
                K.copy(xT[:, c, :], ps[:, 0:T], eng=("act" if c % 2 == 0 else "dve"))
            K.dma(xs[:, t, :, :], xT.all(), eng="act")
        S.barrier()
        arH.reset()
        C["hid"] = arH.carve("hid", [128, 64, T], BF16, nsub=64)

        for layer in range(DEPTH):
            Wv = {"w_up": wview("w_up", layer), "w_down": wview("w_down", layer)}
            S.barrier()
            arM.reset()
            if layer % 2 == 0:
                e = layer // 2
                Wv["w_in"] = wview("w_in", e)
                Wv["w_out"] = wview("w_out", e)
                for nm in ("qq", "qt", "kt", "kh", "gate", "oa"):
                    E[nm] = arM.carve(nm, [128, 8, T], BF16, nsub=8)
                E["oT"] = arM.carve("oT", [128, 8, T], F32, nsub=8)
                E["dec"] = arM.carve("dec", [128, 8, T // 64, 1], F32, nsub=8)
                E["vtok"] = arM.carve("vtok", [128, 2, 1024], BF16, nsub=2)
                E["S"] = arM.carve("S", [128, 8, 128], F32, nsub=8)
                E["Sbf"] = arM.carve("Sbf", [128, 8, 128], BF16, nsub=8)
                E["attnT"] = [arM.carve("attnT%d" % i, [128, 128], BF16) for i in range(2)]
                E["khT"] = [arM.carve("khT%d" % i, [128, 128], BF16) for i in range(2)]
                E["qbT"] = arM.carve("qbT", [64, 16, T], BF16, nsub=16)
                E["obT"] = arM.carve("obT", [64, 16, T], BF16, nsub=16)
                E["kbT"] = arM.carve("kbT", [64, 4, 128 + T], BF16, nsub=4)
                E["vtb"] = arM.carve("vtb", [128, 3, 256], BF16, nsub=3)
                E["ktokf"] = arM.carve("ktokf", [128, 2, 256], F32, nsub=2)
                E["vtokf"] = arM.carve("vtokf", [128, 2, 256], F32, nsub=2)
                E["kc32"] = [arM.carve("kc32%d" % i, [128, 256], F32) for i in range(2)]
                E["vc32"] = [arM.carve("vc32%d" % i, [128, 256], F32) for i in range(2)]
                E["kcT"] = [arM.carve("kcT%d" % i, [64, 4, 128], BF16) for i in range(2)]
                E["vc"] = [arM.carve("vc%d" % i, [128, 256], BF16) for i in range(2)]
                E["pP"] = [arM.carve("pP%d" % i, [128, 512], BF16) for i in range(2)]
                E["pC"] = [arM.carve("pC%d" % i, [128, 512], BF16) for i in range(2)]
                E["den"] = arM.carve("den", [64, 512], F32)
            else:
                oo = layer // 2
                for n_ in ("w1", "a1", "g1", "w2", "a2", "g2", "wr", "wk", "wv", "wo"):
                    Wv[n_] = wview(n_, oo)
                if oo >= 1:
                    Wv["v1"] = wview("v1", oo - 1)
                    Wv["v2"] = wview("v2", oo - 1)
                R["xx"] = arM.carve("xx", [128, KC, T], BF16, nsub=KC)
                R["l1"] = arM.carve("l1", [128, 5, T], BF16, nsub=5)
                R["carry"] = arM.carve("carry", [128, KC], F32)
                R["shs"] = arM.carve("shs", [128, 2, KC], F32, nsub=2)
                R["shout"] = arM.carve("shout", [128, 2, KC], F32)
                for nm in ("rg", "kg", "vg", "lwg", "ag", "vfg"):
                    R[nm] = arM.carve(nm, [128, 2, T], F32, nsub=2)
                R["gg"] = arM.carve("gg", [128, 2, T], BF16, nsub=2)
                R["ft"] = [arM.carve("ft%d" % i, [128, T], F32) for i in range(11)]
                R["sho"] = R["ft"][10]
                nbd = 1 if RD == F32 else 2
                R["bd"] = [arM.carve("bd%d" % i, [128, 6, T // 64, 128], RD) for i in range(nbd)]
                R["rt"] = [arM.carve("rt%d" % i, [128, T], RD) for i in range(nbd)]
                R["decr"] = [arM.carve("decr%d" % i, [128, T // 64, 1], F32) for i in range(2)]
                R["tok"] = [arM.carve("tok%d" % i, [128, 384], RD) for i in range(4)]
                R["gs"] = [arM.carve("gs%d" % i, [128, 512], RD) for i in range(4)]
                R["Q"] = [arM.carve("Q%d" % i, [128, 128], RD) for i in range(4)]
                R["Pn"] = [arM.carve("Pn%d" % i, [128, 256], RD) for i in range(4)]
                R["xsb"] = arM.carve("xsb", [128, 128], RD)
                R["nu"] = arM.carve("nu", [128, 128], RD)
                R["H"] = arM.carve("H", [128, 16, 128], F32, nsub=16)
                R["Hbf"] = arM.carve("Hbf", [128, 16, 128], BF16, nsub=16) if RD == BF16 else R["H"]
                R["stg"] = arM.carve("stg", [128, 16, 128], F32)
                for i in range(len(R["bd"])):
                    K.memset(R["bd"][i].all(), 0.0)
                K.memset(R["stg"].all(), 0.0)
            for t in range(NT):
                ti = TileInfo(t, NTP)
                K.dma(xT.all(), xs[:, t, :, :])
                if layer % 2 == 0:
                    even_tile(K, C, E, Wv, I, O, layer // 2, layer, ti, xT)
                else:
                    S.barrier()
                    arH.reset()
                    for nm in ("xr", "xk", "xv", "xm"):
                        R[nm] = arH.carve(nm, [128, KC, T], BF16, nsub=KC)
                    odd_tile(K, C, R, Wv, I, O, layer // 2, layer, ti, xT, xs_vf)
                    S.barrier()
                    arH.reset()
                    C["hid"] = arH.carve("hid", [128, 64, T], BF16, nsub=64)
                mlp_tile(K, C, Wv, layer, xT)
                K.dma(xs[:, t, :, :], xT.all(), eng="act")

        S.barrier()
        arH.reset()
        yo = [arH.carve("yo%d" % i, [128, 2, D], F32) for i in range(2)]
        for t in range(NT):
            K.dma(xT.all(), xs[:, t, :, :])
            y = yo[t % 2]
            for b in range(2):
                for c4 in range(4):
                    ps = K.ps()
                    for j in range(4):
                        c = c4 * 4 + j
                        K.tr(ps[:, j * 128:(j + 1) * 128], xT[:, c, b * 128:(b + 1) * 128], C["identf"][:, :])
                    K.copy(y[:, b, c4 * 512:(c4 + 1) * 512], ps[:, 0:512], eng=("act" if c4 % 2 == 0 else "dve"))
            if t < NTP:
                dst = O["yp"][t * T:(t + 1) * T, :].rearrange("(b p) d -> p b d", p=128)
                K.store(dst, y.base, y.bufs)
            else:
                for b in range(2):
                    sq = (t - NTP) * 2 + b
                    K.store(O["ysm"][sq], y.base[0:8, b, :], y.bufs)
        S.final_wait("sp")
        S.replay()
    return nc


def t5_bucket_np(dist):
    max_exact = 16
    d = np.maximum(dist, 0)
    large = max_exact + (np.log(np.maximum(d, max_exact).astype(np.float32) / max_exact)
                         / math.log(128 / max_exact) * (32 - max_exact)).astype(np.int32)
    large = np.minimum(large, 31)
    return np.where(d < max_exact, d, large).astype(np.int32)


def make_consts():
    c = {}
    c["ident"] = np.eye(128, dtype=np.float32)
    c["antiid"] = np.ascontiguousarray(np.eye(128, dtype=np.float32)[::-1])
    oh = np.zeros((32, 384), np.float32)
    bk = t5_bucket_np(np.arange(128))
    oh[bk, 128 + np.arange(128)] = 1.0
    c["oh384"] = oh
    vm = np.zeros((16, 384), np.float32)
    vm[:, 128:256] = 1.0
    c["validm"] = vm
    j = np.arange(128)[:, None]
    i = np.arange(128)[None, :]
    c["maskBD"] = ((j <= i) & (j // 64 == i // 64)).astype(np.float32)
    rm = np.ones((128, T), np.float32)
    rm[:, ::64] = 0.0
    c["rm"] = rm
    tm = np.zeros((128, T), np.float32)
    for b in range(T // 128):
        tm[:, b * 128:b * 128 + 8] = 1.0
    c["tokmask"] = tm
    p = np.arange(128)[:, None] % 64
    fcol = np.arange(128)[None, :] % 64
    mg = np.zeros((128, 512), np.float32)
    mg[:, 0:128] = (fcol > p)
    mg[:, 128:256] = (fcol > p)
    mg[:, 256:384] = (fcol < p)
    t64 = np.arange(64)[None, :]
    mg[:, 384:448] = (t64 >= p)
    mg[:, 448:512] = (t64 >= p)
    c["maskG"] = mg
    ob = np.zeros((128, 128), np.float32)
    ob[0:64, 0:64] = 1.0
    ob[64:128, 64:128] = 1.0
    c["onesblk"] = ob
    return c


def kernel(_cfg=None, **inp):
    SEQ, DEPTH = (4096, 4) if _cfg is None else _cfg
    NE = (DEPTH + 1) // 2
    NO = DEPTH // 2
    nc = build_program(SEQ, DEPTH)
    consts = make_consts()
    f = lambda a: np.ascontiguousarray(np.asarray(a, dtype=np.float32))

    def pc(a):
        a = np.asarray(a, dtype=np.float32)
        lead = a.shape[:-1]
        n = a.shape[-1] // 128
        a = a.reshape(lead + (n, 128))
        return np.ascontiguousarray(np.moveaxis(a, -1, 0))
    shared = {}
    for n in ["norm_mix_pre", "norm_mix_post", "norm_ffn_pre", "norm_ffn_post"]:
        shared[n] = pc(inp[n])
    shared["w_up"] = f(inp["w_up"]).reshape(DEPTH * D, DFF)
    shared["w_down"] = f(inp["w_down"]).reshape(DEPTH * DFF, D)
    shared["w_in"] = f(inp["w_in_even"]).reshape(NE * D, 5632)
    shared["w_out"] = f(inp["w_out_even"]).reshape(NE * D, D)
    shared["lb_raw"] = pc(inp["hgrn_lb_raw"])
    shared["gn"] = pc(inp["hgrn_norm_g"])
    shared["rel_bias"] = f(inp["rel_bias"])
    shared["sinks"] = np.ascontiguousarray(np.broadcast_to(f(inp["attn_sinks"])[None], (64, NE, 16)))
    if NO > 0:
        for n, src in (("wr", "rw_wr"), ("wk", "rw_wk"), ("wv", "rw_wv"), ("wo", "rw_wo")):
            shared[n] = f(inp[src]).reshape(NO * D, D)
        shared["w1"] = f(inp["rw_w1"]).reshape(NO * D, 96); shared["w2"] = f(inp["rw_w2"]).reshape(NO * 96, D)
        shared["a1"] = f(inp["rw_a1"]).reshape(NO * D, 96); shared["a2"] = f(inp["rw_a2"]).reshape(NO * 96, D)
        shared["g1"] = f(inp["rw_g1"]).reshape(NO * D, 256); shared["g2"] = f(inp["rw_g2"]).reshape(NO * 256, D)
        if NO > 1:
            shared["v1"] = f(inp["rw_v1"]).reshape((NO - 1) * D, 64); shared["v2"] = f(inp["rw_v2"]).reshape((NO - 1) * 64, D)
            shared["p_v0"] = pc(inp["rw_v0"])
        shared["p_mu"] = pc(inp["rw_mu"])
        for n, src in (("w0", "rw_w0"), ("a0", "rw_a0"), ("kk", "rw_kk"), ("ka", "rw_ka"), ("lnx_g", "rw_lnx_g"), ("lnx_b", "rw_lnx_b")):
            shared["p_" + n] = pc(inp[src])
        shared["p_rk"] = pc(np.asarray(inp["rw_rk"]).reshape(NO, D))
    shared.update(consts)
    in_maps = []
    for c in range(NCORES):
        m = dict(shared)
        m["xp"] = f(inp["x_prompt"][c % 2])
        xs_ = np.zeros((4, 128, D), np.float32)
        xs_[:, 0:8, :] = np.asarray(inp["x_sample"])[4 * c:4 * c + 4]
        m["xsm"] = xs_.reshape(512, D)
        m["state_hgrn"] = f(inp["state_hgrn"][:, 4 * c:4 * c + 4])
        m["cache_k"] = f(inp["cache_swa_k"][:, 4 * c:4 * c + 4])
        m["cache_v"] = f(inp["cache_swa_v"][:, 4 * c:4 * c + 4])
        if NO > 0:
            m["state_rwkv"] = f(inp["state_rwkv"][:, 4 * c:4 * c + 4])
            m["state_shift"] = pc(inp["state_shift"][:, 4 * c:4 * c + 4])
        in_maps.append({"i_" + k: v for k, v in m.items()})
    ncr = int(os.environ.get("K_NCORES", NCORES))
    res = run_bass_kernel_spmd(nc, in_maps[:ncr], core_ids=list(range(ncr)))
    R = [{k[2:]: v for k, v in r.items()} for r in res.results]
    R = (R * NCORES)[:NCORES]
    cat = lambda nm, ax: np.concatenate([R[c][nm] for c in range(NCORES)], axis=ax)
    y_prompt = np.stack([R[0]["yp"], R[1]["yp"]], 0)
    y_sample = cat("ysm", 0)
    hgrn_p = np.stack([R[0]["hgrn_p"], R[1]["hgrn_p"]], 1)
    hgrn_s = cat("hgrn_s", 1)
    k_p = np.stack([R[0]["k_p"], R[1]["k_p"]], 1)
    v_p = np.stack([R[0]["v_p"], R[1]["v_p"]], 1)
    k_s = cat("k_s", 1)
    v_s = cat("v_s", 1)
    if NO == 0:
        return (y_prompt, y_sample, hgrn_p, hgrn_s, k_p, k_s, v_p, v_s)
    rwkv_p = np.stack([R[0]["rwkv_p"], R[1]["rwkv_p"]], 1)
    rwkv_s = cat("rwkv_s", 1)
    shift_p = np.stack([R[0]["shift_p"], R[1]["shift_p"]], 1)
    shift_s = cat("shift_s", 1)
    return (y_prompt, y_sample, hgrn_p, hgrn_s, k_p, k_s, v_p, v_s, rwkv_p, rwkv_s, shift_p, shift_s)


def odd_tile(K, C, R, Wv, I, O, o, layer, ti, xT, xs_vf):
    hT = C["hT"]
    tf = C["tmpf"]
    NCH = T // 64
    ps = K.ps()
    for c in range(KC):
        sq = C["sq"][c % 2]
        K.act(sq[:, :], xT[:, c, :], AF.Square)
        K.mm(ps[:, 0:T], C["ones"][:, :], sq[:, :], start=(c == 0), stop=(c == KC - 1))
    rstd = C["rstd"]
    K.act(rstd[:, :], ps[:, 0:T], AF.Ln, bias=C["epsc"][:, 0:1], scale=1.0 / D)
    K.act(rstd[:, :], rstd[:, :], AF.Exp, scale=-0.5)
    carry = R["carry"]
    xx = R["xx"]
    if ti.first:
        K.memset(carry[:, :], 0.0)
    if ti.sample:
        shs = R["shs"]
        for b in range(2):
            K.dma(shs[:, b, :], V(I["state_shift"][:, o, ti.sq0 + b, :], []))
    for c in range(KC):
        hx = tf[c % 2]
        K.stt(hx[:, 1:T + 1], xT[:, c, :], C["g_mix_pre"][:, layer, c:c + 1], rstd[:, :], ALU.mult, ALU.mult)
        if ti.sample:
            K.copy(hx[:, 0:1], shs[:, 0, c:c + 1], eng="pool")
        else:
            K.copy(hx[:, 0:1], carry[:, c:c + 1], eng="pool")
        K.copy(hT[:, c, :], hx[:, 1:T + 1], eng="act")
        K.tt(xx[:, c, :], hx[:, 0:T], hx[:, 1:T + 1], ALU.subtract)
        if ti.sample:
            K.tt(xx[:, c, 128:129], shs[:, 1, c:c + 1], hx[:, 129:130], ALU.subtract, eng="pool")
            for b in range(2):
                K.copy(R["shout"][:, b, c:c + 1], hx[:, b * 128 + 8:b * 128 + 9], eng="pool")
        else:
            K.copy(carry[:, c:c + 1], hx[:, T:T + 1], eng="pool")
    if ti.sample or ti.last_prompt:
        pst = K.ps()
        srcs = [R["shout"][:, b, :] for b in range(2)] if ti.sample else [carry[:, :]]
        for i_, s_ in enumerate(srcs):
            K.tr(pst[0:16, i_ * 128:(i_ + 1) * 128], s_, C["identf"][:, :])
        sho = R["sho"]
        K.copy(sho[0:16, 0:128 * len(srcs)], pst[0:16, 0:128 * len(srcs)], eng="act")
        if ti.sample:
            for b in range(2):
                K.store(O["shift_s"][o, ti.sq0 + b].rearrange("(c p) -> c p", p=128), sho.base[0:16, b * 128:(b + 1) * 128], sho.bufs)
        else:
            K.store(O["shift_p"][o].rearrange("(c p) -> c p", p=128), sho.base[0:16, 0:128], sho.bufs)

    mu = R["mu"]

    def build_mix(dst, i):
        for c in range(KC):
            K.stt(dst[:, c, :], xx[:, c, :], mu[:, o, i, c:c + 1], hT[:, c, :], ALU.mult, ALU.add, eng="dve")
    xr, xk, xv, xm = R["xr"], R["xk"], R["xv"], R["xm"]
    l1 = R["l1"]
    build_mix(xm, 1)
    dense_fm(K, Wv["w1"], o * D, KC, 0, 96, lambda kc: xm[:, kc, :], lambda m, ps: K.act(l1[0:96, 0, :], ps[0:96, 0:T], AF.Tanh), mrows=96)
    build_mix(xm, 4)
    dense_fm(K, Wv["a1"], o * D, KC, 0, 96, lambda kc: xm[:, kc, :], lambda m, ps: K.copy(l1[0:96, 1, :], ps[0:96, 0:T], eng="act"), mrows=96)
    build_mix(xm, 5)
    dense_fm(K, Wv["g1"], o * D, KC, 0, 256, lambda kc: xm[:, kc, :], lambda m, ps: K.act(l1[:, 2 + m, :], ps[:, 0:T], AF.Sigmoid))
    build_mix(xv, 3)
    if o >= 1:
        dense_fm(K, Wv["v1"], (o - 1) * D, KC, 0, 64, lambda kc: xv[:, kc, :], lambda m, ps: K.copy(l1[0:64, 4, :], ps[0:64, 0:T], eng="act"), mrows=64)
    build_mix(xr, 0)
    build_mix(xk, 2)

    H, Hbf = R["H"], R["Hbf"]
    if ti.first:
        K.memset(H.all(), 0.0)
        if Hbf is not H:
            K.memset(Hbf.all(), 0.0)
    passes = [[0, 1, 2, 3]] if not ti.sample else [[0], [2]]
    yg = R["xx"]
    for pi, chunks in enumerate(passes):
        if ti.sample:
            sq = ti.sq0 + pi
            stg = R["stg"]
            for hh in range(2):
                K.dma(stg[hh * 64:(hh + 1) * 64, :, hh * 64:(hh + 1) * 64],
                      V(I["state_rwkv"][o, sq].rearrange("(hp two) i j -> two i hp j", two=2)[hh], []))
            for g4 in range(4):
                pst = K.ps()
                for j in range(4):
                    K.tr(pst[:, j * 128:(j + 1) * 128], stg[:, g4 * 4 + j, :], C["identf"][:, :])
                K.copy(H[:, g4 * 4:(g4 + 1) * 4, :], pst.v3(slice(None), 0, 512, 128), eng="act")
                if Hbf is not H:
                    K.copy(Hbf[:, g4 * 4:(g4 + 1) * 4, :], H[:, g4 * 4:(g4 + 1) * 4, :], eng="pool")
        rwkv_groups(K, C, R, Wv, I, O, o, ti, chunks, yg, xs_vf, first_pass=(pi == 0))
        if ti.sample or ti.last_prompt:
            stg = R["stg"]
            for g4 in range(4):
                pst = K.ps()
                for j in range(4):
                    K.tr(pst[:, j * 128:(j + 1) * 128], H[:, g4 * 4 + j, :], C["identf"][:, :])
                K.copy(stg[:, g4 * 4:(g4 + 1) * 4, :], pst.v3(slice(None), 0, 512, 128), eng="act")
            dst = (O["rwkv_s"][o, ti.sq0 + pi] if ti.sample else O["rwkv_p"][o]).rearrange("(hp two) i j -> two i hp j", two=2)
            for hh in range(2):
                K.store(dst[hh], stg.base[hh * 64:(hh + 1) * 64, :, hh * 64:(hh + 1) * 64], stg.bufs)
            if ti.sample:
                pass
    mo = C["mo"]
    dense_fm(K, Wv["wo"], o * D, KC, 0, D, lambda kc: yg[:, kc, :],
             lambda m, ps: K.copy(mo[:, m, :], ps[:, 0:T], eng=("act" if m % 2 == 0 else "dve")))
    post_residual2(K, C, mo, lambda c: C["g_mix_post"][:, layer, c:c + 1], xT)


def rwkv_groups(K, C, R, Wv, I, O, o, ti, chunks, yg, xs_vf, first_pass):
    tf = C["tmpf"]
    xr, xk, xv = R["xr"], R["xk"], R["xv"]
    l1 = R["l1"]
    H, Hbf = R["H"], R["Hbf"]
    P = R["P"]
    ft = R["ft"]
    for grp in range(8):
        rg, kg, vg, lwg, ag, gg = R["rg"], R["kg"], R["vg"], R["lwg"], R["ag"], R["gg"]
        c0 = grp * 256
        dense_fm(K, Wv["wr"], o * D, KC, c0, 256, lambda kc: xr[:, kc, :], lambda m, ps: K.copy(rg[:, m, :], ps[:, 0:T], eng="act"))
        dense_fm(K, Wv["wk"], o * D, KC, c0, 256, lambda kc: xk[:, kc, :], lambda m, ps: K.copy(kg[:, m, :], ps[:, 0:T], eng="act"))
        dense_fm(K, Wv["wv"], o * D, KC, c0, 256, lambda kc: xv[:, kc, :], lambda m, ps: K.copy(vg[:, m, :], ps[:, 0:T], eng="act"))
        def ev_w(m, ps):
            hp = grp * 2 + m
            K.act(lwg[:, m, :], ps[:, 0:T], AF.Sigmoid, bias=P["w0"][:, o, hp:hp + 1])
            K.ts(lwg[:, m, :], lwg[:, m, :], -math.exp(-0.5), ALU.mult)
            if ti.sample:
                K.tt(lwg[:, m, :], lwg[:, m, :], C["tokmask"][:, :], ALU.mult, eng="pool")
        dense_fm(K, Wv["w2"], o * 96, 1, c0, 256, lambda kc: l1[0:96, 0, :], ev_w, krows=96)

        def ev_a(m, ps):
            hp = grp * 2 + m
            K.act(ag[:, m, :], ps[:, 0:T], AF.Sigmoid, bias=P["a0"][:, o, hp:hp + 1])
        dense_fm(K, Wv["a2"], o * 96, 1, c0, 256, lambda kc: l1[0:96, 1, :], ev_a, krows=96)
        dense_fm(K, Wv["g2"], o * 256, 2, c0, 256, lambda kc: l1[:, 2 + kc, :], lambda m, ps: K.copy(gg[:, m, :], ps[:, 0:T], eng="act"))
        if o >= 1:
            vfg = R["vfg"]
            K.dma(vfg.all(), xs_vf[:, ti.t, grp * 2:(grp + 1) * 2, :])

            def ev_v(m, ps):
                hp = grp * 2 + m
                sv, dd = ft[0], ft[1]
                K.act(sv[:, :], ps[:, 0:T], AF.Sigmoid, bias=P["v0"][:, o - 1, hp:hp + 1])
                K.tt(dd[:, :], vfg[:, m, :], vg[:, m, :], ALU.subtract)
                K.tt(dd[:, :], dd[:, :], sv[:, :], ALU.mult)
                K.tt(vg[:, m, :], vg[:, m, :], dd[:, :], ALU.add)
            dense_fm(K, Wv["v2"], (o - 1) * 64, 1, c0, 256, lambda kc: l1[0:64, 4, :], ev_v, krows=64)
        elif first_pass and xs_vf is not None:
            K.dma(xs_vf[:, ti.t, grp * 2:(grp + 1) * 2, :], vg.all(), eng="act")
        for m in range(2):
            rwkv_hp(K, C, R, o, ti, chunks, grp * 2 + m, rg[:, m, :], kg[:, m, :], vg[:, m, :], lwg[:, m, :], ag[:, m, :], gg[:, m, :], yg)


def rwkv_hp(K, C, R, o, ti, chunks, hp, r, k, v, lw, a, g, yg):
    P = R["P"]
    ft = R["ft"]
    H, Hbf = R["H"], R["Hbf"]
    NCH = T // 64
    pcol = lambda nm: P[nm][:, o, hp:hp + 1]
    t_kk, t_kap, t_kp, t_b, t_G, t_e, t_x = ft[2], ft[3], ft[4], ft[5], ft[6], ft[7], ft[8]
    K.ts(t_kk[:, :], k, pcol("kk"), ALU.mult)
    sqb = C["sq"][0]
    K.act(sqb[:, :], t_kk[:, :], AF.Square)
    ps = K.ps()
    K.mm(ps[:, 0:T], C["ones_blk"][:, :], sqb[:, :])
    K.act(t_x[:, :], ps[:, 0:T], AF.Ln, bias=C["epsc"][:, 2:3])
    K.act(t_x[:, :], t_x[:, :], AF.Exp, scale=-0.5)
    K.tt(t_kap[:, :], t_kk[:, :], t_x[:, :], ALU.mult)
    if ti.sample:
        K.tt(t_kap[:, :], t_kap[:, :], C["tokmask"][:, :], ALU.mult, eng="pool")
    K.ts(t_x[:, :], a, pcol("ka"), ALU.mult, pcol("omka"), ALU.add)
    K.tt(t_kp[:, :], k, t_x[:, :], ALU.mult)
    if ti.sample:
        K.tt(t_kp[:, :], t_kp[:, :], C["tokmask"][:, :], ALU.mult, eng="pool")
    K.tt(t_b[:, :], t_kap[:, :], a, ALU.mult)
    K.stt(t_x[:, :], r, pcol("rk"), t_kp[:, :], ALU.mult, ALU.mult)
    sq1 = C["sq"][1]
    K.copy(sq1[:, :], t_x[:, :], eng="act")
    psb = K.ps()
    K.mm(psb[:, 0:T], C["ones_blk"][:, :], sq1[:, :])
    bonus = ft[9]
    K.tt(bonus[:, :], psb[:, 0:T], v, ALU.mult)
    K.scan(t_G[:, :], C["rm"][:, :], lw, 0.0, ALU.mult, ALU.add)
    bd = R["bd"][hp % len(R["bd"])]
    identR = C["identf"] if RD == F32 else C["identb"]
    rt = R["rt"][hp % len(R["rt"])]
    dec = R["decr"][hp % 2]
    K.act(t_e[:, :], t_G[:, :], AF.Exp)
    K.tt(rt[:, :], r, t_e[:, :], ALU.mult)

    def to_bd(idx, a_, b_):
        for hh in range(2):
            sl = slice(hh * 64, (hh + 1) * 64)
            o3 = bd[sl, idx, :, hh * 64:(hh + 1) * 64]
            a3 = a_.m(lambda ap: ap[sl].rearrange("p (c t) -> p c t", t=64))
            if b_ is None:
                K.copy(o3, a3, eng="pool")
            else:
                b3 = b_.m(lambda ap: ap[sl].rearrange("p (c t) -> p c t", t=64))
                K.tt(o3, a3, b3, ALU.mult, eng=("dve" if hh == 0 else "pool"))
    K.tt(t_x[:, :], t_G[:, :], lw, ALU.subtract)
    K.act(t_x[:, :], t_x[:, :], AF.Exp)
    to_bd(2, t_kap[:, :], t_x[:, :])
    K.act(t_e[:, :], t_G[:, :], AF.Exp, scale=-1.0)
    to_bd(0, t_kp[:, :], t_e[:, :])
    to_bd(1, t_b[:, :], t_e[:, :])
    g3 = t_G[:, :].m(lambda ap: ap.rearrange("p (c t) -> p c t", t=64))
    gl = g3.m(lambda ap: ap[:, :, 63:64])
    K.act(dec[:, :, :], gl, AF.Exp)
    x3 = t_x[:, :].m(lambda ap: ap.rearrange("p (c t) -> p c t", t=64))
    K.tt(x3, gl.m(lambda ap: ap.to_broadcast([128, NCH, 64])), g3, ALU.subtract)
    K.act(t_x[:, :], t_x[:, :], AF.Exp)
    to_bd(3, t_kp[:, :], t_x[:, :])
    to_bd(4, t_b[:, :], t_x[:, :])
    to_bd(5, v, None)
    yps = K.ps()
    yf = ft[10]
    if ti.sample:
        K.memset(yf[:, :], 0.0)
    toks, gss, Qs, Pns = R["tok"], R["gs"], R["Q"], R["Pn"]
    for c in chunks:
        pst = K.ps()
        for j, idx in enumerate((5, 3, 4)):
            if RD == F32:
                K.tr(pst[:, j * 128:(j + 1) * 128], bd[:, idx, c, :], identR[:, :])
            else:
                K.tr(pst.bf(slice(None), j * 128, (j + 1) * 128), bd[:, idx, c, :], identR[:, :])
        K.copy(toks[c][:, :], pst[:, 0:384] if RD == F32 else pst.bf(slice(None), 0, 384), eng="act")
    for c in chunks:
        cs = slice(c * 64, (c + 1) * 64)
        pg = K.ps()
        K.mm(pg[:, 0:128], bd[:, 0, c, :], bd[:, 2, c, :])
        K.mm(pg[:, 128:256], bd[:, 1, c, :], bd[:, 2, c, :])
        K.mm(pg[:, 256:384], bd[:, 2, c, :], bd[:, 1, c, :])
        K.mm(pg[:, 384:448], bd[:, 0, c, :], rt[:, cs])
        K.mm(pg[:, 448:512], bd[:, 1, c, :], rt[:, cs])
        K.tt(gss[c][:, :], pg[:, 0:512], C["maskG"][:, :], ALU.mult)
    for c in chunks:
        K.tt(Qs[c][:, :], identR[:, :], gss[c][:, 128:256], ALU.subtract, eng="pool")
    for lvl in range(1, 6):
        for c in chunks:
            Pk = gss[c][:, 128:384] if lvl == 1 else Pns[c][:, 0:256]
            M_, MT_ = Pk.m(lambda ap: ap[:, 0:128]), Pk.m(lambda ap: ap[:, 128:256])
            pp = K.ps()
            if lvl < 5:
                K.mm(pp[:, 0:128], MT_, M_)
            K.mm(pp[:, 128:256], M_, MT_)
            if lvl < 5:
                K.copy(Pns[c][:, 0:256], pp[:, 0:256], eng="act")
            else:
                K.copy(Pns[c][:, 128:256], pp[:, 128:256], eng="act")
        for c in chunks:
            pq = K.ps()
            K.mm(pq[:, 0:128], Pns[c][:, 128:256], Qs[c][:, :])
            K.tt(Qs[c][:, :], pq[:, 0:128], Qs[c][:, :], ALU.add)
    for c in chunks:
        cs = slice(c * 64, (c + 1) * 64)
        tok, gs, Q = toks[c], gss[c], Qs[c]
        Vbd, Ktok, Btok = tok[:, 0:128], tok[:, 128:256], tok[:, 256:384]
        px = K.ps()
        K.mm(px[:, 0:128], bd[:, 2, c, :], Hbf[:, hp, :], start=True, stop=False)
        K.mm(px[:, 0:128], gs[:, 0:128], Vbd, start=False, stop=True)
        xsb = R["xsb"]
        K.copy(xsb[:, :], px[:, 0:128], eng="act")
        pu = K.ps()
        K.mm(pu[:, 0:128], Q[:, :], xsb[:, :])
        nu = R["nu"]
        K.ts(nu[:, :], pu[:, 0:128], -1.0, ALU.mult)
        K.mm(yps[:, cs], Hbf[:, hp, :], rt[:, cs], start=True, stop=False)
        K.mm(yps[:, cs], Vbd, gs[:, 384:448], start=False, stop=False)
        K.mm(yps[:, cs], nu[:, :], gs[:, 448:512], start=False, stop=True)
        ph = K.ps()
        K.mm(ph[:, 0:128], Ktok, Vbd, start=True, stop=False)
        K.mm(ph[:, 0:128], Btok, nu[:, :], start=False, stop=True)
        K.stt(H[:, hp, :], H[:, hp, :], dec[:, c, :], ph[:, 0:128], ALU.mult, ALU.add)
        if Hbf is not H:
            K.copy(Hbf[:, hp, :], H[:, hp, :], eng="act")
        K.copy(yf[:, cs], yps[:, cs], eng="act")
    ybf, ysq = C["sq"][0], C["sq"][1]
    K.copy(ybf[:, :], yf[:, :], eng="pool")
    K.act(ysq[:, :], yf[:, :], AF.Square)
    pn = K.ps()
    K.mm(pn[:, 0:T], C["ones_blk"][:, :], ybf[:, :])
    K.mm(pn[:, T:2 * T], C["ones_blk"][:, :], ysq[:, :])
    mean, var = ft[2], ft[3]
    K.ts(mean[:, :], pn[:, 0:T], 1.0 / 64, ALU.mult)
    K.tt(var[:, :], mean[:, :], mean[:, :], ALU.mult)
    K.stt(var[:, :], pn[:, T:2 * T], 1.0 / 64, var[:, :], ALU.mult, ALU.subtract)
    K.act(var[:, :], var[:, :], AF.Ln, bias=C["epsc"][:, 1:2])
    K.act(var[:, :], var[:, :], AF.Exp, scale=-0.5)
    K.tt(yf[:, :], yf[:, :], mean[:, :], ALU.subtract)
    K.tt(yf[:, :], yf[:, :], var[:, :], ALU.mult)
    K.ts(yf[:, :], yf[:, :], pcol("lnx_g"), ALU.mult, pcol("lnx_b"), ALU.add)
    K.tt(yf[:, :], yf[:, :], bonus[:, :], ALU.add)
    cr = slice(min(chunks) * 64, (max(chunks) + 1) * 64)
    K.tt(yg[:, hp, cr], yf[:, cr], g.m(lambda ap: ap[:, cr]), ALU.mult)
```

```python
import math
import numpy as np
from contextlib import ExitStack
import concourse.bass as bass
import concourse.mybir as mybir
from concourse.bass_utils import run_bass_kernel_spmd

F32 = mybir.dt.float32
BF16 = mybir.dt.bfloat16
AF = mybir.ActivationFunctionType
ALU = mybir.AluOpType

D = 2048
KC = 16
T = 256
DFF = 8192
NCORES = 8


class Buf:
    __slots__ = ("name", "last_write", "reads", "sem")

    def __init__(self, name):
        self.name = name
        self.last_write = None
        self.reads = {}
        self.sem = None


class Sched:
    ENGS = ("pe", "act", "dve", "pool", "sp")

    def __init__(self, nc, stack):
        self.nc = nc
        self.stack = stack
        self.ops = {e: [] for e in self.ENGS}
        self.seq = {e: 0 for e in self.ENGS}
        self.seen = {e: {} for e in self.ENGS}
        self.sems = {}
        self.n_dma_sems = 0
        for e in self.ENGS:
            self.sems[("eng", e)] = stack.enter_context(nc.semaphore("s_" + e))
        self.dma_total = {}
        self.out_tokens = {}

    def _buf_sem(self, buf):
        if buf.sem is None:
            key = ("dma", self.n_dma_sems)
            self.n_dma_sems += 1
            self.sems[key] = self.stack.enter_context(self.nc.semaphore("d%d" % key[1]))
            buf.sem = key
        return buf.sem

    def _deps(self, eng, reads, writes):
        deps = {}
        mykey = ("eng", eng)
        for b in reads:
            t = b.last_write
            if t is not None and deps.get(t[0], 0) < t[1]:
                deps[t[0]] = t[1]
        for b in writes:
            t = b.last_write
            if t is not None and t[0] != mykey and deps.get(t[0], 0) < t[1]:
                deps[t[0]] = t[1]
            for k, v in b.reads.items():
                if k != mykey and deps.get(k, 0) < v:
                    deps[k] = v
        out = []
        seen = self.seen[eng]
        for k, v in deps.items():
            if k == mykey and eng == "pe":
                continue
            if seen.get(k, 0) >= v:
                continue
            seen[k] = v
            out.append((k, v))
        return out

    def op(self, eng, fn, reads=(), writes=()):
        waits = self._deps(eng, reads, writes)
        self.seq[eng] += 1
        k = ("eng", eng)
        v = self.seq[eng]
        self.ops[eng].append((waits, fn, (k, 1)))
        for b in reads:
            if b.reads.get(k, 0) < v:
                b.reads[k] = v
        for b in writes:
            b.last_write = (k, v)
            b.reads = {}

    def dma(self, eng, fns, dst, reads=()):
        if dst is None:
            dsts = []
            waits = self._deps(eng, reads, [])
            key = self._buf_sem(reads[0])
        else:
            dsts = list(dst) if isinstance(dst, (list, tuple)) else [dst]
            waits = self._deps(eng, reads, dsts)
            key = self._buf_sem(dsts[0])
        for i, fn in enumerate(fns):
            self.ops[eng].append((waits if i == 0 else [], fn, (key, 16)))
        self.dma_total[key] = self.dma_total.get(key, 0) + 16 * len(fns)
        v = self.dma_total[key]
        for b in reads:
            if b.reads.get(key, 0) < v:
                b.reads[key] = v
        for d_ in dsts:
            d_.last_write = (key, v)
            d_.reads = {}
        if dst is None:
            self.out_tokens[key] = v

    def barrier(self):
        snap = {("eng", e): self.seq[e] for e in self.ENGS}
        snap.update(self.dma_total)
        for e in self.ENGS:
            waits = []
            for k, v in snap.items():
                if k == ("eng", e) or v <= self.seen[e].get(k, 0):
                    continue
                self.seen[e][k] = v
                waits.append((k, v))
            self.ops[e].append((waits, None, None))

    def final_wait(self, eng):
        waits = [(k, v) for k, v in self.out_tokens.items()]
        self.ops[eng].append((waits, None, None))

    def replay(self):
        nc = self.nc
        with nc.Block() as block:
            def mk(e):
                def body(engobj):
                    sems = self.sems
                    for waits, fn, inc in self.ops[e]:
                        for k, v in waits:
                            engobj.wait_ge(sems[k], v)
                        if fn is None:
                            continue
                        ins = fn(engobj)
                        if inc is not None:
                            ins.then_inc(sems[inc[0]], inc[1])
                return body
            block.tensor(mk("pe"))
            block.scalar(mk("act"))
            block.vector(mk("dve"))
            block.gpsimd(mk("pool"))
            block.sync(mk("sp"))


class V:
    __slots__ = ("ap", "bufs")

    def __init__(self, ap, bufs):
        self.ap = ap
        self.bufs = bufs

    def m(self, f):
        return V(f(self.ap), self.bufs)


class TT:
    def __init__(self, K, name, shape, dtype, kind="sb", nsub=1, dram_kind="Internal"):
        nc = K.nc
        if kind == "sb":
            h = K.st.enter_context(nc.sbuf_tensor(name, list(shape), dtype))
            self.base = h[:]
        elif kind == "ps":
            h = K.st.enter_context(nc.psum_tensor(name, list(shape), dtype))
            self.base = h[:]
        else:
            self.base = nc.dram_tensor(name, list(shape), dtype, kind=dram_kind).ap()
        self.name = name
        self.nsub = nsub
        self.bufs = [Buf("%s.%d" % (name, i)) for i in range(nsub)]

    def __getitem__(self, idx):
        ap = self.base[idx]
        if self.nsub == 1:
            return V(ap, self.bufs)
        i1 = idx[1] if isinstance(idx, tuple) and len(idx) > 1 else slice(None)
        if isinstance(i1, int):
            return V(ap, [self.bufs[i1]])
        lo, hi, _ = i1.indices(self.nsub)
        return V(ap, self.bufs[lo:hi])

    def all(self):
        return V(self.base, self.bufs)


class Arena:
    def __init__(self, K, name, nbytes):
        self.K = K
        self.n4 = nbytes // 4
        h = K.st.enter_context(K.nc.sbuf_tensor(name, [128, self.n4], F32))
        self.base = h[:]
        self.off = 0
        self.name = name

    def reset(self):
        self.off = 0

    def carve(self, name, shape, dtype, nsub=1):
        esz = 2 if dtype == BF16 else 4
        nel = 1
        for d_ in shape[1:]:
            nel *= d_
        n4 = (nel * esz + 31) // 32 * 8
        assert self.off + n4 <= self.n4, "arena %s overflow at %s (%d > %d)" % (self.name, name, (self.off + n4) * 4, self.n4 * 4)
        ap = self.base[0:shape[0], self.off:self.off + n4]
        self.off += n4
        if dtype == BF16:
            ap = ap.bitcast(BF16)
        ap = ap[:, 0:nel]
        if len(shape) == 3:
            ap = ap.rearrange("p (a b) -> p a b", a=shape[1])
        elif len(shape) == 4:
            ap = ap.rearrange("p (a b c) -> p a b c", a=shape[1], b=shape[2])
        t = TT.__new__(TT)
        t.base = ap
        t.name = name
        t.nsub = nsub
        t.bufs = [Buf("%s.%d" % (name, i)) for i in range(nsub)]
        return t


def _b(vs):
    out = []
    for v in vs:
        if isinstance(v, V):
            out.extend(v.bufs)
    return out


def _a(x):
    return x.ap if isinstance(x, V) else x


class PSB:
    def __init__(self, K, i):
        h = K.st.enter_context(K.nc.psum_tensor("psb%d" % i, [128, 512], F32))
        self.base = h[:]
        b_ = Buf("psb%d" % i)
        self.bufs = [b_, b_, b_, b_]

    def __getitem__(self, idx):
        ps_, cs = idx
        c0, c1, _ = cs.indices(512)
        return V(self.base[ps_, cs], self.bufs[c0 // 128:(c1 - 1) // 128 + 1])

    def bf(self, ps_, c0, c1):
        return V(self.base.bitcast(BF16)[ps_, c0:c1], self.bufs[c0 // 256:(c1 - 1) // 256 + 1])

    def v3(self, ps_, c0, c1, inner):
        return V(self.base[ps_, c0:c1].rearrange("p (a b) -> p a b", b=inner), self.bufs[c0 // 128:(c1 - 1) // 128 + 1])


class KB:
    def __init__(self, nc, st):
        self.nc = nc
        self.st = st
        self.S = Sched(nc, st)
        self.psb = [PSB(self, i) for i in range(8)]
        self.psi = 0
        self.wsl = [TT(self, "wsl%d" % i, [128, 16, 256], BF16) for i in range(3)]
        self.wsi = 0
        self.outs = []
        self.pending = []

    def ps(self):
        p = self.psb[self.psi % 8]
        self.psi += 1
        return p

    def wslot(self):
        p = self.wsl[self.wsi % 3]
        self.wsi += 1
        return p

    def act(self, out, in_, func, bias=0.0, scale=1.0, eng="act"):
        o, i, b, s = out.ap, in_.ap, _a(bias), _a(scale)
        self.S.op(eng, lambda e: e.activation(o, i, func, bias=b, scale=s), _b([in_, bias, scale]), _b([out]))

    def tt(self, out, a, b, op, eng="dve"):
        o, x, y = out.ap, a.ap, b.ap
        self.S.op(eng, lambda e: e.tensor_tensor(o, x, y, op), _b([a, b]), _b([out]))

    def ts(self, out, a, s1, op0, s2=None, op1=None, eng="dve"):
        o, x, p1, p2 = out.ap, a.ap, _a(s1), _a(s2)
        if op1 is None:
            self.S.op(eng, lambda e: e.tensor_scalar(o, x, p1, None, op0), _b([a, s1]), _b([out]))
        else:
            self.S.op(eng, lambda e: e.tensor_scalar(o, x, p1, p2, op0, op1), _b([a, s1, s2]), _b([out]))

    def stt(self, out, a, s, b, op0, op1, eng="dve"):
        o, x, p, y = out.ap, a.ap, _a(s), b.ap
        self.S.op(eng, lambda e: e.scalar_tensor_tensor(o, x, p, y, op0, op1), _b([a, s, b]), _b([out]))

    def copy(self, out, a, eng="dve"):
        o, x = out.ap, a.ap
        if eng == "act":
            self.S.op(eng, lambda e: e.copy(o, x), _b([a]), _b([out]))
        else:
            self.S.op(eng, lambda e: e.tensor_copy(o, x), _b([a]), _b([out]))

    def memset(self, out, val, eng="pool"):
        o = out.ap
        self.S.op(eng, lambda e: e.memset(o, val), [], _b([out]))

    def mm(self, out, lhsT, rhs, start=True, stop=True):
        o, l, r = out.ap, lhsT.ap, rhs.ap
        self.S.op("pe", lambda e: e.matmul(o, l, r, start=start, stop=stop), _b([lhsT, rhs]), _b([out]))

    def tr(self, out, in_, ident):
        o, i, d = out.ap, in_.ap, ident.ap
        self.S.op("pe", lambda e: e.transpose(o, i, d), _b([in_, ident]), _b([out]))

    def scan(self, out, d0, d1, init, op0, op1, eng="dve"):
        o, x, y = out.ap, d0.ap, d1.ap
        self.S.op(eng, lambda e: e.tensor_tensor_scan(o, x, y, init, op0, op1), _b([d0, d1]), _b([out]))

    def recip(self, out, a):
        o, x = out.ap, a.ap
        self.S.op("dve", lambda e: e.reciprocal(o, x), _b([a]), _b([out]))

    def dma(self, out, in_, eng="sp"):
        o, i = out.ap, in_.ap
        self.S.dma(eng, [lambda e: e.dma_start(out=o, in_=i)], out.bufs, _b([in_]))

    def dma_multi(self, outs_ins, dstbuf, reads, eng="sp"):
        fns = [(lambda o, i: (lambda e: e.dma_start(out=o, in_=i)))(o, i) for o, i in outs_ins]
        self.S.dma(eng, fns, dstbuf, reads)

    def store(self, out_ap, src_ap, src_bufs, eng="act"):
        self.S.dma(eng, [lambda e: e.dma_start(out=out_ap, in_=src_ap)], None, src_bufs)

    def pdma(self, out, in_ap):
        self.pending.append((out, in_ap))

    def flush_params(self):
        if not self.pending:
            return
        bufs = []
        for o, _ in self.pending:
            for b_ in o.bufs:
                if b_ not in bufs:
                    bufs.append(b_)
        fns = [(lambda o, i: (lambda e: e.dma_start(out=o, in_=i)))(o.ap, i) for o, i in self.pending]
        self.S.dma("sp", fns, bufs, [])
        self.pending = []


NB_HEADS = 16
KVH = 4
import os
DBG = os.environ.get("KDBG", "")
KSTOP = int(os.environ.get("KSTOP", "99"))
RD = F32 if os.environ.get("RWP", "f32") == "f32" else BF16


def rmsnorm_fm(K, C, xin, gcol, out, nch, width, eps=1e-6):
    ps = K.ps()
    for c in range(nch):
        sq = C["sq"][c % 2]
        K.act(sq[:, :], xin(c), AF.Square)
        K.mm(ps[:, 0:T], C["ones"][:, :], sq[:, :], start=(c == 0), stop=(c == nch - 1))
    rstd = C["rstd"]
    K.act(rstd[:, :], ps[:, 0:T], AF.Ln, bias=C["epsc"][:, 0:1] if eps == 1e-6 else C["epsc"][:, 1:2], scale=1.0 / width)
    K.act(rstd[:, :], rstd[:, :], AF.Exp, scale=-0.5)
    for c in range(nch):
        K.stt(out(c), xin(c), gcol(c), rstd[:, :], ALU.mult, ALU.mult)


def load_w(K, w16, r0, nkc, c0, nb, krows=128):
    slot = K.wslot()
    src = w16.base[r0:r0 + nkc * krows, c0:c0 + nb].rearrange("(kc p) c -> p kc c", p=krows)
    K.dma(slot[0:krows, 0:nkc, 0:nb], V(src, w16.bufs))
    return slot


def dense_fm(K, w16, r0, nkc, c0, ncols, rhs, evac, mrows=128, krows=128):
    for cb in range(0, ncols, 256):
        nb = min(256, ncols - cb)
        slot = load_w(K, w16, r0, nkc, c0 + cb, nb, krows)
        for m in range(0, nb, mrows):
            ps = K.ps()
            for kc in range(nkc):
                K.mm(ps[0:mrows, 0:T], slot[0:krows, kc, m:m + mrows], rhs(kc), start=(kc == 0), stop=(kc == nkc - 1))
            evac((cb + m) // mrows, ps)


def dense_tm(K, w16, r0, c0, ncols, hT, evac):
    for cb in range(0, ncols, 256):
        slot = load_w(K, w16, r0, KC, c0 + cb, 256)
        for b in range(T // 128):
            ps = K.ps()
            for kc in range(KC):
                K.mm(ps[:, 0:256], hT[:, kc, b * 128:(b + 1) * 128], slot[:, kc, 0:256], start=(kc == 0), stop=(kc == KC - 1))
            evac(b, cb // 256, ps)


def post_residual2(K, C, mo, gcol, xT):
    ps = K.ps()
    for c in range(KC):
        sq = C["sq"][c % 2]
        K.act(sq[:, :], mo[:, c, :], AF.Square)
        K.mm(ps[:, 0:T], C["ones"][:, :], sq[:, :], start=(c == 0), stop=(c == KC - 1))
    rstd = C["rstd"]
    K.act(rstd[:, :], ps[:, 0:T], AF.Ln, bias=C["epsc"][:, 0:1], scale=1.0 / D)
    K.act(rstd[:, :], rstd[:, :], AF.Exp, scale=-0.5)
    for c in range(KC):
        tmp = C["tmpf"][c % 2]
        K.stt(tmp[:, 0:T], mo[:, c, :], gcol(c), rstd[:, :], ALU.mult, ALU.mult)
        K.tt(xT[:, c, :], xT[:, c, :], tmp[:, 0:T], ALU.add, eng="pool")


def mlp_tile(K, C, W, layer, xT):
    uT = C["hT"]
    rmsnorm_fm(K, C, lambda c: xT[:, c, :], lambda c: C["g_ffn_pre"][:, layer, c:c + 1], lambda c: uT[:, c, :], KC, D)
    hid = C["hid"]

    def ev_up(m, ps):
        r = C["tmpa"][m % 2]
        K.act(r[:, :], ps[:, 0:T], AF.Relu)
        K.tt(hid[:, m, :], r[:, :], r[:, :], ALU.mult, eng=("dve" if m % 2 == 0 else "pool"))
    dense_fm(K, W["w_up"], layer * D, KC, 0, DFF, lambda kc: uT[:, kc, :], ev_up)
    mo = C["mo"]
    wd = W["w_down"]
    for cb in range(0, D, 256):
        pss = [K.ps(), K.ps()]
        for kg in range(4):
            slot = load_w(K, wd, layer * DFF + kg * 2048, 16, cb, 256)
            for m in range(2):
                for kc in range(16):
                    K.mm(pss[m][:, 0:T], slot[:, kc, m * 128:(m + 1) * 128], hid[:, kg * 16 + kc, :],
                         start=(kg == 0 and kc == 0), stop=(kg == 3 and kc == 15))
        for m in range(2):
            K.copy(mo[:, cb // 128 + m, :], pss[m][:, 0:T], eng=("act" if m == 0 else "dve"))
    post_residual2(K, C, mo, lambda c: C["g_ffn_post"][:, layer, c:c + 1], xT)


class TileInfo:
    def __init__(self, t, NTP):
        self.t = t
        self.sample = t >= NTP
        self.first = (t == 0)
        self.last_prompt = (t == NTP - 1)
        self.sq0 = (t - NTP) * 2


def even_tile(K, C, E, Wv, I, O, e, layer, ti, xT):
    hT = C["hT"]
    rmsnorm_fm(K, C, lambda c: xT[:, c, :], lambda c: C["g_mix_pre"][:, layer, c:c + 1], lambda c: hT[:, c, :], KC, D)
    if KSTOP <= 0:
        return
    win = Wv["w_in"]
    r0 = e * D
    rhs = lambda kc: hT[:, kc, :]
    qq, qt, kt, kh, dec, gate, oT, oa = E["qq"], E["qt"], E["kt"], E["kh"], E["dec"], E["gate"], E["oT"], E["oa"]
    tf = C["tmpf"]
    dense_fm(K, win, r0, KC, 0, 1024, rhs, lambda m, ps: K.act(qq[:, m, :], ps[:, 0:T], AF.Silu))
    if KSTOP <= 1:
        return

    def ev_f(m, ps):
        s_, f_, g_, x_ = tf[0], tf[1], tf[2], tf[3]
        K.act(s_[:, 0:T], ps[:, 0:T], AF.Sigmoid)
        K.ts(f_[:, 0:T], s_[:, 0:T], E["oml"][:, e, m:m + 1], ALU.mult, E["lb"][:, e, m:m + 1], ALU.add)
        K.act(f_[:, 0:T], f_[:, 0:T], AF.Ln)
        K.ts(s_[:, 0:T], s_[:, 0:T], E["noml"][:, e, m:m + 1], ALU.mult, E["oml"][:, e, m:m + 1], ALU.add)
        if ti.sample:
            K.tt(f_[:, 0:T], f_[:, 0:T], C["tokmask"][:, :], ALU.mult, eng="pool")
            K.tt(s_[:, 0:T], s_[:, 0:T], C["tokmask"][:, :], ALU.mult, eng="pool")
        K.scan(g_[:, 0:T], C["rm"][:, :], f_[:, 0:T], 0.0, ALU.mult, ALU.add)
        K.act(x_[:, 0:T], g_[:, 0:T], AF.Exp)
        K.stt(qt[:, m, :], qq[:, m, :], 128 ** -0.5, x_[:, 0:T], ALU.mult, ALU.mult)
        K.act(x_[:, 0:T], g_[:, 0:T], AF.Exp, scale=-1.0)
        K.tt(kt[:, m, :], s_[:, 0:T], x_[:, 0:T], ALU.mult)
        g3 = g_[:, 0:T].m(lambda ap: ap.rearrange("p (c t) -> p c t", t=64))
        gl = g3.m(lambda ap: ap[:, :, 63:64])
        K.act(dec[:, m, :, :], gl, AF.Exp)
        x3 = x_[:, 0:T].m(lambda ap: ap.rearrange("p (c t) -> p c t", t=64))
        K.tt(x3, gl.m(lambda ap: ap.to_broadcast([128, T // 64, 64])), g3, ALU.subtract)
        K.act(x_[:, 0:T], x_[:, 0:T], AF.Exp)
        K.tt(kh[:, m, :], s_[:, 0:T], x_[:, 0:T], ALU.mult)
    dense_fm(K, win, r0, KC, 1024, 1024, rhs, ev_f)
    if KSTOP <= 2:
        return
    vtok = E["vtok"]
    dense_tm(K, win, r0, 2048, 1024, hT, lambda b, cb, ps: K.copy(vtok[:, b, cb * 256:(cb + 1) * 256], ps[:, 0:256], eng=("act" if cb % 2 == 0 else "dve")))
    if KSTOP <= 3:
        return
    dense_fm(K, win, r0, KC, 3072, 1024, rhs, lambda m, ps: K.act(gate[:, m, :], ps[:, 0:T], AF.Silu))
    if KSTOP <= 4:
        return
    S, Sbf = E["S"], E["Sbf"]
    if ti.first:
        K.memset(S.all(), 0.0)
        K.memset(Sbf.all(), 0.0)
    for b in range(T // 128) if "H" not in DBG else ():
        c0 = b * 128
        if ti.sample:
            sq = ti.sq0 + b
            K.dma(S.all(), V(I["state_hgrn"][e, sq].rearrange("h k v -> k h v"), []))
            K.copy(Sbf.all(), S.all(), eng="pool")
        for h0 in range(0, 8, 2):
            hs = (h0, h0 + 1)
            pss, pos, ats, khTs = {}, {}, {}, {}
            for h in hs:
                ps = pss[h] = K.ps()
                K.mm(ps[:, 0:128], kt[:, h, c0:c0 + 128], qt[:, h, c0:c0 + 128])
            for h in hs:
                ps = pss[h]
                at = ats[h] = E["attnT"][h % 2]
                K.tt(at[:, :], ps[:, 0:128], C["maskBD"][:, :], ALU.mult)
                K.tr(ps.bf(slice(None), 512, 640), kh[:, h, c0:c0 + 128], C["identb"][:, :])
            for h in hs:
                khT = khTs[h] = E["khT"][h % 2]
                K.copy(khT[:, :], pss[h].bf(slice(None), 512, 640), eng="act")
            for h in hs:
                po = pos[h] = K.ps()
                K.mm(po[:, 0:128], vtok[:, b, h * 128:(h + 1) * 128], ats[h][:, :], start=True, stop=False)
                K.mm(po[:, 0:64], Sbf[:, h, :], qt[:, h, c0:c0 + 64], start=False, stop=ti.sample)
            pds = {}
            for h in hs:
                pd = pds[h] = K.ps()
                K.mm(pd[:, 0:128], khTs[h][0:64, :], vtok[0:64, b, h * 128:(h + 1) * 128])
            for h in hs:
                K.stt(S[:, h, :], S[:, h, :], dec[:, h, 2 * b, :], pds[h][:, 0:128], ALU.mult, ALU.add)
            if not ti.sample:
                for h in hs:
                    K.copy(Sbf[:, h, :], S[:, h, :], eng="act")
                for h in hs:
                    K.mm(pos[h][:, 64:128], Sbf[:, h, :], qt[:, h, c0 + 64:c0 + 128], start=False, stop=True)
                    pd = pds[h] = K.ps()
                    K.mm(pd[:, 0:128], khTs[h][64:128, :], vtok[64:128, b, h * 128:(h + 1) * 128])
                for h in hs:
                    K.stt(S[:, h, :], S[:, h, :], dec[:, h, 2 * b + 1, :], pds[h][:, 0:128], ALU.mult, ALU.add)
                for h in hs:
                    K.copy(Sbf[:, h, :], S[:, h, :], eng="act")
            for h in hs:
                K.copy(oT[:, h, c0:c0 + 128], pos[h][:, 0:128], eng="act")
        if ti.sample:
            K.store(O["hgrn_s"][e, ti.sq0 + b].rearrange("h k v -> k h v"), S.base, S.bufs)
    if ti.last_prompt:
        K.store(O["hgrn_p"][e].rearrange("h k v -> k h v"), S.base, S.bufs)
    if KSTOP <= 5:
        return
    rmsnorm_fm(K, C, lambda c: oT[:, c, :], lambda c: E["gn"][:, e, c:c + 1], lambda c: oT[:, c, :], 8, 1024)
    for h in range(8):
        K.tt(oa[:, h, :], oT[:, h, :], gate[:, h, :], ALU.mult, eng=("dve" if h % 2 == 0 else "pool"))

    if KSTOP <= 6:
        return
    qbT, kbT, vtb, ktokf, vtokf, obT = E["qbT"], E["kbT"], E["vtb"], E["ktokf"], E["vtokf"], E["obT"]
    dense_fm(K, win, r0, KC, 4096, 1024, rhs, lambda m, ps: K.ts(qbT[0:64, m, :], ps[0:64, 0:T], 0.125, ALU.mult), mrows=64)
    if os.environ.get("KSUB") == "1":
        return
    dense_fm(K, win, r0, KC, 5120, 256, rhs, lambda m, ps: K.copy(kbT[0:64, m, 128:128 + T], ps[0:64, 0:T], eng="act"), mrows=64)

    KV = os.environ.get("KV", "")

    def ev_kv(b, cb, ps):
        if cb == 0:
            if "a" not in KV:
                K.copy(ktokf[:, b, :], ps[:, 0:256], eng="act")
        else:
            if "b" not in KV:
                K.copy(vtokf[:, b, :], ps[:, 0:256], eng="act")
            if "c" not in KV:
                K.copy(vtb[:, 1 + b, :], vtokf[:, b, :], eng="dve")
    if os.environ.get("KSUB") == "2":
        return
    dense_tm(K, win, r0, 5120, 512, hT, ev_kv)
    if KSTOP <= 7:
        return
    for b in range(T // 128) if "W" not in DBG else ():
        c0 = b * 128
        if ti.sample:
            sq = ti.sq0 + b
            kc32, vc32 = E["kc32"][b % 2], E["vc32"][b % 2]
            K.dma(kc32[:, :], V(I["cache_k"][e, sq].rearrange("s h d -> s (h d)"), []))
            K.dma(vc32[:, :], V(I["cache_v"][e, sq].rearrange("s h d -> s (h d)"), []))
            pst = K.ps()
            for kv in range(KVH):
                K.tr(pst[0:64, kv * 128:(kv + 1) * 128], kc32[:, kv * 64:(kv + 1) * 64], C["identf"][:, :])
            kcT = E["kcT"][b % 2]
            K.copy(kcT[0:64, :, :], pst.v3(slice(0, 64), 0, 512, 128), eng="act")
            vc = E["vc"][b % 2]
            K.copy(vc[:, :], vc32[:, :], eng="pool")
            kprev = lambda kv: kcT[0:64, kv, :]
            vprev = lambda kv: vc[:, kv * 64:(kv + 1) * 64]
            has_prev = True
            for nm, src32, inp in (("k_s", ktokf, "cache_k"), ("v_s", vtokf, "cache_v")):
                K.store(O[nm][e, sq, 120:128].rearrange("s h d -> s (h d)"), src32.base[0:8, b, :], [src32.bufs[b]])
                K.store(O[nm][e, sq, 0:120].rearrange("s h d -> s (h d)"), I[inp][e, sq, 8:128].rearrange("s h d -> s (h d)"), [src32.bufs[b]])
        else:
            kprev = (lambda b_: (lambda kv: kbT[0:64, kv, b_ * 128:(b_ + 1) * 128]))(b)
            vprev = (lambda b_: (lambda kv: vtb[:, b_, kv * 64:(kv + 1) * 64]))(b)
            has_prev = not (ti.first and b == 0)
        for kv in range(KVH):
            q3 = qbT[0:64, kv * 4:(kv + 1) * 4, c0:c0 + 128]
            pP, pC = E["pP"][kv % 2], E["pC"][kv % 2]
            v3f = lambda vv: vv.m(lambda ap: ap.rearrange("p (g q) -> p g q", g=4))
            if has_prev:
                psP = K.ps()
                K.mm(psP.v3(slice(None), 0, 512, 128), kprev(kv), q3)
                K.act(tf[0][:, :], psP[:, 0:512], AF.Exp)
                K.tt(v3f(pP[:, :]), v3f(tf[0][:, :]), E["EBp"][:, kv * 4:(kv + 1) * 4, :], ALU.mult)
            psC = K.ps()
            K.mm(psC.v3(slice(None), 0, 512, 128), kbT[0:64, kv, 128 + c0:128 + c0 + 128], q3)
            K.act(tf[1][:, :], psC[:, 0:512], AF.Exp)
            K.tt(v3f(pC[:, :]), v3f(tf[1][:, :]), E["EBc"][:, kv * 4:(kv + 1) * 4, :], ALU.mult, eng="pool")
            psN, psD = K.ps(), K.ps()
            if has_prev:
                K.mm(psN[0:64, 0:512], vprev(kv), pP[:, :], start=True, stop=False)
                K.mm(psD[0:64, 0:512], C["ones"][:, 0:64], pP[:, :], start=True, stop=False)
            K.mm(psN[0:64, 0:512], vtb[:, 1 + b, kv * 64:(kv + 1) * 64], pC[:, :], start=not has_prev, stop=True)
            K.mm(psD[0:64, 0:512], C["ones"][:, 0:64], pC[:, :], start=not has_prev, stop=True)
            den = E["den"]
            K.tt(v3f(den[0:64, :]), psD.v3(slice(0, 64), 0, 512, 128),
                 E["sinkexp"][0:64, e, kv * 4:(kv + 1) * 4, :].m(lambda ap: ap.to_broadcast([64, 4, 128])), ALU.add)
            K.recip(den[0:64, :], den[0:64, :])
            K.tt(obT[0:64, kv * 4:(kv + 1) * 4, c0:c0 + 128], psN.v3(slice(0, 64), 0, 512, 128), v3f(den[0:64, :]), ALU.mult)
    if not ti.sample:
        K.copy(kbT[0:64, :, 0:128], kbT[0:64, :, T:T + 128], eng="pool")
        K.copy(vtb[:, 0, :], vtb[:, 2, :], eng="pool")
        if ti.last_prompt:
            for nm, src32 in (("k_p", ktokf), ("v_p", vtokf)):
                K.store(O[nm][e].rearrange("s h d -> s (h d)"), src32.base[:, 1, :], [src32.bufs[1]])
    if KSTOP <= 8:
        return
    wout = Wv["w_out"]
    mo = C["mo"]
    for cb in range(0, D, 256):
        slotA = load_w(K, wout, e * D, 8, cb, 256)
        slotB = load_w(K, wout, e * D + 1024, 16, cb, 256, krows=64)
        for m in range(2):
            ps = K.ps()
            for kc in range(8):
                K.mm(ps[:, 0:T], slotA[:, kc, m * 128:(m + 1) * 128], oa[:, kc, :], start=(kc == 0), stop=False)
            for hh in range(16):
                K.mm(ps[:, 0:T], slotB[0:64, hh, m * 128:(m + 1) * 128], obT[0:64, hh, :], start=False, stop=(hh == 15))
            K.copy(mo[:, cb // 128 + m, :], ps[:, 0:T], eng=("act" if m == 0 else "dve"))
    post_residual2(K, C, mo, lambda c: C["g_mix_post"][:, layer, c:c + 1], xT)


def build_program(SEQ, DEPTH):
    NTP = SEQ // T
    NT = NTP + 2
    NE = (DEPTH + 1) // 2
    NO = DEPTH // 2
    nc = bass.Bass("TRN2", target_bir_lowering=False)
    dth = lambda name, shape, dtype=F32, kind="ExternalInput": nc.dram_tensor({"ExternalInput": "i_", "ExternalOutput": "o_", "Internal": "s_"}[kind] + name, list(shape), dtype, kind=kind)
    dt = lambda *a, **k: dth(*a, **k).ap()
    I = {}
    I["xp"] = dt("xp", [SEQ, D])
    I["xsm"] = dt("xsm", [512, D])
    for n in ["norm_mix_pre", "norm_mix_post", "norm_ffn_pre", "norm_ffn_post"]:
        I[n] = dt(n, [128, DEPTH, KC])
    I["w_up"] = dt("w_up", [DEPTH * D, DFF])
    I["w_down"] = dt("w_down", [DEPTH * DFF, D])
    I["w_in"] = dt("w_in", [NE * D, 5632])
    I["w_out"] = dt("w_out", [NE * D, D])
    I["state_hgrn"] = dt("state_hgrn", [NE, 4, 8, 128, 128])
    I["cache_k"] = dt("cache_k", [NE, 4, 128, 4, 64])
    I["cache_v"] = dt("cache_v", [NE, 4, 128, 4, 64])
    I["lb_raw"] = dt("lb_raw", [128, NE, 8])
    I["gn"] = dt("gn", [128, NE, 8])
    I["rel_bias"] = dt("rel_bias", [32, 16])
    I["sinks"] = dt("sinks", [64, NE, 16])
    if NO > 0:
        for n in ("wr", "wk", "wv", "wo"):
            I[n] = dt(n, [NO * D, D])
        I["w1"] = dt("w1", [NO * D, 96]); I["w2"] = dt("w2", [NO * 96, D])
        I["a1"] = dt("a1", [NO * D, 96]); I["a2"] = dt("a2", [NO * 96, D])
        I["g1"] = dt("g1", [NO * D, 256]); I["g2"] = dt("g2", [NO * 256, D])
        if NO > 1:
            I["v1"] = dt("v1", [(NO - 1) * D, 64]); I["v2"] = dt("v2", [(NO - 1) * 64, D])
            I["p_v0"] = dt("p_v0", [128, NO - 1, 16])
        I["p_mu"] = dt("p_mu", [128, NO, 6, 16])
        for n in ("w0", "a0", "kk", "ka", "rk", "lnx_g", "lnx_b"):
            I["p_" + n] = dt("p_" + n, [128, NO, 16])
        I["state_rwkv"] = dt("state_rwkv", [NO, 4, 32, 64, 64])
        I["state_shift"] = dt("state_shift", [128, NO, 4, 16])
        I["maskG"] = dt("maskG", [128, 512])
        I["onesblk"] = dt("onesblk", [128, 128])
    for n, shp in (("ident", [128, 128]), ("antiid", [128, 128]), ("oh384", [32, 384]), ("validm", [16, 384]), ("maskBD", [128, 128]),
                   ("rm", [128, T]), ("tokmask", [128, T])):
        I[n] = dt(n, shp)
    O = {"_buf": Buf("outputs")}
    O["yp"] = dt("yp", [SEQ, D], kind="ExternalOutput")
    O["ysm"] = dt("ysm", [4, 8, D], kind="ExternalOutput")
    O["hgrn_p"] = dt("hgrn_p", [NE, 8, 128, 128], kind="ExternalOutput")
    O["hgrn_s"] = dt("hgrn_s", [NE, 4, 8, 128, 128], kind="ExternalOutput")
    for nm in ("k_p", "v_p"):
        O[nm] = dt(nm, [NE, 128, 4, 64], kind="ExternalOutput")
    for nm in ("k_s", "v_s"):
        O[nm] = dt(nm, [NE, 4, 128, 4, 64], kind="ExternalOutput")
    if NO > 0:
        O["rwkv_p"] = dt("rwkv_p", [NO, 32, 64, 64], kind="ExternalOutput")
        O["rwkv_s"] = dt("rwkv_s", [NO, 4, 32, 64, 64], kind="ExternalOutput")
        O["shift_p"] = dt("shift_p", [NO, D], kind="ExternalOutput")
        O["shift_s"] = dt("shift_s", [NO, 4, D], kind="ExternalOutput")

    with ExitStack() as st:
        K = KB(nc, st)
        S = K.S
        W = {}
        Wl = {}
        wspecs = [("w_up", D, DFF, DEPTH), ("w_down", DFF, D, DEPTH), ("w_in", D, 5632, NE), ("w_out", D, D, NE)]
        if NO > 0:
            wspecs += [("wr", D, D, NO), ("wk", D, D, NO), ("wv", D, D, NO), ("wo", D, D, NO), ("w1", D, 96, NO), ("w2", 96, D, NO),
                       ("a1", D, 96, NO), ("a2", 96, D, NO), ("g1", D, 256, NO), ("g2", 256, D, NO)]
        if NO > 1:
            wspecs += [("v1", D, 64, NO - 1), ("v2", 64, D, NO - 1)]
        WS = {n_: (r_, c_) for n_, r_, c_, _ in wspecs}
        Lb = {}
        for l in range(DEPTH):
            Lb[(l, "mix")] = Buf("wmix%d" % l)
            Lb[(l, "mlp")] = Buf("wmlp%d" % l)
        for name, rows, cols, nl in wspecs:
            W[name] = TT(K, name + "16", [nl * rows, cols], BF16, kind="dram")
            for l in range(nl):
                if name in ("w_up", "w_down"):
                    Wl[(name, l)] = Lb[(l, "mlp")]
                elif name in ("w_in", "w_out"):
                    Wl[(name, l)] = Lb[(2 * l, "mix")]
                elif name in ("v1", "v2"):
                    Wl[(name, l)] = Lb[(2 * (l + 1) + 1, "mix")]
                else:
                    Wl[(name, l)] = Lb[(2 * l + 1, "mix")]
        xs = TT(K, "xs", [128, NT, KC, T], F32, kind="dram", nsub=NT)
        xsb3 = [Buf("xs%d" % i) for i in range(3)]
        xs.bufs = [xsb3[t % 3] for t in range(NT)]
        xs_vf = TT(K, "xs_vf", [128, NT, KC, T], F32, kind="dram", nsub=NT) if NO > 1 else None
        if xs_vf is not None:
            vfb3 = [Buf("xsvf%d" % i) for i in range(3)]
            xs_vf.bufs = [vfb3[t % 3] for t in range(NT)]
        cast_pairs = {}
        Edh = dth("Ed", [16, 384], F32, kind="Internal")
        Edbuf = Buf("Ed")

        def wview(name, l):
            tv = TT.__new__(TT)
            tv.base = W[name].base
            tv.name = name
            tv.nsub = 1
            tv.bufs = [Wl[(name, l)]]
            return tv

        def cast_weight(name, l, rows_per_layer, cols):
            step = max(1, (1 << 20) // cols)
            pairs = []
            rend = (l + 1) * rows_per_layer
            for r in range(l * rows_per_layer, rend, step):
                r1 = min(r + step, rend)
                pairs.append((W[name].base[r:r1, :], I[name][r:r1, :]))
            cast_pairs.setdefault(Wl[(name, l)].name, (Wl[(name, l)], []))[1].extend(pairs)

        C = {}
        C["ones"] = TT(K, "ones", [128, 128], BF16)
        C["identf"] = TT(K, "identf", [128, 128], F32)
        C["identb"] = TT(K, "identb", [128, 128], BF16)
        C["maskBD"] = TT(K, "maskBD", [128, 128], F32)
        C["rm"] = TT(K, "rm", [128, T], F32)
        C["tokmask"] = TT(K, "tokmask", [128, T], F32)
        C["sq"] = [TT(K, "sq%d" % i, [128, T], BF16) for i in range(2)]
        C["rstd"] = TT(K, "rstd", [128, T], F32)
        C["tmpa"] = [TT(K, "tmpa%d" % i, [128, T], BF16) for i in range(2)]
        C["tmpf"] = [TT(K, "tmpf%d" % i, [128, 512 if i < 2 else T], F32) for i in range(4)]
        C["hT"] = TT(K, "hT", [128, KC, T], BF16, nsub=KC)
        C["mo"] = TT(K, "mo", [128, KC, T], F32, nsub=KC)
        C["epsc"] = TT(K, "epsc", [128, 3], F32)
        xT = TT(K, "xT", [128, KC, T], F32, nsub=KC)
        for n, short in (("norm_mix_pre", "g_mix_pre"), ("norm_mix_post", "g_mix_post"), ("norm_ffn_pre", "g_ffn_pre"), ("norm_ffn_post", "g_ffn_post")):
            C[short] = TT(K, short, [128, DEPTH, KC], F32)
            K.pdma(C[short][:, :, :], I[n])
        K.memset(C["ones"][:, :], 1.0)
        K.memset(C["epsc"][:, 0:1], 1e-6)
        K.memset(C["epsc"][:, 1:2], 64e-5)
        K.memset(C["epsc"][:, 2:3], 1e-24)
        R = {}
        if NO > 0:
            C["maskG"] = TT(K, "maskG_sb", [128, 512], F32)
            K.pdma(C["maskG"][:, :], I["maskG"])
            ob32 = TT(K, "ob32", [128, 128], F32)
            K.pdma(ob32[:, :], I["onesblk"])
            C["ones_blk"] = TT(K, "ones_blk", [128, 128], BF16)
            P = {}
            for n in ("w0", "a0", "kk", "ka", "rk", "lnx_g", "lnx_b"):
                P[n] = TT(K, "P_" + n, [128, NO, 16], F32)
                K.pdma(P[n][:, :, :], I["p_" + n])
            P["omka"] = TT(K, "P_omka", [128, NO, 16], F32)
            if NO > 1:
                P["v0"] = TT(K, "P_v0", [128, NO - 1, 16], F32)
                K.pdma(P["v0"][:, :, :], I["p_v0"])
            R["mu"] = TT(K, "P_mu", [128, NO, 6, 16], F32)
            K.pdma(R["mu"][:, :, :, :], I["p_mu"])
            R["P"] = P
        for n in ("maskBD", "rm", "tokmask"):
            K.pdma(C[n][:, :], I[n])
        K.pdma(C["identf"][:, :], I["ident"])
        K.flush_params()
        K.copy(C["identb"][:, :], C["identf"][:, :])
        if NO > 0:
            K.copy(C["ones_blk"][:, :], ob32[:, :])
            K.ts(R["P"]["omka"][:, :, :], R["P"]["ka"][:, :, :], -1.0, ALU.mult, 1.0, ALU.add)
        arH = Arena(K, "arH", 32 * 1024)
        arM = Arena(K, "arM", 85 * 1024)
        C["hid"] = arH.carve("hid", [128, 64, T], BF16, nsub=64)

        for l in range(DEPTH):
            if l % 2 == 0:
                cast_weight("w_in", l // 2, D, 5632)
                cast_weight("w_out", l // 2, D, D)
            else:
                oo = l // 2
                for n_ in ("w1", "a1", "g1", "w2", "a2", "g2", "wr", "wk", "wv", "wo"):
                    cast_weight(n_, oo, WS[n_][0], WS[n_][1])
                if oo >= 1:
                    cast_weight("v1", oo - 1, D, 64)
                    cast_weight("v2", oo - 1, 64, D)
            cast_weight("w_up", l, D, DFF)
            cast_weight("w_down", l, DFF, D)
            for kind_ in ("mix", "mlp"):
                bb, prs = cast_pairs[Lb[(l, kind_)].name]
                K.dma_multi(prs, bb, [], eng="pool")

        E = {}
        if NE > 0:
            E["lb"] = TT(K, "lb", [128, NE, 8], F32)
            E["oml"] = TT(K, "oml", [128, NE, 8], F32)
            E["noml"] = TT(K, "noml", [128, NE, 8], F32)
            E["gn"] = TT(K, "gn", [128, NE, 8], F32)
            lbr = TT(K, "lbr", [128, NE, 8], F32)
            E["sinkexp"] = TT(K, "sinkexp", [64, NE, 16, 1], F32)
            E["EBp"] = TT(K, "EBp", [128, 16, 128], BF16)
            E["EBc"] = TT(K, "EBc", [128, 16, 128], BF16)
            arH.reset()
            brev = arH.carve("brev", [128, 16, 128], F32)
            relb = arH.carve("relb", [32, 16], F32)
            oh = arH.carve("oh", [32, 384], F32)
            vm = arH.carve("vm", [16, 384], F32)
            Et = arH.carve("Et", [16, 384], F32)
            antiid = arH.carve("antiid", [128, 128], F32)
            K.pdma(lbr[:, :, :], I["lb_raw"])
            K.pdma(E["gn"][:, :, :], I["gn"])
            K.pdma(E["sinkexp"][:, :, :, :], I["sinks"].rearrange("p e (h o) -> p e h o", o=1))
            K.pdma(relb[:, :], I["rel_bias"])
            K.pdma(oh[:, :], I["oh384"])
            K.pdma(vm[:, :], I["validm"])
            K.pdma(antiid[:, :], I["antiid"])
            K.flush_params()
            K.memset(E["lb"][:, :, :], 0.0)
            if NE > 1:
                K.tt(lbr[:, 1, :], lbr[:, 1, :], lbr[:, 0, :], ALU.subtract)
                K.act(E["lb"][:, 1, :], lbr[:, 1, :], AF.Sigmoid)
            K.ts(E["oml"][:, :, :], E["lb"][:, :, :], -1.0, ALU.mult, 1.0, ALU.add)
            K.ts(E["noml"][:, :, :], E["oml"][:, :, :], -1.0, ALU.mult)
            K.act(E["sinkexp"][:, :, :, :], E["sinkexp"][:, :, :, :], AF.Exp)
            ps = K.ps()
            if "E" in DBG:
                K.memset(E["EBp"].all(), 1.0)
                K.memset(E["EBc"].all(), 1.0)
            K.mm(ps[0:16, 0:384], relb[:, :], oh[:, :])
            K.act(Et[:, :], ps[0:16, 0:384], AF.Exp)
            K.tt(Et[:, :], Et[:, :], vm[:, :], ALU.mult)
            K.S.dma("sp", [lambda en: en.dma_start(out=Edh.ap(), in_=Et.base)], Edbuf, Et.bufs)
            for off, name in ((1, "EBc"), (129, "EBp")) if "E" not in DBG else ():
                src = bass.AP(Edh, off, [[1, 128], [384, 16], [1, 128]])
                K.S.dma("sp", [(lambda s_: (lambda en: en.dma_start(out=brev.base, in_=s_)))(src)], brev.bufs, [Edbuf])
                for g in range(4):
                    ps = K.ps()
                    K.mm(ps.v3(slice(None), 0, 512, 128), antiid[:, :], brev[:, g * 4:(g + 1) * 4, :])
                    K.copy(E[name][:, g * 4:(g + 1) * 4, :], ps.v3(slice(None), 0, 512, 128), eng=("act" if g % 2 == 0 else "dve"))

        S.barrier()
        arH.reset()
        xin = [arH.carve("xin%d" % i, [128, 2, D], F32) for i in range(2)]
        for t in range(NT):
            xi = xin[t % 2]
            if t < NTP:
                src = I["xp"][t * T:(t + 1) * T, :].rearrange("(b p) d -> p b d", p=128)
            else:
                src = I["xsm"][(t - NTP) * T:(t - NTP + 1) * T, :].rearrange("(b p) d -> p b d", p=128)
            K.dma(xi[:, :, :], V(src, []))
            for c in range(KC):
                ps = K.ps()
                for b in range(2):
                    K.tr(ps[:, b * 128:(b + 1) * 128], xi[:, b, c * 128:(c + 1) * 128], C["identf"][:, :])
                K.copy(xT[:, c, :], ps[:, 0:T], eng=("act" if c % 2 == 0 else "dve"))
            K.dma(xs[:, t, :, :], xT.all(), eng="act")
        S.barrier()
        arH.reset()
        C["hid"] = arH.carve("hid", [128, 64, T], BF16, nsub=64)

        for layer in range(DEPTH):
            Wv = {"w_up": wview("w_up", layer), "w_down": wview("w_down", layer)}
            S.barrier()
            arM.reset()
            if layer % 2 == 0:
                e = layer // 2
                Wv["w_in"] = wview("w_in", e)
                Wv["w_out"] = wview("w_out", e)
                for nm in ("qq", "qt", "kt", "kh", "gate", "oa"):
                    E[nm] = arM.carve(nm, [128, 8, T], BF16, nsub=8)
                E["oT"] = arM.carve("oT", [128, 8, T], F32, nsub=8)
                E["dec"] = arM.carve("dec", [128, 8, T // 64, 1], F32, nsub=8)
                E["vtok"] = arM.carve("vtok", [128, 2, 1024], BF16, nsub=2)
                E["S"] = arM.carve("S", [128, 8, 128], F32, nsub=8)
                E["Sbf"] = arM.carve("Sbf", [128, 8, 128], BF16, nsub=8)
                E["attnT"] = [arM.carve("attnT%d" % i, [128, 128], BF16) for i in range(2)]
                E["khT"] = [arM.carve("khT%d" % i, [128, 128], BF16) for i in range(2)]
                E["qbT"] = arM.carve("qbT", [64, 16, T], BF16, nsub=16)
                E["obT"] = arM.carve("obT", [64, 16, T], BF16, nsub=16)
                E["kbT"] = arM.carve("kbT", [64, 4, 128 + T], BF16, nsub=4)
                E["vtb"] = arM.carve("vtb", [128, 3, 256], BF16, nsub=3)
                E["ktokf"] = arM.carve("ktokf", [128, 2, 256], F32, nsub=2)
                E["vtokf"] = arM.carve("vtokf", [128, 2, 256], F32, nsub=2)
                E["kc32"] = [arM.carve("kc32%d" % i, [128, 256], F32) for i in range(2)]
                E["vc32"] = [arM.carve("vc32%d" % i, [128, 256], F32) for i in range(2)]
                E["kcT"] = [arM.carve("kcT%d" % i, [64, 4, 128], BF16) for i in range(2)]
                E["vc"] = [arM.carve("vc%d" % i, [128, 256], BF16) for i in range(2)]
                E["pP"] = [arM.carve("pP%d" % i, [128, 512], BF16) for i in range(2)]
                E["pC"] = [arM.carve("pC%d" % i, [128, 512], BF16) for i in range(2)]
                E["den"] = arM.carve("den", [64, 512], F32)
            else:
                oo = layer // 2
                for n_ in ("w1", "a1", "g1", "w2", "a2", "g2", "wr", "wk", "wv", "wo"):
                    Wv[n_] = wview(n_, oo)
                if oo >= 1:
                    Wv["v1"] = wview("v1", oo - 1)
                    Wv["v2"] = wview("v2", oo - 1)
                R["xx"] = arM.carve("xx", [128, KC, T], BF16, nsub=KC)
                R["l1"] = arM.carve("l1", [128, 5, T], BF16, nsub=5)
                R["carry"] = arM.carve("carry", [128, KC], F32)
                R["shs"] = arM.carve("shs", [128, 2, KC], F32, nsub=2)
                R["shout"] = arM.carve("shout", [128, 2, KC], F32)
                for nm in ("rg", "kg", "vg", "lwg", "ag", "vfg"):
                    R[nm] = arM.carve(nm, [128, 2, T], F32, nsub=2)
                R["gg"] = arM.carve("gg", [128, 2, T], BF16, nsub=2)
                R["ft"] = [arM.carve("ft%d" % i, [128, T], F32) for i in range(11)]
                R["sho"] = R["ft"][10]
                nbd = 1 if RD == F32 else 2
                R["bd"] = [arM.carve("bd%d" % i, [128, 6, T // 64, 128], RD) for i in range(nbd)]
                R["rt"] = [arM.carve("rt%d" % i, [128, T], RD) for i in range(nbd)]
                R["decr"] = [arM.carve("decr%d" % i, [128, T // 64, 1], F32) for i in range(2)]
                R["tok"] = [arM.carve("tok%d" % i, [128, 384], RD) for i in range(4)]
                R["gs"] = [arM.carve("gs%d" % i, [128, 512], RD) for i in range(4)]
                R["Q"] = [arM.carve("Q%d" % i, [128, 128], RD) for i in range(4)]
                R["Pn"] = [arM.carve("Pn%d" % i, [128, 256], RD) for i in range(4)]
                R["xsb"] = arM.carve("xsb", [128, 128], RD)
                R["nu"] = arM.carve("nu", [128, 128], RD)
                R["H"] = arM.carve("H", [128, 16, 128], F32, nsub=16)
                R["Hbf"] = arM.carve("Hbf", [128, 16, 128], BF16, nsub=16) if RD == BF16 else R["H"]
                R["stg"] = arM.carve("stg", [128, 16, 128], F32)
                for i in range(len(R["bd"])):
                    K.memset(R["bd"][i].all(), 0.0)
                K.memset(R["stg"].all(), 0.0)
            for t in range(NT):
                ti = TileInfo(t, NTP)
                K.dma(xT.all(), xs[:, t, :, :])
                if layer % 2 == 0:
                    even_tile(K, C, E, Wv, I, O, layer // 2, layer, ti, xT)
                else:
                    S.barrier()
                    arH.reset()
                    for nm in ("xr", "xk", "xv", "xm"):
                        R[nm] = arH.carve(nm, [128, KC, T], BF16, nsub=KC)
                    odd_tile(K, C, R, Wv, I, O, layer // 2, layer, ti, xT, xs_vf)
                    S.barrier()
                    arH.reset()
                    C["hid"] = arH.carve("hid", [128, 64, T], BF16, nsub=64)
                mlp_tile(K, C, Wv, layer, xT)
                K.dma(xs[:, t, :, :], xT.all(), eng="act")

        S.barrier()
        arH.reset()
        yo = [arH.carve("yo%d" % i, [128, 2, D], F32) for i in range(2)]
        for t in range(NT):
            K.dma(xT.all(), xs[:, t, :, :])
            y = yo[t % 2]
            for b in range(2):
                for c4 in range(4):
                    ps = K.ps()
                    for j in range(4):
                        c = c4 * 4 + j
                        K.tr(ps[:, j * 128:(j + 1) * 128], xT[:, c, b * 128:(b + 1) * 128], C["identf"][:, :])
                    K.copy(y[:, b, c4 * 512:(c4 + 1) * 512], ps[:, 0:512], eng=("act" if c4 % 2 == 0 else "dve"))
            if t < NTP:
                dst = O["yp"][t * T:(t + 1) * T, :].rearrange("(b p) d -> p b d", p=128)
                K.store(dst, y.base, y.bufs)
            else:
                for b in range(2):
                    sq = (t - NTP) * 2 + b
                    K.store(O["ysm"][sq], y.base[0:8, b, :], y.bufs)
        S.final_wait("sp")
        S.replay()
    return nc


def t5_bucket_np(dist):
    max_exact = 16
    d = np.maximum(dist, 0)
    large = max_exact + (np.log(np.maximum(d, max_exact).astype(np.float32) / max_exact)
                         / math.log(128 / max_exact) * (32 - max_exact)).astype(np.int32)
    large = np.minimum(large, 31)
    return np.where(d < max_exact, d, large).astype(np.int32)


def make_consts():
    c = {}
    c["ident"] = np.eye(128, dtype=np.float32)
    c["antiid"] = np.ascontiguousarray(np.eye(128, dtype=np.float32)[::-1])
    oh = np.zeros((32, 384), np.float32)
    bk = t5_bucket_np(np.arange(128))
    oh[bk, 128 + np.arange(128)] = 1.0
    c["oh384"] = oh
    vm = np.zeros((16, 384), np.float32)
    vm[:, 128:256] = 1.0
    c["validm"] = vm
    j = np.arange(128)[:, None]
    i = np.arange(128)[None, :]
    c["maskBD"] = ((j <= i) & (j // 64 == i // 64)).astype(np.float32)
    rm = np.ones((128, T), np.float32)
    rm[:, ::64] = 0.0
    c["rm"] = rm
    tm = np.zeros((128, T), np.float32)
    for b in range(T // 128):
        tm[:, b * 128:b * 128 + 8] = 1.0
    c["tokmask"] = tm
    p = np.arange(128)[:, None] % 64
    fcol = np.arange(128)[None, :] % 64
    mg = np.zeros((128, 512), np.float32)
    mg[:, 0:128] = (fcol > p)
    mg[:, 128:256] = (fcol > p)
    mg[:, 256:384] = (fcol < p)
    t64 = np.arange(64)[None, :]
    mg[:, 384:448] = (t64 >= p)
    mg[:, 448:512] = (t64 >= p)
    c["maskG"] = mg
    ob = np.zeros((128, 128), np.float32)
    ob[0:64, 0:64] = 1.0
    ob[64:128, 64:128] = 1.0
    c["onesblk"] = ob
    return c


def kernel(_cfg=None, **inp):
    SEQ, DEPTH = (4096, 4) if _cfg is None else _cfg
    NE = (DEPTH + 1) // 2
    NO = DEPTH // 2
    nc = build_program(SEQ, DEPTH)
    consts = make_consts()
    f = lambda a: np.ascontiguousarray(np.asarray(a, dtype=np.float32))

    def pc(a):
        a = np.asarray(a, dtype=np.float32)
        lead = a.shape[:-1]
        n = a.shape[-1] // 128
        a = a.reshape(lead + (n, 128))
        return np.ascontiguousarray(np.moveaxis(a, -1, 0))
    shared = {}
    for n in ["norm_mix_pre", "norm_mix_post", "norm_ffn_pre", "norm_ffn_post"]:
        shared[n] = pc(inp[n])
    shared["w_up"] = f(inp["w_up"]).reshape(DEPTH * D, DFF)
    shared["w_down"] = f(inp["w_down"]).reshape(DEPTH * DFF, D)
    shared["w_in"] = f(inp["w_in_even"]).reshape(NE * D, 5632)
    shared["w_out"] = f(inp["w_out_even"]).reshape(NE * D, D)
    shared["lb_raw"] = pc(inp["hgrn_lb_raw"])
    shared["gn"] = pc(inp["hgrn_norm_g"])
    shared["rel_bias"] = f(inp["rel_bias"])
    shared["sinks"] = np.ascontiguousarray(np.broadcast_to(f(inp["attn_sinks"])[None], (64, NE, 16)))
    if NO > 0:
        for n, src in (("wr", "rw_wr"), ("wk", "rw_wk"), ("wv", "rw_wv"), ("wo", "rw_wo")):
            shared[n] = f(inp[src]).reshape(NO * D, D)
        shared["w1"] = f(inp["rw_w1"]).reshape(NO * D, 96); shared["w2"] = f(inp["rw_w2"]).reshape(NO * 96, D)
        shared["a1"] = f(inp["rw_a1"]).reshape(NO * D, 96); shared["a2"] = f(inp["rw_a2"]).reshape(NO * 96, D)
        shared["g1"] = f(inp["rw_g1"]).reshape(NO * D, 256); shared["g2"] = f(inp["rw_g2"]).reshape(NO * 256, D)
        if NO > 1:
            shared["v1"] = f(inp["rw_v1"]).reshape((NO - 1) * D, 64); shared["v2"] = f(inp["rw_v2"]).reshape((NO - 1) * 64, D)
            shared["p_v0"] = pc(inp["rw_v0"])
        shared["p_mu"] = pc(inp["rw_mu"])
        for n, src in (("w0", "rw_w0"), ("a0", "rw_a0"), ("kk", "rw_kk"), ("ka", "rw_ka"), ("lnx_g", "rw_lnx_g"), ("lnx_b", "rw_lnx_b")):
            shared["p_" + n] = pc(inp[src])
        shared["p_rk"] = pc(np.asarray(inp["rw_rk"]).reshape(NO, D))
    shared.update(consts)
    in_maps = []
    for c in range(NCORES):
        m = dict(shared)
        m["xp"] = f(inp["x_prompt"][c % 2])
        xs_ = np.zeros((4, 128, D), np.float32)
        xs_[:, 0:8, :] = np.asarray(inp["x_sample"])[4 * c:4 * c + 4]
        m["xsm"] = xs_.reshape(512, D)
        m["state_hgrn"] = f(inp["state_hgrn"][:, 4 * c:4 * c + 4])
        m["cache_k"] = f(inp["cache_swa_k"][:, 4 * c:4 * c + 4])
        m["cache_v"] = f(inp["cache_swa_v"][:, 4 * c:4 * c + 4])
        if NO > 0:
            m["state_rwkv"] = f(inp["state_rwkv"][:, 4 * c:4 * c + 4])
            m["state_shift"] = pc(inp["state_shift"][:, 4 * c:4 * c + 4])
        in_maps.append({"i_" + k: v for k, v in m.items()})
    ncr = int(os.environ.get("K_NCORES", NCORES))
    res = run_bass_kernel_spmd(nc, in_maps[:ncr], core_ids=list(range(ncr)))
    R = [{k[2:]: v for k, v in r.items()} for r in res.results]
    R = (R * NCORES)[:NCORES]
    cat = lambda nm, ax: np.concatenate([R[c][nm] for c in range(NCORES)], axis=ax)
    y_prompt = np.stack([R[0]["yp"], R[1]["yp"]], 0)
    y_sample = cat("ysm", 0)
    hgrn_p = np.stack([R[0]["hgrn_p"], R[1]["hgrn_p"]], 1)
    hgrn_s = cat("hgrn_s", 1)
    k_p = np.stack([R[0]["k_p"], R[1]["k_p"]], 1)
    v_p = np.stack([R[0]["v_p"], R[1]["v_p"]], 1)
    k_s = cat("k_s", 1)
    v_s = cat("v_s", 1)
    if NO == 0:
        return (y_prompt, y_sample, hgrn_p, hgrn_s, k_p, k_s, v_p, v_s)
    rwkv_p = np.stack([R[0]["rwkv_p"], R[1]["rwkv_p"]], 1)
    rwkv_s = cat("rwkv_s", 1)
    shift_p = np.stack([R[0]["shift_p"], R[1]["shift_p"]], 1)
    shift_s = cat("shift_s", 1)
    return (y_prompt, y_sample, hgrn_p, hgrn_s, k_p, k_s, v_p, v_s, rwkv_p, rwkv_s, shift_p, shift_s)


def odd_tile(K, C, R, Wv, I, O, o, layer, ti, xT, xs_vf):
    hT = C["hT"]
    tf = C["tmpf"]
    NCH = T // 64
    ps = K.ps()
    for c in range(KC):
        sq = C["sq"][c % 2]
        K.act(sq[:, :], xT[:, c, :], AF.Square)
        K.mm(ps[:, 0:T], C["ones"][:, :], sq[:, :], start=(c == 0), stop=(c == KC - 1))
    rstd = C["rstd"]
    K.act(rstd[:, :], ps[:, 0:T], AF.Ln, bias=C["epsc"][:, 0:1], scale=1.0 / D)
    K.act(rstd[:, :], rstd[:, :], AF.Exp, scale=-0.5)
    carry = R["carry"]
    xx = R["xx"]
    if ti.first:
        K.memset(carry[:, :], 0.0)
    if ti.sample:
        shs = R["shs"]
        for b in range(2):
            K.dma(shs[:, b, :], V(I["state_shift"][:, o, ti.sq0 + b, :], []))
    for c in range(KC):
        hx = tf[c % 2]
        K.stt(hx[:, 1:T + 1], xT[:, c, :], C["g_mix_pre"][:, layer, c:c + 1], rstd[:, :], ALU.mult, ALU.mult)
        if ti.sample:
            K.copy(hx[:, 0:1], shs[:, 0, c:c + 1], eng="pool")
        else:
            K.copy(hx[:, 0:1], carry[:, c:c + 1], eng="pool")
        K.copy(hT[:, c, :], hx[:, 1:T + 1], eng="act")
        K.tt(xx[:, c, :], hx[:, 0:T], hx[:, 1:T + 1], ALU.subtract)
        if ti.sample:
            K.tt(xx[:, c, 128:129], shs[:, 1, c:c + 1], hx[:, 129:130], ALU.subtract, eng="pool")
            for b in range(2):
                K.copy(R["shout"][:, b, c:c + 1], hx[:, b * 128 + 8:b * 128 + 9], eng="pool")
        else:
            K.copy(carry[:, c:c + 1], hx[:, T:T + 1], eng="pool")
    if ti.sample or ti.last_prompt:
        pst = K.ps()
        srcs = [R["shout"][:, b, :] for b in range(2)] if ti.sample else [carry[:, :]]
        for i_, s_ in enumerate(srcs):
            K.tr(pst[0:16, i_ * 128:(i_ + 1) * 128], s_, C["identf"][:, :])
        sho = R["sho"]
        K.copy(sho[0:16, 0:128 * len(srcs)], pst[0:16, 0:128 * len(srcs)], eng="act")
        if ti.sample:
            for b in range(2):
                K.store(O["shift_s"][o, ti.sq0 + b].rearrange("(c p) -> c p", p=128), sho.base[0:16, b * 128:(b + 1) * 128], sho.bufs)
        else:
            K.store(O["shift_p"][o].rearrange("(c p) -> c p", p=128), sho.base[0:16, 0:128], sho.bufs)

    mu = R["mu"]

    def build_mix(dst, i):
        for c in range(KC):
            K.stt(dst[:, c, :], xx[:, c, :], mu[:, o, i, c:c + 1], hT[:, c, :], ALU.mult, ALU.add, eng="dve")
    xr, xk, xv, xm = R["xr"], R["xk"], R["xv"], R["xm"]
    l1 = R["l1"]
    build_mix(xm, 1)
    dense_fm(K, Wv["w1"], o * D, KC, 0, 96, lambda kc: xm[:, kc, :], lambda m, ps: K.act(l1[0:96, 0, :], ps[0:96, 0:T], AF.Tanh), mrows=96)
    build_mix(xm, 4)
    dense_fm(K, Wv["a1"], o * D, KC, 0, 96, lambda kc: xm[:, kc, :], lambda m, ps: K.copy(l1[0:96, 1, :], ps[0:96, 0:T], eng="act"), mrows=96)
    build_mix(xm, 5)
    dense_fm(K, Wv["g1"], o * D, KC, 0, 256, lambda kc: xm[:, kc, :], lambda m, ps: K.act(l1[:, 2 + m, :], ps[:, 0:T], AF.Sigmoid))
    build_mix(xv, 3)
    if o >= 1:
        dense_fm(K, Wv["v1"], (o - 1) * D, KC, 0, 64, lambda kc: xv[:, kc, :], lambda m, ps: K.copy(l1[0:64, 4, :], ps[0:64, 0:T], eng="act"), mrows=64)
    build_mix(xr, 0)
    build_mix(xk, 2)

    H, Hbf = R["H"], R["Hbf"]
    if ti.first:
        K.memset(H.all(), 0.0)
        if Hbf is not H:
            K.memset(Hbf.all(), 0.0)
    passes = [[0, 1, 2, 3]] if not ti.sample else [[0], [2]]
    yg = R["xx"]
    for pi, chunks in enumerate(passes):
        if ti.sample:
            sq = ti.sq0 + pi
            stg = R["stg"]
            for hh in range(2):
                K.dma(stg[hh * 64:(hh + 1) * 64, :, hh * 64:(hh + 1) * 64],
                      V(I["state_rwkv"][o, sq].rearrange("(hp two) i j -> two i hp j", two=2)[hh], []))
            for g4 in range(4):
                pst = K.ps()
                for j in range(4):
                    K.tr(pst[:, j * 128:(j + 1) * 128], stg[:, g4 * 4 + j, :], C["identf"][:, :])
                K.copy(H[:, g4 * 4:(g4 + 1) * 4, :], pst.v3(slice(None), 0, 512, 128), eng="act")
                if Hbf is not H:
                    K.copy(Hbf[:, g4 * 4:(g4 + 1) * 4, :], H[:, g4 * 4:(g4 + 1) * 4, :], eng="pool")
        rwkv_groups(K, C, R, Wv, I, O, o, ti, chunks, yg, xs_vf, first_pass=(pi == 0))
        if ti.sample or ti.last_prompt:
            stg = R["stg"]
            for g4 in range(4):
                pst = K.ps()
                for j in range(4):
                    K.tr(pst[:, j * 128:(j + 1) * 128], H[:, g4 * 4 + j, :], C["identf"][:, :])
                K.copy(stg[:, g4 * 4:(g4 + 1) * 4, :], pst.v3(slice(None), 0, 512, 128), eng="act")
            dst = (O["rwkv_s"][o, ti.sq0 + pi] if ti.sample else O["rwkv_p"][o]).rearrange("(hp two) i j -> two i hp j", two=2)
            for hh in range(2):
                K.store(dst[hh], stg.base[hh * 64:(hh + 1) * 64, :, hh * 64:(hh + 1) * 64], stg.bufs)
            if ti.sample:
                pass
    mo = C["mo"]
    dense_fm(K, Wv["wo"], o * D, KC, 0, D, lambda kc: yg[:, kc, :],
             lambda m, ps: K.copy(mo[:, m, :], ps[:, 0:T], eng=("act" if m % 2 == 0 else "dve")))
    post_residual2(K, C, mo, lambda c: C["g_mix_post"][:, layer, c:c + 1], xT)


def rwkv_groups(K, C, R, Wv, I, O, o, ti, chunks, yg, xs_vf, first_pass):
    tf = C["tmpf"]
    xr, xk, xv = R["xr"], R["xk"], R["xv"]
    l1 = R["l1"]
    H, Hbf = R["H"], R["Hbf"]
    P = R["P"]
    ft = R["ft"]
    for grp in range(8):
        rg, kg, vg, lwg, ag, gg = R["rg"], R["kg"], R["vg"], R["lwg"], R["ag"], R["gg"]
        c0 = grp * 256
        dense_fm(K, Wv["wr"], o * D, KC, c0, 256, lambda kc: xr[:, kc, :], lambda m, ps: K.copy(rg[:, m, :], ps[:, 0:T], eng="act"))
        dense_fm(K, Wv["wk"], o * D, KC, c0, 256, lambda kc: xk[:, kc, :], lambda m, ps: K.copy(kg[:, m, :], ps[:, 0:T], eng="act"))
        dense_fm(K, Wv["wv"], o * D, KC, c0, 256, lambda kc: xv[:, kc, :], lambda m, ps: K.copy(vg[:, m, :], ps[:, 0:T], eng="act"))
        def ev_w(m, ps):
            hp = grp * 2 + m
            K.act(lwg[:, m, :], ps[:, 0:T], AF.Sigmoid, bias=P["w0"][:, o, hp:hp + 1])
            K.ts(lwg[:, m, :], lwg[:, m, :], -math.exp(-0.5), ALU.mult)
            if ti.sample:
                K.tt(lwg[:, m, :], lwg[:, m, :], C["tokmask"][:, :], ALU.mult, eng="pool")
        dense_fm(K, Wv["w2"], o * 96, 1, c0, 256, lambda kc: l1[0:96, 0, :], ev_w, krows=96)

        def ev_a(m, ps):
            hp = grp * 2 + m
            K.act(ag[:, m, :], ps[:, 0:T], AF.Sigmoid, bias=P["a0"][:, o, hp:hp + 1])
        dense_fm(K, Wv["a2"], o * 96, 1, c0, 256, lambda kc: l1[0:96, 1, :], ev_a, krows=96)
        dense_fm(K, Wv["g2"], o * 256, 2, c0, 256, lambda kc: l1[:, 2 + kc, :], lambda m, ps: K.copy(gg[:, m, :], ps[:, 0:T], eng="act"))
        if o >= 1:
            vfg = R["vfg"]
            K.dma(vfg.all(), xs_vf[:, ti.t, grp * 2:(grp + 1) * 2, :])

            def ev_v(m, ps):
                hp = grp * 2 + m
                sv, dd = ft[0], ft[1]
                K.act(sv[:, :], ps[:, 0:T], AF.Sigmoid, bias=P["v0"][:, o - 1, hp:hp + 1])
                K.tt(dd[:, :], vfg[:, m, :], vg[:, m, :], ALU.subtract)
                K.tt(dd[:, :], dd[:, :], sv[:, :], ALU.mult)
                K.tt(vg[:, m, :], vg[:, m, :], dd[:, :], ALU.add)
            dense_fm(K, Wv["v2"], (o - 1) * 64, 1, c0, 256, lambda kc: l1[0:64, 4, :], ev_v, krows=64)
        elif first_pass and xs_vf is not None:
            K.dma(xs_vf[:, ti.t, grp * 2:(grp + 1) * 2, :], vg.all(), eng="act")
        for m in range(2):
            rwkv_hp(K, C, R, o, ti, chunks, grp * 2 + m, rg[:, m, :], kg[:, m, :], vg[:, m, :], lwg[:, m, :], ag[:, m, :], gg[:, m, :], yg)


def rwkv_hp(K, C, R, o, ti, chunks, hp, r, k, v, lw, a, g, yg):
    P = R["P"]
    ft = R["ft"]
    H, Hbf = R["H"], R["Hbf"]
    NCH = T // 64
    pcol = lambda nm: P[nm][:, o, hp:hp + 1]
    t_kk, t_kap, t_kp, t_b, t_G, t_e, t_x = ft[2], ft[3], ft[4], ft[5], ft[6], ft[7], ft[8]
    K.ts(t_kk[:, :], k, pcol("kk"), ALU.mult)
    sqb = C["sq"][0]
    K.act(sqb[:, :], t_kk[:, :], AF.Square)
    ps = K.ps()
    K.mm(ps[:, 0:T], C["ones_blk"][:, :], sqb[:, :])
    K.act(t_x[:, :], ps[:, 0:T], AF.Ln, bias=C["epsc"][:, 2:3])
    K.act(t_x[:, :], t_x[:, :], AF.Exp, scale=-0.5)
    K.tt(t_kap[:, :], t_kk[:, :], t_x[:, :], ALU.mult)
    if ti.sample:
        K.tt(t_kap[:, :], t_kap[:, :], C["tokmask"][:, :], ALU.mult, eng="pool")
    K.ts(t_x[:, :], a, pcol("ka"), ALU.mult, pcol("omka"), ALU.add)
    K.tt(t_kp[:, :], k, t_x[:, :], ALU.mult)
    if ti.sample:
        K.tt(t_kp[:, :], t_kp[:, :], C["tokmask"][:, :], ALU.mult, eng="pool")
    K.tt(t_b[:, :], t_kap[:, :], a, ALU.mult)
    K.stt(t_x[:, :], r, pcol("rk"), t_kp[:, :], ALU.mult, ALU.mult)
    sq1 = C["sq"][1]
    K.copy(sq1[:, :], t_x[:, :], eng="act")
    psb = K.ps()
    K.mm(psb[:, 0:T], C["ones_blk"][:, :], sq1[:, :])
    bonus = ft[9]
    K.tt(bonus[:, :], psb[:, 0:T], v, ALU.mult)
    K.scan(t_G[:, :], C["rm"][:, :], lw, 0.0, ALU.mult, ALU.add)
    bd = R["bd"][hp % len(R["bd"])]
    identR = C["identf"] if RD == F32 else C["identb"]
    rt = R["rt"][hp % len(R["rt"])]
    dec = R["decr"][hp % 2]
    K.act(t_e[:, :], t_G[:, :], AF.Exp)
    K.tt(rt[:, :], r, t_e[:, :], ALU.mult)

    def to_bd(idx, a_, b_):
        for hh in range(2):
            sl = slice(hh * 64, (hh + 1) * 64)
            o3 = bd[sl, idx, :, hh * 64:(hh + 1) * 64]
            a3 = a_.m(lambda ap: ap[sl].rearrange("p (c t) -> p c t", t=64))
            if b_ is None:
                K.copy(o3, a3, eng="pool")
            else:
                b3 = b_.m(lambda ap: ap[sl].rearrange("p (c t) -> p c t", t=64))
                K.tt(o3, a3, b3, ALU.mult, eng=("dve" if hh == 0 else "pool"))
    K.tt(t_x[:, :], t_G[:, :], lw, ALU.subtract)
    K.act(t_x[:, :], t_x[:, :], AF.Exp)
    to_bd(2, t_kap[:, :], t_x[:, :])
    K.act(t_e[:, :], t_G[:, :], AF.Exp, scale=-1.0)
    to_bd(0, t_kp[:, :], t_e[:, :])
    to_bd(1, t_b[:, :], t_e[:, :])
    g3 = t_G[:, :].m(lambda ap: ap.rearrange("p (c t) -> p c t", t=64))
    gl = g3.m(lambda ap: ap[:, :, 63:64])
    K.act(dec[:, :, :], gl, AF.Exp)
    x3 = t_x[:, :].m(lambda ap: ap.rearrange("p (c t) -> p c t", t=64))
    K.tt(x3, gl.m(lambda ap: ap.to_broadcast([128, NCH, 64])), g3, ALU.subtract)
    K.act(t_x[:, :], t_x[:, :], AF.Exp)
    to_bd(3, t_kp[:, :], t_x[:, :])
    to_bd(4, t_b[:, :], t_x[:, :])
    to_bd(5, v, None)
    yps = K.ps()
    yf = ft[10]
    if ti.sample:
        K.memset(yf[:, :], 0.0)
    toks, gss, Qs, Pns = R["tok"], R["gs"], R["Q"], R["Pn"]
    for c in chunks:
        pst = K.ps()
        for j, idx in enumerate((5, 3, 4)):
            if RD == F32:
                K.tr(pst[:, j * 128:(j + 1) * 128], bd[:, idx, c, :], identR[:, :])
            else:
                K.tr(pst.bf(slice(None), j * 128, (j + 1) * 128), bd[:, idx, c, :], identR[:, :])
        K.copy(toks[c][:, :], pst[:, 0:384] if RD == F32 else pst.bf(slice(None), 0, 384), eng="act")
    for c in chunks:
        cs = slice(c * 64, (c + 1) * 64)
        pg = K.ps()
        K.mm(pg[:, 0:128], bd[:, 0, c, :], bd[:, 2, c, :])
        K.mm(pg[:, 128:256], bd[:, 1, c, :], bd[:, 2, c, :])
        K.mm(pg[:, 256:384], bd[:, 2, c, :], bd[:, 1, c, :])
        K.mm(pg[:, 384:448], bd[:, 0, c, :], rt[:, cs])
        K.mm(pg[:, 448:512], bd[:, 1, c, :], rt[:, cs])
        K.tt(gss[c][:, :], pg[:, 0:512], C["maskG"][:, :], ALU.mult)
    for c in chunks:
        K.tt(Qs[c][:, :], identR[:, :], gss[c][:, 128:256], ALU.subtract, eng="pool")
    for lvl in range(1, 6):
        for c in chunks:
            Pk = gss[c][:, 128:384] if lvl == 1 else Pns[c][:, 0:256]
            M_, MT_ = Pk.m(lambda ap: ap[:, 0:128]), Pk.m(lambda ap: ap[:, 128:256])
            pp = K.ps()
            if lvl < 5:
                K.mm(pp[:, 0:128], MT_, M_)
            K.mm(pp[:, 128:256], M_, MT_)
            if lvl < 5:
                K.copy(Pns[c][:, 0:256], pp[:, 0:256], eng="act")
            else:
                K.copy(Pns[c][:, 128:256], pp[:, 128:256], eng="act")
        for c in chunks:
            pq = K.ps()
            K.mm(pq[:, 0:128], Pns[c][:, 128:256], Qs[c][:, :])
            K.tt(Qs[c][:, :], pq[:, 0:128], Qs[c][:, :], ALU.add)
    for c in chunks:
        cs = slice(c * 64, (c + 1) * 64)
        tok, gs, Q = toks[c], gss[c], Qs[c]
        Vbd, Ktok, Btok = tok[:, 0:128], tok[:, 128:256], tok[:, 256:384]
        px = K.ps()
        K.mm(px[:, 0:128], bd[:, 2, c, :], Hbf[:, hp, :], start=True, stop=False)
        K.mm(px[:, 0:128], gs[:, 0:128], Vbd, start=False, stop=True)
        xsb = R["xsb"]
        K.copy(xsb[:, :], px[:, 0:128], eng="act")
        pu = K.ps()
        K.mm(pu[:, 0:128], Q[:, :], xsb[:, :])
        nu = R["nu"]
        K.ts(nu[:, :], pu[:, 0:128], -1.0, ALU.mult)
        K.mm(yps[:, cs], Hbf[:, hp, :], rt[:, cs], start=True, stop=False)
        K.mm(yps[:, cs], Vbd, gs[:, 384:448], start=False, stop=False)
        K.mm(yps[:, cs], nu[:, :], gs[:, 448:512], start=False, stop=True)
        ph = K.ps()
        K.mm(ph[:, 0:128], Ktok, Vbd, start=True, stop=False)
        K.mm(ph[:, 0:128], Btok, nu[:, :], start=False, stop=True)
        K.stt(H[:, hp, :], H[:, hp, :], dec[:, c, :], ph[:, 0:128], ALU.mult, ALU.add)
        if Hbf is not H:
            K.copy(Hbf[:, hp, :], H[:, hp, :], eng="act")
        K.copy(yf[:, cs], yps[:, cs], eng="act")
    ybf, ysq = C["sq"][0], C["sq"][1]
    K.copy(ybf[:, :], yf[:, :], eng="pool")
    K.act(ysq[:, :], yf[:, :], AF.Square)
    pn = K.ps()
    K.mm(pn[:, 0:T], C["ones_blk"][:, :], ybf[:, :])
    K.mm(pn[:, T:2 * T], C["ones_blk"][:, :], ysq[:, :])
    mean, var = ft[2], ft[3]
    K.ts(mean[:, :], pn[:, 0:T], 1.0 / 64, ALU.mult)
    K.tt(var[:, :], mean[:, :], mean[:, :], ALU.mult)
    K.stt(var[:, :], pn[:, T:2 * T], 1.0 / 64, var[:, :], ALU.mult, ALU.subtract)
    K.act(var[:, :], var[:, :], AF.Ln, bias=C["epsc"][:, 1:2])
    K.act(var[:, :], var[:, :], AF.Exp, scale=-0.5)
    K.tt(yf[:, :], yf[:, :], mean[:, :], ALU.subtract)
    K.tt(yf[:, :], yf[:, :], var[:, :], ALU.mult)
    K.ts(yf[:, :], yf[:, :], pcol("lnx_g"), ALU.mult, pcol("lnx_b"), ALU.add)
    K.tt(yf[:, :], yf[:, :], bonus[:, :], ALU.add)
    cr = slice(min(chunks) * 64, (max(chunks) + 1) * 64)
    K.tt(yg[:, hp, cr], yf[:, cr], g.m(lambda ap: ap[:, cr]), ALU.mult)
```
